# Optimizing a Trainium2 kernel written in Bass

```python
import math
import jax, jax.numpy as jnp
from jax import lax
import numpy as np

D_MODEL = 2048
BATCH = 4
SEQ = 2048
DEPTH = 4

GRID_W = 64
CTX_LEN = 256

GLA_HEADS = 4
GLA_DK = 128
GLA_DV = 256
GLA_QK = GLA_HEADS * GLA_DK
GLA_V = GLA_HEADS * GLA_DV
GLA_GATE_RANK = 16
GLA_GATE_TAU = 16.0
GLA_CHUNK = 64
S5_WIDTH = 768
S5_GROUP = 16
S5_GROUPS = S5_WIDTH // S5_GROUP
S5_STATE = 64
ATTN_Q_HEADS = 8
ATTN_KV_HEADS = 2
ATTN_HEAD_DIM = 128
ATTN_Q = ATTN_Q_HEADS * ATTN_HEAD_DIM
ATTN_KV = ATTN_KV_HEADS * ATTN_HEAD_DIM
ATTN_BLOCK = 128
ROPE_THETA = 10000.0
D_FF = ((8 * D_MODEL + 3 * 256 - 1) // (3 * 256)) * 256
IN_SPLITS = (GLA_QK, GLA_QK, GLA_V, GLA_V, GLA_GATE_RANK, S5_WIDTH, ATTN_Q, ATTN_KV, ATTN_KV, 3 * D_MODEL)
IN_WIDTH = 2 * GLA_QK + 2 * GLA_V + GLA_GATE_RANK + S5_WIDTH + ATTN_Q + 2 * ATTN_KV + 3 * D_MODEL
DN_ALPHA = (2 * DEPTH) ** 0.25
DN_BETA = (8 * DEPTH) ** -0.25
EPS = 1e-6

kernel_name = "hybrid_gla_s5_gqa_prefix_dit"

F32 = jnp.float32


def layer_norm(x, w, b):
    xf = x.astype(F32)
    mu = jnp.mean(xf, -1, keepdims=True)
    var = jnp.mean(jnp.square(xf - mu), -1, keepdims=True)
    return ((xf - mu) * lax.rsqrt(var + EPS) * w + b).astype(x.dtype)


def rms_norm(x, w):
    xf = x.astype(F32)
    return (xf * lax.rsqrt(jnp.mean(xf * xf, -1, keepdims=True) + EPS) * w).astype(x.dtype)


def modulate(h, shift, scale):
    return h * (1 + scale) + shift


def post_norm(x, y, w, b):
    return layer_norm(DN_ALPHA * x + y, w, b)


def axial_rope(n_rows):
    rows = jnp.repeat(jnp.arange(n_rows), GRID_W).astype(F32)
    cols = jnp.tile(jnp.arange(GRID_W), n_rows).astype(F32)
    n_freq = ATTN_HEAD_DIM // 4
    inv = ROPE_THETA ** (-jnp.arange(n_freq, dtype=F32) / n_freq)
    ang = jnp.concatenate([rows[:, None] * inv, cols[:, None] * inv], -1)
    return jnp.cos(ang)[:, None, :], jnp.sin(ang)[:, None, :]


def apply_rope(x, cos, sin):
    xf = x.astype(F32).reshape(x.shape[:-1] + (x.shape[-1] // 2, 2))
    x0, x1 = xf[..., 0], xf[..., 1]
    out = jnp.stack([x0 * cos - x1 * sin, x0 * sin + x1 * cos], -1)
    return out.reshape(x.shape).astype(x.dtype)


def attend(q, k, v):
    s = jnp.einsum('bqhgd,bkhd->bhgqk', q, k, preferred_element_type=F32) * (ATTN_HEAD_DIM ** -0.5)
    p = jax.nn.softmax(s, axis=-1).astype(v.dtype)
    return jnp.einsum('bhgqk,bkhd->bqhgd', p, v)


def gqa_blocks(q, k, v):
    bsz, t = q.shape[:2]
    grp = ATTN_Q_HEADS // ATTN_KV_HEADS
    qb = q.reshape(bsz, t // ATTN_BLOCK, ATTN_BLOCK, ATTN_KV_HEADS, grp, ATTN_HEAD_DIM).swapaxes(0, 1)
    ob = lax.map(lambda qq: attend(qq, k, v), qb)
    return ob.swapaxes(0, 1).reshape(bsz, t, ATTN_Q)


def gla_chunked(q, k, v, log_a, s0):
    bsz, t, nh, _ = q.shape
    dv = v.shape[-1]
    n = t // GLA_CHUNK
    mask = jnp.tril(jnp.ones((GLA_CHUNK, GLA_CHUNK), bool))

    def to_chunks(z):
        return z.reshape(bsz, n, GLA_CHUNK, nh, z.shape[-1]).swapaxes(0, 1)

    def step(s, inp):
        qc, kc, vc, ac = inp
        b = jnp.cumsum(ac, axis=1)
        qe = qc * jnp.exp(b)
        ke = kc * jnp.exp(-b)
        att = jnp.where(mask, jnp.einsum('bihd,bjhd->bhij', qe, ke), 0.0)
        o = jnp.einsum('bhij,bjhv->bihv', att, vc) + jnp.einsum('bihd,bhdv->bihv', qe, s)
        b_last = b[:, -1]
        kd = kc * jnp.exp(b_last[:, None] - b)
        s = jnp.exp(b_last)[..., None] * s + jnp.einsum('bjhd,bjhv->bhdv', kd, vc)
        return s, o

    s_fin, o = lax.scan(step, s0, (to_chunks(q), to_chunks(k), to_chunks(v), to_chunks(log_a)))
    return o.swapaxes(0, 1).reshape(bsz, t, nh, dv), s_fin


def gla_bidirectional(q, k, v, glr, w_gate, b_gate, n_ctx):
    q, k, v = q.astype(F32), k.astype(F32), v.astype(F32)
    bsz, n_tok, nh, dk = q.shape
    dv = v.shape[-1]
    outs = []
    for d in range(2):
        log_a = jax.nn.log_sigmoid((glr @ w_gate[d] + b_gate[d]).astype(F32)) / GLA_GATE_TAU
        log_a = log_a.reshape(bsz, n_tok, nh, dk)
        ctx_in = [z[:, :n_ctx] for z in (q, k, v, log_a)]
        lat_in = [z[:, n_ctx:] for z in (q, k, v, log_a)]
        if d == 1:
            ctx_in = [jnp.flip(z, 1) for z in ctx_in]
            lat_in = [jnp.flip(z, 1) for z in lat_in]
        s0 = jnp.zeros((bsz, nh, dk, dv), F32)
        o_c, s_c = gla_chunked(*ctx_in, s0)
        o_l, _ = gla_chunked(*lat_in, s_c)
        if d == 1:
            o_c, o_l = jnp.flip(o_c, 1), jnp.flip(o_l, 1)
        outs.append(jnp.concatenate([o_c, o_l], 1))
    return outs[0] + outs[1]


def s5_discretise(lam_re, lam_im, log_dt):
    lam_re, lam_im = lam_re.astype(F32), lam_im.astype(F32)
    dt = jnp.exp(log_dt.astype(F32))[:, None]
    mag = jnp.exp(lam_re * dt)
    a_re, a_im = mag * jnp.cos(lam_im * dt), mag * jnp.sin(lam_im * dt)
    den = lam_re * lam_re + lam_im * lam_im
    nr, ni = a_re - 1, a_im
    k_re = (nr * lam_re + ni * lam_im) / den
    k_im = (ni * lam_re - nr * lam_im) / den
    return a_re, a_im, k_re, k_im


def s5_scan(bu_re, bu_im, a_re, a_im, s0_re, s0_im, reverse):
    idx = -1 if reverse else 0
    bu_re = bu_re.at[:, idx].add(a_re * s0_re - a_im * s0_im)
    bu_im = bu_im.at[:, idx].add(a_re * s0_im + a_im * s0_re)
    t = bu_re.shape[1]
    ar = jnp.broadcast_to(a_re, (1, t) + a_re.shape)
    ai = jnp.broadcast_to(a_im, (1, t) + a_im.shape)

    def combine(e1, e2):
        a1r, a1i, b1r, b1i = e1
        a2r, a2i, b2r, b2i = e2
        return (a2r * a1r - a2i * a1i, a2r * a1i + a2i * a1r,
                a2r * b1r - a2i * b1i + b2r, a2r * b1i + a2i * b1r + b2i)

    _, _, xr, xi = lax.associative_scan(combine, (ar, ai, bu_re, bu_im), axis=1, reverse=reverse)
    return xr, xi


def s5_bidirectional(u, n_ctx, lam_re, lam_im, log_dt, b_re, b_im, c_re, c_im, d_skip):
    bsz, n_tok, _ = u.shape
    uf = u.astype(F32).reshape(bsz, n_tok, S5_GROUPS, S5_GROUP)
    bu_re = jnp.einsum('btgc,gpc->btgp', uf, b_re.astype(F32))
    bu_im = jnp.einsum('btgc,gpc->btgp', uf, b_im.astype(F32))
    y = uf * d_skip.astype(F32).reshape(S5_GROUPS, S5_GROUP)
    zeros = jnp.zeros((bsz, S5_GROUPS, S5_STATE), F32)
    for d in range(2):
        rev = d == 1
        a_re, a_im, k_re, k_im = s5_discretise(lam_re[d], lam_im[d], log_dt[d])
        xr = k_re * bu_re - k_im * bu_im
        xi = k_re * bu_im + k_im * bu_re
        cr, ci = s5_scan(xr[:, :n_ctx], xi[:, :n_ctx], a_re, a_im, zeros, zeros, rev)
        end = 0 if rev else -1
        lr, li = s5_scan(xr[:, n_ctx:], xi[:, n_ctx:], a_re, a_im, cr[:, end], ci[:, end], rev)
        sr = jnp.concatenate([cr, lr], 1)
        si = jnp.concatenate([ci, li], 1)
        y = y + jnp.einsum('btgp,gcp->btgc', sr, c_re[d].astype(F32)) \
              - jnp.einsum('btgp,gcp->btgc', si, c_im[d].astype(F32))
    return y.reshape(bsz, n_tok, S5_WIDTH)


def token_mixers(h, n_ctx, keep_ctx, cos, sin, w_in, w_gla_gate, b_gla_gate, gla_norm_w,
                 s5_lam_re, s5_lam_im, s5_log_dt, s5_b_re, s5_b_im, s5_c_re, s5_c_im, s5_d,
                 w_s5_glu, q_norm_w, k_norm_w, w_proj_gla, w_proj_s5, w_proj_attn, w_out):
    bsz, n_tok, _ = h.shape
    cuts = [int(s) for s in np.cumsum(IN_SPLITS)[:-1]]
    gq, gk, gv, gr, glr, su, aq, ak, av, bg = jnp.split(h @ w_in, cuts, axis=-1)
    lo = 0 if keep_ctx else n_ctx

    def heads(z, nh):
        return z.reshape(bsz, n_tok, nh, -1)

    o = gla_bidirectional(heads(gq, GLA_HEADS) * (GLA_DK ** -0.5), heads(gk, GLA_HEADS),
                          heads(gv, GLA_HEADS), glr, w_gla_gate, b_gla_gate, n_ctx)[:, lo:]
    mu = jnp.mean(o, -1, keepdims=True)
    var = jnp.mean(jnp.square(o - mu), -1, keepdims=True)
    o = (o - mu) * lax.rsqrt(var + EPS) * gla_norm_w.astype(F32).reshape(GLA_HEADS, GLA_DV)
    o_gla = o.reshape(bsz, n_tok - lo, GLA_V).astype(h.dtype) * jax.nn.silu(gr[:, lo:])

    y = s5_bidirectional(su, n_ctx, s5_lam_re, s5_lam_im, s5_log_dt, s5_b_re, s5_b_im,
                         s5_c_re, s5_c_im, s5_d)[:, lo:].astype(h.dtype)
    y = jax.nn.gelu(y)
    o_s5 = y * jax.nn.sigmoid(y @ w_s5_glu)

    q = rms_norm(heads(aq, ATTN_Q_HEADS), q_norm_w)
    k = rms_norm(heads(ak, ATTN_KV_HEADS), k_norm_w)
    v = heads(av, ATTN_KV_HEADS)
    q_lat = apply_rope(q[:, n_ctx:], cos, sin)
    k_lat = apply_rope(k[:, n_ctx:], cos, sin)
    k_all = jnp.concatenate([k_lat, k[:, :n_ctx]], 1)
    v_all = jnp.concatenate([v[:, n_ctx:], v[:, :n_ctx]], 1)
    o_attn = gqa_blocks(q_lat, k_all, v_all)
    if keep_ctx:
        grp = ATTN_Q_HEADS // ATTN_KV_HEADS
        q_c = q[:, :n_ctx].reshape(bsz, n_ctx, ATTN_KV_HEADS, grp, ATTN_HEAD_DIM)
        o_c = attend(q_c, k[:, :n_ctx], v[:, :n_ctx]).reshape(bsz, n_ctx, ATTN_Q)
        o_attn = jnp.concatenate([o_c, o_attn], 1)

    g_a, g_b, g_c = jnp.split(jax.nn.sigmoid(bg[:, lo:]), 3, axis=-1)
    merged = g_a * (o_gla @ w_proj_gla) + g_b * (o_s5 @ w_proj_s5) + g_c * (o_attn @ w_proj_attn)
    return merged @ w_out


def swiglu(h, w_ffn_in, w_ffn_out):
    a, b = jnp.split(h @ w_ffn_in, 2, axis=-1)
    return (jax.nn.silu(a) * b) @ w_ffn_out


def setup_inputs(seed: int = 0) -> dict:
    key = jax.random.key(seed)
    keys = iter(jax.random.split(key, 40))

    def nrm(shape, std):
        return std * jax.random.normal(next(keys), shape, F32)

    L, D, G, P = DEPTH, D_MODEL, S5_GROUPS, S5_STATE
    lam_im = jnp.pi * jnp.arange(P, dtype=F32) + nrm((L, 2, G, P), 0.01)
    return {
        "x": nrm((BATCH, SEQ, D), 1.0),
        "c": nrm((BATCH, D), 1.0),
        "ctx": nrm((BATCH, CTX_LEN, D), 1.0),
        "c_ctx": nrm((D,), 1.0),
        "w_ada": nrm((L, D, 6 * D), 0.5 * D ** -0.5),
        "b_ada": nrm((L, 6 * D), 0.02),
        "w_in": nrm((L, D, IN_WIDTH), D ** -0.5),
        "w_gla_gate": nrm((L, 2, GLA_GATE_RANK, GLA_QK), GLA_GATE_RANK ** -0.5),
        "b_gla_gate": nrm((L, 2, GLA_QK), 0.1),
        "gla_norm_w": 1.0 + nrm((L, GLA_V), 0.02),
        "s5_lam_re": -0.5 + nrm((L, 2, G, P), 0.01),
        "s5_lam_im": lam_im,
        "s5_log_dt": jax.random.uniform(next(keys), (L, 2, G), F32, math.log(1e-3), math.log(1e-1)),
        "s5_b_re": nrm((L, G, P, S5_GROUP), (2 * S5_GROUP) ** -0.5),
        "s5_b_im": nrm((L, G, P, S5_GROUP), (2 * S5_GROUP) ** -0.5),
        "s5_c_re": nrm((L, 2, G, S5_GROUP, P), 0.5),
        "s5_c_im": nrm((L, 2, G, S5_GROUP, P), 0.5),
        "s5_d": nrm((L, S5_WIDTH), 1.0),
        "w_s5_glu": nrm((L, S5_WIDTH, S5_WIDTH), S5_WIDTH ** -0.5),
        "q_norm_w": 1.0 + nrm((L, ATTN_HEAD_DIM), 0.02),
        "k_norm_w": 1.0 + nrm((L, ATTN_HEAD_DIM), 0.02),
        "w_proj_gla": nrm((L, GLA_V, D), GLA_V ** -0.5),
        "w_proj_s5": nrm((L, S5_WIDTH, D), S5_WIDTH ** -0.5),
        "w_proj_attn": nrm((L, ATTN_Q, D), ATTN_Q ** -0.5),
        "w_out": nrm((L, D, D), DN_BETA * D ** -0.5),
        "ln1_w": 1.0 + nrm((L, D), 0.02),
        "ln1_b": nrm((L, D), 0.02),
        "ln2_w": 1.0 + nrm((L, D), 0.02),
        "ln2_b": nrm((L, D), 0.02),
        "w_ffn_in": nrm((L, D, 2 * D_FF), D ** -0.5),
        "w_ffn_out": nrm((L, D_FF, D), DN_BETA * D_FF ** -0.5),
    }


def reference(x, c, ctx, c_ctx, w_ada, b_ada, w_in, w_gla_gate, b_gla_gate, gla_norm_w,
              s5_lam_re, s5_lam_im, s5_log_dt, s5_b_re, s5_b_im, s5_c_re, s5_c_im, s5_d,
              w_s5_glu, q_norm_w, k_norm_w, w_proj_gla, w_proj_s5, w_proj_attn, w_out,
              ln1_w, ln1_b, ln2_w, ln2_b, w_ffn_in, w_ffn_out):
    n_ctx = ctx.shape[1]
    n_lat = x.shape[1]
    rows = n_lat // GRID_W
    cos, sin = axial_rope(rows)
    xc = ctx
    silu_c = jax.nn.silu(c)
    silu_cc = jax.nn.silu(c_ctx)
    for l in range(DEPTH):
        keep_ctx = l < DEPTH - 1
        mod = (silu_c @ w_ada[l] + b_ada[l])[:, None, :]
        mod_c = silu_cc @ w_ada[l] + b_ada[l]
        sh1, sc1, g1, sh2, sc2, g2 = jnp.split(mod, 6, axis=-1)
        csh1, csc1, cg1, csh2, csc2, cg2 = jnp.split(mod_c, 6, axis=-1)
        h = jnp.concatenate([modulate(xc, csh1, csc1), modulate(x, sh1, sc1)], axis=1)
        mix = token_mixers(h, n_ctx, keep_ctx, cos, sin, w_in[l], w_gla_gate[l], b_gla_gate[l],
                           gla_norm_w[l], s5_lam_re[l], s5_lam_im[l], s5_log_dt[l], s5_b_re[l],
                           s5_b_im[l], s5_c_re[l], s5_c_im[l], s5_d[l], w_s5_glu[l], q_norm_w[l],
                           k_norm_w[l], w_proj_gla[l], w_proj_s5[l], w_proj_attn[l], w_out[l])
        mix_lat = mix[:, mix.shape[1] - n_lat:]
        x = post_norm(x, g1 * mix_lat, ln1_w[l], ln1_b[l])
        x = post_norm(x, g2 * swiglu(modulate(x, sh2, sc2), w_ffn_in[l], w_ffn_out[l]), ln2_w[l], ln2_b[l])
        if keep_ctx:
            xc = post_norm(xc, cg1 * mix[:, :n_ctx], ln1_w[l], ln1_b[l])
            xc = post_norm(xc, cg2 * swiglu(modulate(xc, csh2, csc2), w_ffn_in[l], w_ffn_out[l]),
                           ln2_w[l], ln2_b[l])
    return x
```

```python
import contextlib
import math

import numpy as np
import concourse.bass as bass
import concourse.mybir as mybir
from concourse.bass_utils import run_bass_kernel_spmd

F32 = mybir.dt.float32
F32R = mybir.dt.float32r
I32 = mybir.dt.int32
AF = mybir.ActivationFunctionType
ALU = mybir.AluOpType

D = 2048
NLAT = 2048
NCTX = 256
T = NCTX + NLAT
DEPTH = 4
GLA_H, GLA_DK, GLA_DV = 4, 128, 256
GLA_TAU = 16.0
CH = 64
NCH = T // CH
S5_W, S5_NG, S5_P = 768, 48, 64
AQ_H, AKV_H, HD = 8, 2, 128
DFF = 5632
IN_W = 11536
OFF_GQ, OFF_GK, OFF_GV, OFF_GR, OFF_GLR, OFF_SU, OFF_AQ, OFF_AK, OFF_AV, OFF_BG = (
    0, 512, 1024, 2048, 3072, 3088, 3856, 4880, 5136, 5392)
DN_ALPHA = (2 * DEPTH) ** 0.25
EPS = 1e-6
KT = D // 128
NT = T // 128
TWO_PI = 2.0 * math.pi
SIN_SCALE = TWO_PI * 0.999999
TBLK = [(0, 256), (256, 512), (768, 512), (1280, 512), (1792, 512)]

EPOCH_LIM = 30000


def _tkey(tok):
    return (tok[0], tok[1]) if tok[0] == "c" else (tok[0], tok[1], tok[2])


def _tval(tok):
    return (tok[2], tok[3]) if tok[0] == "c" else tok[3]


class Buf:
    __slots__ = ("name", "lw", "rd", "multi")

    def __init__(self, name, multi=False):
        self.name = name
        self.lw = {}
        self.rd = {}
        self.multi = multi


def _merge(dct, tok):
    k = _tkey(tok)
    cur = dct.get(k)
    if cur is None or _tval(cur) < _tval(tok):
        dct[k] = tok


class Prog:
    def __init__(self, nc, es):
        self.nc = nc
        self.es = es
        self.eng = {"pe": nc.tensor, "act": nc.scalar, "dve": nc.vector,
                    "pool": nc.gpsimd, "sp": nc.sync}
        self.csem = {}
        self.cnt = {}
        self.epoch = {}
        for e in ("pe", "act", "dve", "pool"):
            self.csem[e] = [es.enter_context(nc.semaphore(f"c_{e}_0"))]
            self.cnt[e] = 0
            self.epoch[e] = 0
        self.seen = {e: {} for e in self.eng}
        self.dsem = {}
        self.dval = {}
        self.dlast = {}
        self.drr = {}
        for q, n in (("sp", 24), ("pool", 24)):
            self.dsem[q] = [es.enter_context(nc.semaphore(f"d_{q}_{i}")) for i in range(n)]
            self.dval[q] = [0] * n
            self.dlast[q] = [None] * n
            self.drr[q] = 0
        self.n_inst = 0

    def _wait(self, e, tok):
        if tok is None:
            return
        seen = self.seen[e]
        k = _tkey(tok)
        v = _tval(tok)
        if tok[0] == "c":
            if tok[1] == e and e == "pe":
                return
            cur = seen.get(k)
            if cur is not None and cur >= v:
                return
            self.eng[e].wait_ge(self.csem[tok[1]][tok[2]], tok[3])
        else:
            if seen.get(k, 0) >= v:
                return
            self.eng[e].wait_ge(self.dsem[tok[1]][tok[2]], tok[3])
        seen[k] = v
        self.n_inst += 1

    def _deps(self, reads, writes):
        deps = []
        for b in reads:
            deps.extend(b.lw.values())
        for b in writes:
            if not b.multi:
                deps.extend(b.lw.values())
            deps.extend(b.rd.values())
        return deps

    def _commit(self, tok, reads, writes):
        for b in reads:
            _merge(b.rd, tok)
        for b in writes:
            if b.multi:
                _merge(b.lw, tok)
            else:
                b.lw = {_tkey(tok): tok}
                b.rd = {}

    def op(self, e, fn, reads=(), writes=()):
        for d in self._deps(reads, writes):
            self._wait(e, d)
        if self.cnt[e] >= EPOCH_LIM:
            self.epoch[e] += 1
            self.cnt[e] = 0
            self.csem[e].append(self.es.enter_context(
                self.nc.semaphore(f"c_{e}_{self.epoch[e]}")))
        ins = fn()
        self.cnt[e] += 1
        ins.then_inc(self.csem[e][self.epoch[e]], 1)
        tok = ("c", e, self.epoch[e], self.cnt[e])
        self._commit(tok, reads, writes)
        self.n_inst += 1
        return tok

    def dma(self, q, out, in_, reads=(), writes=(), **kw):
        e = q
        for d in self._deps(reads, writes):
            self._wait(e, d)
        si = self.drr[q]
        self.drr[q] = (si + 1) % len(self.dsem[q])
        self._wait(e, self.dlast[q][si])
        ins = self.eng[e].dma_start(out=out, in_=in_, **kw)
        self.dval[q][si] += 16
        ins.then_inc(self.dsem[q][si], 16)
        tok = ("d", q, si, self.dval[q][si])
        self.dlast[q][si] = tok
        self._commit(tok, reads, writes)
        self.n_inst += 1
        return tok

    def barrier(self):
        toks = []
        for e in ("pe", "act", "dve", "pool"):
            if self.cnt[e] > 0 or self.epoch[e] > 0:
                toks.append(("c", e, self.epoch[e], self.cnt[e]))
        for q in self.dsem:
            for t in self.dlast[q]:
                if t is not None:
                    toks.append(t)
        for e in self.eng:
            for t in toks:
                if t[0] == "c" and t[1] == e:
                    continue
                self._wait(e, t)


WEIGHT_NAMES = [
    ("w_ada", [DEPTH, D, 6 * D]), ("b_ada", [DEPTH, 6 * D]), ("w_in", [DEPTH, D, IN_W]),
    ("w_gla_gate", [DEPTH, 2, 16, 512]), ("b_gla_gate", [DEPTH, 2, 512]),
    ("gla_norm_w", [DEPTH, 1024]),
    ("s5_lam_re", [DEPTH, 2, 48, 64]), ("s5_lam_im", [DEPTH, 2, 48, 64]),
    ("s5_log_dt", [DEPTH, 2, 48]),
    ("s5_b_re", [DEPTH, 48, 64, 16]), ("s5_b_im", [DEPTH, 48, 64, 16]),
    ("s5_c_re", [DEPTH, 2, 48, 16, 64]), ("s5_c_im", [DEPTH, 2, 48, 16, 64]),
    ("s5_d", [DEPTH, 768]), ("w_s5_glu", [DEPTH, 768, 768]),
    ("q_norm_w", [DEPTH, 128]), ("k_norm_w", [DEPTH, 128]),
    ("w_proj_gla", [DEPTH, 1024, D]), ("w_proj_s5", [DEPTH, 768, D]),
    ("w_proj_attn", [DEPTH, 1024, D]), ("w_out", [DEPTH, D, D]),
    ("ln1_w", [DEPTH, D]), ("ln1_b", [DEPTH, D]), ("ln2_w", [DEPTH, D]), ("ln2_b", [DEPTH, D]),
    ("w_ffn_in", [DEPTH, D, 2 * DFF]), ("w_ffn_out", [DEPTH, DFF, D]),
]

CONST_SHAPES = {
    "ident": [128, 128], "ones128": [128, 128], "prot": [128, 128],
    "cmask": [128, T], "trif": [64, 64], "trib": [64, 64],
    "rmask": [128, 8], "colmask": [128, 8 * 128], "iota512": [128, 512],
    "ropepos": [128, NLAT], "fidx": [128, 1],
}

STAGES = ["init", "ada", "A", "gla", "s5", "att", "C"]


def build_program(debug=False, upto="all", nlayers=DEPTH):
    nc = bass.Bass("TRN2", target_bir_lowering=False)
    stage_on = {s: True for s in STAGES}
    if upto != "all":
        cut = STAGES.index(upto)
        for i, s in enumerate(STAGES):
            stage_on[s] = i <= cut

    def din(name, shape, dt=F32):
        return nc.dram_tensor(name, list(shape), dt, kind="ExternalInput").ap()

    skind = "ExternalOutput" if debug else "Internal"

    def dscr(name, shape, dt=F32):
        return nc.dram_tensor(name, list(shape), dt, kind=skind).ap()

    x_in = din("x", [NLAT, D])
    ctx_in = din("ctx", [NCTX, D])
    cc_in = din("cc", [2, D])
    Wd = {n: din(n, s) for n, s in WEIGHT_NAMES}
    Cd = {n: din(n, s) for n, s in CONST_SHAPES.items()}
    out_d = nc.dram_tensor("out", [NLAT, D], F32, kind="ExternalOutput").ap()

    xT = dscr("xT", [D, T])
    rope_d = dscr("rope_d", [2, 128, T])
    qgT = dscr("qgT", [512, T])
    kgT = dscr("kgT", [512, T])
    vg = dscr("vg", [T, 1024])
    grt = dscr("grt", [T, 1024])
    glrT = dscr("glrT", [16, T])
    suT = dscr("suT", [768, T])
    qaT = dscr("qaT", [1024, T])
    kaT = dscr("kaT", [256, T])
    va = dscr("va", [T, 256])
    oglaT = dscr("oglaT", [1024, T])
    os5T = dscr("os5T", [768, T])
    oattT = dscr("oattT", [1024, T])
    if debug:
        dbg_mod = dscr("dbg_mod", [128, DEPTH * 96 * 2])
        dbg_cs = dscr("dbg_cs", [128, T])
        dbg_y2 = dscr("dbg_y2", [128, T])
        dbg_oraw = dscr("dbg_oraw", [64, NCH * 256])
    xT_v = xT.rearrange("(ft p) t -> p ft t", p=128)

    def fv(ap_2d):
        return ap_2d.rearrange("(k p) t -> p k t", p=128)

    with contextlib.ExitStack() as es:
        P = Prog(nc, es)

        def sb(name, shape, dt=F32, scope=None):
            return (scope or es).enter_context(nc.sbuf_tensor(name, list(shape), dt))

        psum = [es.enter_context(nc.psum_tensor(f"ps{i}", [128, 512], F32)) for i in range(8)]
        pbuf = [Buf(f"ps{i}") for i in range(8)]

        b_xT = [Buf(f"xT{i}") for i in range(len(TBLK))]
        bd = {n: Buf(n, multi=True) for n in
              ["rope", "qgT", "kgT", "vg", "grt", "glrT", "suT", "qaT", "kaT", "va",
               "oglaT", "os5T", "oattT"]}

        def xT_bufs(t0, n):
            return [b_xT[i] for i, (s, m) in enumerate(TBLK) if s < t0 + n and t0 < s + m]

        V = nc.vector
        A = nc.scalar

        def mm(out, lhsT, rhs, start, stop, reads, pb):
            return P.op("pe", lambda: nc.tensor.matmul(out, lhsT=lhsT, rhs=rhs, start=start, stop=stop),
                        reads=reads, writes=[pb])

        def tr(out, in_, reads, pb):
            kk_ = in_.shape[0]
            return P.op("pe", lambda: nc.tensor.transpose(out, in_, ident[:kk_, :kk_]),
                        reads=list(reads) + [b_ident], writes=[pb])

        def dve(fn, reads, writes):
            return P.op("dve", fn, reads=reads, writes=writes)

        def act(fn, reads, writes):
            return P.op("act", fn, reads=reads, writes=writes)

        ident = sb("ident_sb", [128, 128])
        b_ident = Buf("ident")
        P.dma("sp", ident[:], Cd["ident"][:, :], writes=[b_ident])
        onesf = sb("onesf", [128, 128])
        b_onesf = Buf("onesf")
        P.dma("sp", onesf[:], Cd["ones128"][:, :], writes=[b_onesf])
        onesr = sb("onesr", [128, 128], F32R)
        b_onesr = Buf("onesr")
        P.dma("pool", onesr[:], Cd["ones128"][:, :], writes=[b_onesr])
        prot = sb("prot_sb", [128, 128], F32R)
        b_prot = Buf("prot")
        P.dma("pool", prot[:], Cd["prot"][:, :], writes=[b_prot])
        epsc = sb("epsc", [128, 1])
        b_epsc = Buf("epsc")
        dve(lambda: V.memset(epsc[:], EPS), [], [b_epsc])

        VC = {}
        ncol = 0
        for nm, n in (("ln1_w", DEPTH * 16), ("ln1_b", DEPTH * 16), ("ln2_w", DEPTH * 16),
                      ("ln2_b", DEPTH * 16), ("b_ada", DEPTH * 96), ("q_norm_w", DEPTH),
                      ("k_norm_w", DEPTH), ("b_gla_gate", DEPTH * 8), ("s5_d", DEPTH * 6)):
            VC[nm] = ncol
            ncol += n
        vecT = sb("vecT", [128, ncol])
        b_vecT = Buf("vecT")
        modT = sb("modT", [128, DEPTH, 96, 2])
        b_modT = Buf("modT")

        def vcol(nm, idx):
            c = VC[nm] + idx
            return vecT[:, c:c + 1]

        with contextlib.ExitStack() as ph:
            stg = [sb(f"pstg{i}", [128, 128], scope=ph) for i in range(2)]
            b_stg = [Buf(f"pstg{i}") for i in range(2)]
            k = 0

            def load_T(src_rows, R, c0):
                nonlocal k
                s = k % 2
                k += 1
                P.dma("sp", stg[s][:R, :], src_rows, writes=[b_stg[s]])
                tr(psum[s][:, :R], stg[s][:R, :], [b_stg[s]], pbuf[s])
                dve(lambda: V.tensor_copy(out=vecT[:, c0:c0 + R], in_=psum[s][:, :R]),
                    [pbuf[s]], [b_vecT])

            for nm in ("ln1_w", "ln1_b", "ln2_w", "ln2_b"):
                load_T(Wd[nm].rearrange("l (k p) -> (l k) p", p=128), DEPTH * 16, VC[nm])
            bav = Wd["b_ada"].rearrange("l (k p) -> (l k) p", p=128)
            for i in range(3):
                load_T(bav[i * 128:(i + 1) * 128, :], 128, VC["b_ada"] + i * 128)
            load_T(Wd["q_norm_w"], DEPTH, VC["q_norm_w"])
            load_T(Wd["k_norm_w"], DEPTH, VC["k_norm_w"])
            load_T(Wd["b_gla_gate"].rearrange("l d (k p) -> (l d k) p", p=128), DEPTH * 8,
                   VC["b_gla_gate"])
            load_T(Wd["s5_d"].rearrange("l (k p) -> (l k) p", p=128), DEPTH * 6, VC["s5_d"])
            c0 = VC["b_gla_gate"]
            dve(lambda: V.tensor_scalar(out=vecT[:, c0:c0 + DEPTH * 8], in0=vecT[:, c0:c0 + DEPTH * 8],
                                        scalar1=-1.0, scalar2=None, op0=ALU.mult, op1=ALU.bypass),
                [b_vecT], [b_vecT])

            fidx = sb("fidx_sb", [128, 1], scope=ph)
            b_fidx = Buf("fidx")
            P.dma("sp", fidx[:], Cd["fidx"][:, :], writes=[b_fidx])
            invr = sb("invr", [128, 1], scope=ph)
            b_invr = Buf("invr")
            act(lambda: A.activation(out=invr[:], in_=fidx[:], func=AF.Exp,
                                     scale=-math.log(10000.0) / 32.0), [b_fidx], [b_invr])
            dve(lambda: V.tensor_scalar(out=invr[:], in0=invr[:], scalar1=1.0 / TWO_PI, scalar2=None,
                                        op0=ALU.mult, op1=ALU.bypass), [b_invr], [b_invr])
            rp = sb("rp", [128, 512], scope=ph)
            rr = sb("rr", [128, 512], scope=ph)
            ri = sb("ri", [128, 512], I32, scope=ph)
            rf = sb("rf", [128, 512], scope=ph)
            ro = [sb(f"ro{i}", [128, 512], scope=ph) for i in range(2)]
            b_rp, b_rr, b_ri, b_rf = Buf("rp"), Buf("rr"), Buf("ri"), Buf("rf")
            b_ro = [Buf("ro0"), Buf("ro1")]

            def frac_wrap(r_t, b_r, n):
                dve(lambda: V.tensor_copy(out=ri[:, :n], in_=r_t[:, :n]), [b_r], [b_ri])
                dve(lambda: V.tensor_copy(out=rf[:, :n], in_=ri[:, :n]), [b_ri], [b_rf])
                dve(lambda: V.tensor_tensor(out=r_t[:, :n], in0=r_t[:, :n], in1=rf[:, :n],
                                            op=ALU.subtract), [b_r, b_rf], [b_r])
                wrap_half(r_t, b_r, n)

            def wrap_half(r_t, b_r, n):
                dve(lambda: V.tensor_single_scalar(out=rf[:, :n], in_=r_t[:, :n], scalar=0.5,
                                                   op=ALU.is_gt), [b_r], [b_rf])
                dve(lambda: V.tensor_tensor(out=r_t[:, :n], in0=r_t[:, :n], in1=rf[:, :n],
                                            op=ALU.subtract), [b_r, b_rf], [b_r])
                dve(lambda: V.tensor_single_scalar(out=rf[:, :n], in_=r_t[:, :n], scalar=-0.5,
                                                   op=ALU.is_lt), [b_r], [b_rf])
                dve(lambda: V.tensor_tensor(out=r_t[:, :n], in0=r_t[:, :n], in1=rf[:, :n],
                                            op=ALU.add), [b_r, b_rf], [b_r])

            if stage_on["A"]:
                dve(lambda: V.memset(ro[0][:, :NCTX], 1.0), [], [b_ro[0]])
                dve(lambda: V.memset(ro[1][:, :NCTX], 0.0), [], [b_ro[1]])
                P.dma("sp", rope_d[0, :, 0:NCTX], ro[0][:, :NCTX], reads=[b_ro[0]], writes=[bd["rope"]])
                P.dma("sp", rope_d[1, :, 0:NCTX], ro[1][:, :NCTX], reads=[b_ro[1]], writes=[bd["rope"]])
                for cch in range(4):
                    P.dma("sp", rp[:], Cd["ropepos"][:, cch * 512:(cch + 1) * 512], writes=[b_rp])
                    dve(lambda: V.tensor_scalar(out=rr[:], in0=rp[:], scalar1=invr[:, 0:1], scalar2=None,
                                                op0=ALU.mult, op1=ALU.bypass), [b_rp, b_invr], [b_rr])
                    frac_wrap(rr, b_rr, 512)
                    act(lambda: A.activation(out=ro[1][:], in_=rr[:], func=AF.Sin, scale=SIN_SCALE),
                        [b_rr], [b_ro[1]])
                    P.dma("sp", rope_d[1, :, NCTX + cch * 512:NCTX + (cch + 1) * 512], ro[1][:],
                          reads=[b_ro[1]], writes=[bd["rope"]])
                    dve(lambda: V.tensor_scalar(out=rr[:], in0=rr[:], scalar1=0.25, scalar2=None,
                                                op0=ALU.add, op1=ALU.bypass), [b_rr], [b_rr])
                    wrap_half(rr, b_rr, 512)
                    act(lambda: A.activation(out=ro[0][:], in_=rr[:], func=AF.Sin, scale=SIN_SCALE),
                        [b_rr], [b_ro[0]])
                    P.dma("sp", rope_d[0, :, NCTX + cch * 512:NCTX + (cch + 1) * 512], ro[0][:],
                          reads=[b_ro[0]], writes=[bd["rope"]])

            if stage_on["ada"]:
                cst = sb("cst", [32, 128], scope=ph)
                b_cst = Buf("cst")
                siluT = sb("siluT", [128, 16, 2], scope=ph)
                b_siluT = Buf("siluT")
                P.dma("sp", cst[:], cc_in.rearrange("j (k p) -> (j k) p", p=128), writes=[b_cst])
                tr(psum[2][:, :32], cst[:, :], [b_cst], pbuf[2])
                act(lambda: A.activation(out=siluT[:].rearrange("p k j -> p j k"),
                                         in_=psum[2][:, :32].rearrange("p (j k) -> p j k", j=2),
                                         func=AF.Silu), [pbuf[2]], [b_siluT])
                wad = [sb(f"wad{i}", [128, 16, 256], scope=ph) for i in range(3)]
                b_wad = [Buf(f"wad{i}") for i in range(3)]
                kk = 0
                for l in range(nlayers):
                    wv = Wd["w_ada"][l].rearrange("(kt p) c -> p kt c", p=128)
                    for cb in range(48):
                        s = kk % 3
                        pb = 4 + kk % 2
                        kk += 1
                        P.dma("sp", wad[s][:], wv[:, :, cb * 256:(cb + 1) * 256], writes=[b_wad[s]])
                        for j in range(2):
                            for kt in range(KT):
                                mm(psum[pb][:, 2 * j:2 * j + 2], wad[s][:, kt, j * 128:(j + 1) * 128],
                                   siluT[:, kt, :], kt == 0, kt == KT - 1, [b_wad[s], b_siluT], pbuf[pb])
                        for j in range(2):
                            ft = cb * 2 + j
                            dve(lambda: V.tensor_scalar(out=modT[:, l, ft, :], in0=psum[pb][:, 2 * j:2 * j + 2],
                                                        scalar1=vcol("b_ada", l * 96 + ft), scalar2=None,
                                                        op0=ALU.add, op1=ALU.bypass),
                                [pbuf[pb], b_vecT], [b_modT])
                    for c0_ in (16, 64):
                        dve(lambda: V.tensor_scalar(out=modT[:, l, c0_:c0_ + 16, :], in0=modT[:, l, c0_:c0_ + 16, :],
                                                    scalar1=1.0, scalar2=None, op0=ALU.add, op1=ALU.bypass),
                            [b_modT], [b_modT])
                if debug:
                    P.dma("sp", dbg_mod, modT[:].rearrange("p l f j -> p (l f j)"), reads=[b_modT], writes=[])
        P.barrier()

        def mcol(l, chunk, f, col):
            return modT[:, l, chunk * 16 + f, col:col + 1]

        with contextlib.ExitStack() as ph:
            xin = [sb(f"xin{i}", [128, D], scope=ph) for i in range(2)]
            b_xin = [Buf(f"xin{i}") for i in range(2)]
            stg = [sb(f"stg{i}", [128, 4, 128], scope=ph) for i in range(2)]
            b_stg = [Buf(f"stg{i}") for i in range(2)]
            k = 0
            for ti in range(NT):
                src = ctx_in[ti * 128:(ti + 1) * 128, :] if ti < 2 else \
                    x_in[(ti - 2) * 128:(ti - 1) * 128, :]
                s = ti % 2
                P.dma("sp", xin[s][:], src, writes=[b_xin[s]])
                for g4 in range(4):
                    pb = k % 2
                    for j in range(4):
                        ft = g4 * 4 + j
                        tr(psum[pb][:, j * 128:(j + 1) * 128], xin[s][:, ft * 128:(ft + 1) * 128],
                           [b_xin[s]], pbuf[pb])
                    ss = k % 2
                    dve(lambda: V.tensor_copy(out=stg[ss][:].rearrange("p a b -> p (a b)"), in_=psum[pb][:]),
                        [pbuf[pb]], [b_stg[ss]])
                    P.dma("sp", xT_v[:, g4 * 4:(g4 + 1) * 4, ti * 128:(ti + 1) * 128], stg[ss][:],
                          reads=[b_stg[ss]], writes=xT_bufs(ti * 128, 128))
                    k += 1
        P.barrier()

        for l in range(nlayers):
            if stage_on["A"]:
                with contextlib.ExitStack() as ph:
                    phase_A(nc, P, ph, l, locals())
                P.barrier()
            if stage_on["gla"]:
                with contextlib.ExitStack() as ph:
                    phase_gla(nc, P, ph, l, locals())
                P.barrier()
            if stage_on["s5"]:
                with contextlib.ExitStack() as ph:
                    phase_s5(nc, P, ph, l, locals())
                P.barrier()
            if stage_on["att"]:
                with contextlib.ExitStack() as ph:
                    phase_att(nc, P, ph, l, locals())
                P.barrier()
            if stage_on["C"]:
                with contextlib.ExitStack() as ph:
                    phase_C(nc, P, ph, l, locals())
                P.barrier()

        with contextlib.ExitStack() as ph:
            xf = [sb(f"xf{i}", [128, KT, 128], scope=ph) for i in range(2)]
            b_xf = [Buf(f"xf{i}") for i in range(2)]
            ost = [sb(f"ost{i}", [128, D], scope=ph) for i in range(2)]
            b_ost = [Buf(f"ost{i}") for i in range(2)]
            k = 0
            toks = []
            for ti in range(2, NT):
                s = ti % 2
                P.dma("sp", xf[s][:], xT_v[:, :, ti * 128:(ti + 1) * 128],
                      reads=xT_bufs(ti * 128, 128), writes=[b_xf[s]])
                for g4 in range(4):
                    pb = k % 2
                    for j in range(4):
                        ft = g4 * 4 + j
                        tr(psum[pb][:, j * 128:(j + 1) * 128], xf[s][:, ft, :], [b_xf[s]], pbuf[pb])
                    dve(lambda: V.tensor_copy(out=ost[s][:, g4 * 512:(g4 + 1) * 512], in_=psum[pb][:]),
                        [pbuf[pb]], [b_ost[s]])
                    k += 1
                toks.append(P.dma("sp", out_d[(ti - 2) * 128:(ti - 1) * 128, :], ost[s][:],
                                  reads=[b_ost[s]], writes=[]))
            for t in toks:
                P._wait("sp", t)
        print("instructions:", P.n_inst)
    return nc


class NS:
    def __init__(self, d):
        self.__dict__.update(d)


def phase_A(nc, P, ph, l, env):
    E = NS(env)
    V, A = nc.vector, nc.scalar
    sb, psum, pbuf, Wd, bd = E.sb, E.psum, E.pbuf, E.Wd, E.bd
    mm, tr, dve, act = E.mm, E.tr, E.dve, E.act
    L = f"A{l}"
    h1 = sb(L + "h1", [128, KT, 512], F32R, scope=ph)
    b_h1 = [Buf(f"h1_{i}") for i in range(KT)]
    xs = [sb(L + f"xs{i}", [128, 512], scope=ph) for i in range(3)]
    b_xs = [Buf(f"xs{i}") for i in range(3)]
    NW = 3
    wb = [sb(L + f"wb{i}", [128, 4096], F32R, scope=ph) for i in range(NW)]
    b_wb = [Buf(f"wb{i}") for i in range(NW)]
    wglr = sb(L + "wglr", [128, KT, 16], scope=ph)
    b_wglr = Buf("wglr")
    ost = [sb(L + f"ost{i}", [128, 512], scope=ph) for i in range(4)]
    b_ost = [Buf(f"ost{i}") for i in range(4)]
    cosT = sb(L + "cosT", [128, T], scope=ph)
    sinT = sb(L + "sinT", [128, T], scope=ph)
    b_rope = Buf("ropesb")
    sq = sb(L + "sq", [128, 512], F32R, scope=ph)
    b_sq = Buf("sq")
    rs = sb(L + "rs", [128, 512], scope=ph)
    b_rs = Buf("rs")
    qn = sb(L + "qn", [128, 512], F32R, scope=ph)
    b_qn = Buf("qn")
    t1 = sb(L + "t1", [128, 512], scope=ph)
    b_t1 = Buf("t1")
    t2 = sb(L + "t2", [128, 512], scope=ph)
    b_t2 = Buf("t2")
    cnt = {"w": 0, "o": 0, "x": 0, "p": 0}

    P.dma("sp", cosT[:], E.rope_d[0], reads=[bd["rope"]], writes=[b_rope])
    P.dma("sp", sinT[:], E.rope_d[1], reads=[bd["rope"]], writes=[b_rope])
    P.dma("sp", wglr[:], Wd["w_in"][l].rearrange("(kt p) c -> p kt c", p=128)[:, :, OFF_GLR:OFF_GLR + 16],
          writes=[b_wglr])
    wv = Wd["w_in"][l].rearrange("(kt p) c -> p kt c", p=128)

    def wload(c0, ncols):
        s = cnt["w"] % NW
        cnt["w"] += 1
        view = wb[s][:, :KT * ncols].rearrange("p (k c) -> p k c", c=ncols)
        P.dma("pool", view, wv[:, :, c0:c0 + ncols], writes=[b_wb[s]])
        return view, b_wb[s]

    def nps():
        p = cnt["p"] % 4
        cnt["p"] += 1
        return p

    def nost():
        s = cnt["o"] % 4
        cnt["o"] += 1
        return s

    for bi, (t0, n) in enumerate(TBLK):
        col = 1 if bi == 0 else 0
        for ft in range(KT):
            s = cnt["x"] % 3
            cnt["x"] += 1
            P.dma("sp", xs[s][:, :n], E.xT_v[:, ft, t0:t0 + n], reads=[E.b_xT[bi]], writes=[b_xs[s]])
            dve(lambda: V.tensor_scalar(out=h1[:, ft, :n], in0=xs[s][:, :n],
                                        scalar1=E.mcol(l, 1, ft, col), scalar2=E.mcol(l, 0, ft, col),
                                        op0=ALU.mult, op1=ALU.add),
                [b_xs[s], E.b_modT], [b_h1[ft]])

        def form1(c0, ntile, post):
            for cb in range(0, ntile, 2):
                nb = min(2, ntile - cb)
                view, bw = wload(c0 + cb * 128, nb * 128)
                for j in range(nb):
                    p = nps()
                    for kt in range(KT):
                        mm(psum[p][:, :n], view[:, kt, j * 128:(j + 1) * 128], h1[:, kt, :n],
                           kt == 0, kt == KT - 1, [bw, b_h1[kt]], pbuf[p])
                    post(cb + j, p)

        def store_fm(dst, name, fi, src_tile, b_src):
            P.dma("sp", dst[fi * 128:(fi + 1) * 128, t0:t0 + n], src_tile[:, :n],
                  reads=[b_src], writes=[bd[name]])

        def post_copy(dst, name, scale=None):
            def f(fi, p):
                s = nost()
                if scale is None:
                    act(lambda: A.copy(out=ost[s][:, :n], in_=psum[p][:, :n]), [pbuf[p]], [b_ost[s]])
                else:
                    act(lambda: A.mul(out=ost[s][:, :n], in_=psum[p][:, :n], mul=scale), [pbuf[p]], [b_ost[s]])
                store_fm(dst, name, fi, ost[s], b_ost[s])
            return f

        def post_qk(dst, name, wname):
            def f(fi, p):
                act(lambda: A.activation(out=sq[:, :n], in_=psum[p][:, :n], func=AF.Square),
                    [pbuf[p]], [b_sq])
                p2 = 4 + (cnt["p"] % 2)
                mm(psum[p2][:, :n], E.onesr[:], sq[:, :n], True, True, [E.b_onesr, b_sq], pbuf[p2])
                act(lambda: A.activation(out=rs[:, :n], in_=psum[p2][:, :n], func=AF.Sqrt,
                                         bias=E.epsc[:, 0:1], scale=1.0 / HD), [pbuf[p2], E.b_epsc], [b_rs])
                dve(lambda: V.reciprocal(out=rs[:, :n], in_=rs[:, :n]), [b_rs], [b_rs])
                dve(lambda: V.scalar_tensor_tensor(out=qn[:, :n], in0=psum[p][:, :n],
                                                   scalar=E.vcol(wname, l), in1=rs[:, :n],
                                                   op0=ALU.mult, op1=ALU.mult),
                    [pbuf[p], b_rs, E.b_vecT], [b_qn])
                p3 = 6 + (cnt["p"] % 2)
                mm(psum[p3][:, :n], E.prot[:], qn[:, :n], True, True, [E.b_prot, b_qn], pbuf[p3])
                dve(lambda: V.tensor_tensor(out=t1[:, :n], in0=qn[:, :n].bitcast(F32), in1=cosT[:, t0:t0 + n],
                                            op=ALU.mult), [b_qn, b_rope], [b_t1])
                dve(lambda: V.tensor_tensor(out=t2[:, :n], in0=psum[p3][:, :n], in1=sinT[:, t0:t0 + n],
                                            op=ALU.mult), [pbuf[p3], b_rope], [b_t2])
                s = nost()
                dve(lambda: V.tensor_tensor(out=ost[s][:, :n], in0=t1[:, :n], in1=t2[:, :n], op=ALU.add),
                    [b_t1, b_t2], [b_ost[s]])
                store_fm(dst, name, fi, ost[s], b_ost[s])
            return f

        form1(OFF_GQ, 4, post_copy(E.qgT, "qgT", GLA_DK ** -0.5))
        form1(OFF_GK, 4, post_copy(E.kgT, "kgT"))
        form1(OFF_SU, 6, post_copy(E.suT, "suT"))
        form1(OFF_AQ, 8, post_qk(E.qaT, "qaT", "q_norm_w"))
        form1(OFF_AK, 2, post_qk(E.kaT, "kaT", "k_norm_w"))
        p = nps()
        for kt in range(KT):
            mm(psum[p][:16, :n], wglr[:, kt, :], h1[:, kt, :n].bitcast(F32), kt == 0, kt == KT - 1,
               [b_wglr, b_h1[kt]], pbuf[p])
        s = nost()
        act(lambda: A.copy(out=ost[s][:16, :n], in_=psum[p][:16, :n]), [pbuf[p]], [b_ost[s]])
        P.dma("sp", E.glrT[:, t0:t0 + n], ost[s][:16, :n], reads=[b_ost[s]], writes=[bd["glrT"]])

        def form2(c0, ncols, dst, name, silu):
            for cb in range(0, ncols, 256):
                view, bw = wload(c0 + cb, 256)
                for tt in range(n // 128):
                    p = nps()
                    for kt in range(KT):
                        mm(psum[p][:, :256], h1[:, kt, tt * 128:(tt + 1) * 128], view[:, kt, :],
                           kt == 0, kt == KT - 1, [bw, b_h1[kt]], pbuf[p])
                    s = nost()
                    if silu:
                        act(lambda: A.activation(out=ost[s][:, :256], in_=psum[p][:, :256], func=AF.Silu),
                            [pbuf[p]], [b_ost[s]])
                    else:
                        act(lambda: A.copy(out=ost[s][:, :256], in_=psum[p][:, :256]), [pbuf[p]], [b_ost[s]])
                    r0 = t0 + tt * 128
                    P.dma("sp", dst[r0:r0 + 128, cb:cb + 256], ost[s][:, :256],
                          reads=[b_ost[s]], writes=[bd[name]])

        form2(OFF_GV, 1024, E.vg, "vg", False)
        form2(OFF_GR, 1024, E.grt, "grt", True)
        form2(OFF_AV, 256, E.va, "va", False)


def phase_gla(nc, P, ph, l, env):
    E = NS(env)
    V, A = nc.vector, nc.scalar
    sb, psum, pbuf, Wd, bd, Cd = E.sb, E.psum, E.pbuf, E.Wd, E.bd, E.Cd
    mm, tr, dve, act = E.mm, E.tr, E.dve, E.act
    L = f"G{l}"

    def t_(name, shape, dt=F32):
        return sb(L + name, shape, dt, scope=ph), Buf(name)

    cmask, b_cmask = t_("cmask", [128, T])
    trim, b_trim = t_("trim", [64, 2, 64])
    wg, b_wg = t_("wg", [16, 2, 512])
    glr, b_glr = t_("glr", [16, T])
    gnw, b_gnw = t_("gnw", [64, 1024])
    y2, b_y2 = t_("y2", [128, T])
    cs, b_cs = t_("cs", [128, T])
    eb, b_eb = t_("eb", [128, T])
    qe, b_qe = t_("qe", [128, T])
    ke, b_ke = t_("ke", [128, T])
    kd, b_kd = t_("kd", [128, T])
    v_tm, b_v = t_("v_tm", [64, NCH, 256])
    kd_tm, b_kdt = t_("kd_tm", [64, NCH, 128])
    o_acc, b_oacc = t_("o_acc", [64, NCH, 256])
    b_oc = [Buf(f"oacc{c}") for c in range(NCH)]
    S, b_S = t_("S", [128, 256])
    attm = [t_(f"attm{i}", [64, 64]) for i in range(2)]
    grs = [t_(f"grs{i}", [64, 256]) for i in range(2)]
    on = [t_(f"on{i}", [64, 256]) for i in range(2)]
    ots = [t_(f"ots{i}", [128, 2, 64]) for i in range(2)]
    stats, b_stats = t_("stats", [64, 6])
    mv, b_mv = t_("mv", [64, 2])
    rstd, b_rstd = t_("rstd", [64, 1])

    P.dma("sp", cmask[:], Cd["cmask"][:, :], writes=[b_cmask])
    P.dma("sp", trim[:, 0, :], Cd["trif"][:, :], writes=[b_trim])
    P.dma("sp", trim[:, 1, :], Cd["trib"][:, :], writes=[b_trim])
    P.dma("sp", wg[:], Wd["w_gla_gate"][l].rearrange("d r c -> r d c"), writes=[b_wg])
    P.dma("sp", glr[:], E.glrT[:, :], reads=[bd["glrT"]], writes=[b_glr])
    P.dma("sp", gnw[:], Wd["gla_norm_w"][l:l + 1, :].to_broadcast([64, 1024]), writes=[b_gnw])
    kk = 0
    for h in range(GLA_H):
        P.dma("sp", v_tm[:], E.vg[:, h * 256:(h + 1) * 256].rearrange("(c p) f -> p c f", p=64),
              reads=[bd["vg"]], writes=[b_v])
        for d in range(2):
            P.dma("sp", qe[:], E.qgT[h * 128:(h + 1) * 128, :], reads=[bd["qgT"]], writes=[b_qe])
            P.dma("sp", ke[:], E.kgT[h * 128:(h + 1) * 128, :], reads=[bd["kgT"]], writes=[b_ke])
            negb = E.vcol("b_gla_gate", l * 8 + d * 4 + h)
            for bi, (t0, n) in enumerate(TBLK):
                pb = bi % 2
                mm(psum[pb][:, :n], wg[:, d, h * 128:(h + 1) * 128], glr[:, t0:t0 + n], True, True,
                   [b_wg, b_glr], pbuf[pb])
                act(lambda: A.activation(out=y2[:, t0:t0 + n], in_=psum[pb][:, :n], func=AF.Exp,
                                         scale=-1.0, bias=negb), [pbuf[pb], E.b_vecT], [b_y2])
            act(lambda: A.activation(out=y2[:], in_=y2[:], func=AF.Ln, bias=1.0, scale=1.0), [b_y2], [b_y2])
            if d == 0:
                dve(lambda: V.tensor_tensor_scan(out=cs[:], data0=cmask[:], data1=y2[:], initial=0.0,
                                                 op0=ALU.mult, op1=ALU.add), [b_cmask, b_y2], [b_cs])
            else:
                dve(lambda: V.tensor_tensor_scan(out=cs[:, ::-1], data0=cmask[:], data1=y2[:, ::-1],
                                                 initial=0.0, op0=ALU.mult, op1=ALU.add),
                    [b_cmask, b_y2], [b_cs])
            if E.debug and h == 0 and d == 0 and l == 0:
                P.dma("sp", E.dbg_y2, y2[:], reads=[b_y2], writes=[])
                P.dma("sp", E.dbg_cs, cs[:], reads=[b_cs], writes=[])
            act(lambda: A.activation(out=eb[:], in_=cs[:], func=AF.Exp, scale=-1.0 / GLA_TAU), [b_cs], [b_eb])
            act(lambda: A.activation(out=y2[:], in_=cs[:], func=AF.Exp, scale=1.0 / GLA_TAU), [b_cs], [b_y2])
            dve(lambda: V.tensor_tensor(out=qe[:], in0=qe[:], in1=eb[:], op=ALU.mult), [b_qe, b_eb], [b_qe])
            dve(lambda: V.tensor_tensor(out=ke[:], in0=ke[:], in1=y2[:], op=ALU.mult), [b_ke, b_y2], [b_ke])
            lpos = CH - 1 if d == 0 else 0
            eb3 = eb[:].rearrange("p (c i) -> p c i", i=CH)
            dve(lambda: V.tensor_tensor(out=kd[:].rearrange("p (c i) -> p c i", i=CH),
                                        in0=ke[:].rearrange("p (c i) -> p c i", i=CH),
                                        in1=eb3[:, :, lpos:lpos + 1].to_broadcast([128, NCH, CH]),
                                        op=ALU.mult), [b_ke, b_eb], [b_kd])
            for c4 in range(NCH // 4):
                pb = 2 + c4 % 2
                for j in range(4):
                    c = c4 * 4 + j
                    tr(psum[pb][:64, j * 128:(j + 1) * 128], kd[:, c * CH:(c + 1) * CH], [b_kd], pbuf[pb])
                act(lambda: A.copy(out=kd_tm[:, c4 * 4:(c4 + 1) * 4, :].rearrange("p a b -> p (a b)"),
                                   in_=psum[pb][:64, :]), [pbuf[pb]], [b_kdt])
            if E.debug and h == 0 and l == 0 and d == 1:
                P.dma("sp", E.dbg_oraw, o_acc[:].rearrange("p a b -> p (a b)"), reads=b_oc, writes=[])
            dve(lambda: V.memset(S[:], 0.0), [], [b_S])
            order = list(range(NCH)) if d == 0 else [3, 2, 1, 0] + list(range(NCH - 1, 3, -1))
            for c in order:
                sl = slice(c * CH, (c + 1) * CH)
                s = kk % 2
                kk += 1
                am, b_am = attm[s]
                mm(psum[4][:64, :64], ke[:, sl], qe[:, sl], True, True, [b_ke, b_qe], pbuf[4])
                dve(lambda: V.tensor_tensor(out=am[:], in0=psum[4][:64, :64], in1=trim[:, d, :], op=ALU.mult),
                    [pbuf[4], b_trim], [b_am])
                mm(psum[5][:64, :256], am[:], v_tm[:, c, :], True, False, [b_am, b_v], pbuf[5])
                mm(psum[5][:64, :256], qe[:, sl], S[:], False, True, [b_qe, b_S], pbuf[5])
                mm(psum[6][:, :256], kd_tm[:, c, :], v_tm[:, c, :], True, True, [b_kdt, b_v], pbuf[6])
                if d == 0:
                    act(lambda: A.copy(out=o_acc[:, c, :], in_=psum[5][:64, :256]), [pbuf[5]], [b_oc[c]])
                else:
                    dve(lambda: V.tensor_tensor(out=o_acc[:, c, :], in0=psum[5][:64, :256], in1=o_acc[:, c, :],
                                                op=ALU.add), [pbuf[5], b_oc[c]], [b_oc[c]])
                li = c * CH + lpos
                dve(lambda: V.scalar_tensor_tensor(out=S[:], in0=S[:], scalar=eb[:, li:li + 1], in1=psum[6][:, :256],
                                                   op0=ALU.mult, op1=ALU.add), [b_S, b_eb, pbuf[6]], [b_S])
        for c in range(NCH):
            s = c % 2
            ont, b_on = on[s]
            grt_, b_gr = grs[s]
            ott, b_ot = ots[s]
            P.dma("sp", grt_[:], E.grt[c * CH:(c + 1) * CH, h * 256:(h + 1) * 256], reads=[bd["grt"]],
                  writes=[b_gr])
            dve(lambda: V.bn_stats(out=stats[:], in_=o_acc[:, c, :]), [b_oc[c]], [b_stats])
            dve(lambda: V.bn_aggr(out=mv[:], in_=stats[:]), [b_stats], [b_mv])
            act(lambda: A.activation(out=rstd[:], in_=mv[:, 1:2], func=AF.Sqrt, bias=E.epsc[:64, 0:1], scale=1.0),
                [b_mv, E.b_epsc], [b_rstd])
            dve(lambda: V.reciprocal(out=rstd[:], in_=rstd[:]), [b_rstd], [b_rstd])
            dve(lambda: V.tensor_scalar(out=ont[:], in0=o_acc[:, c, :], scalar1=mv[:, 0:1], scalar2=rstd[:, 0:1],
                                        op0=ALU.subtract, op1=ALU.mult), [b_oc[c], b_mv, b_rstd], [b_on])
            dve(lambda: V.tensor_tensor(out=ont[:], in0=ont[:], in1=gnw[:, h * 256:(h + 1) * 256], op=ALU.mult),
                [b_on, b_gnw], [b_on])
            dve(lambda: V.tensor_tensor(out=ont[:], in0=ont[:], in1=grt_[:], op=ALU.mult), [b_on, b_gr], [b_on])
            for half in range(2):
                tr(psum[7][:, half * 64:(half + 1) * 64], ont[:, half * 128:(half + 1) * 128], [b_on], pbuf[7])
            act(lambda: A.copy(out=ott[:].rearrange("p a b -> p (a b)"), in_=psum[7][:, :128]), [pbuf[7]], [b_ot])
            P.dma("sp", E.oglaT[h * 256:(h + 1) * 256, c * CH:(c + 1) * CH].rearrange("(a p) t -> p a t", p=128),
                  ott[:], reads=[b_ot], writes=[bd["oglaT"]])


def phase_s5(nc, P, ph, l, env):
    E = NS(env)
    V, A = nc.vector, nc.scalar
    sb, psum, pbuf, Wd, bd, Cd = E.sb, E.psum, E.pbuf, E.Wd, E.bd, E.Cd
    mm, tr, dve, act = E.mm, E.tr, E.dve, E.act
    L = f"S{l}"
    NG = S5_NG

    def t_(name, shape, dt=F32):
        return sb(L + name, shape, dt, scope=ph), Buf(name)

    def ts(out, in0, s1, s2, op0, op1, rd, wr):
        dve(lambda: V.tensor_scalar(out=out, in0=in0, scalar1=s1, scalar2=s2, op0=op0, op1=op1), rd, wr)

    def tt(out, in0, in1, op, rd, wr):
        dve(lambda: V.tensor_tensor(out=out, in0=in0, in1=in1, op=op), rd, wr)

    ri, b_ri = t_("ri", [128, 512], I32)
    rf, b_rf = t_("rf", [128, 512])

    def wrap_half(r, b_r):
        dve(lambda: V.tensor_single_scalar(out=rf[:, :r.shape[1]], in_=r, scalar=0.5, op=ALU.is_gt), [b_r], [b_rf])
        tt(r, r, rf[:, :r.shape[1]], ALU.subtract, [b_r, b_rf], [b_r])
        dve(lambda: V.tensor_single_scalar(out=rf[:, :r.shape[1]], in_=r, scalar=-0.5, op=ALU.is_lt), [b_r], [b_rf])
        tt(r, r, rf[:, :r.shape[1]], ALU.add, [b_r, b_rf], [b_r])

    def frac_wrap(r, b_r):
        n = r.shape[1]
        dve(lambda: V.tensor_copy(out=ri[:, :n], in_=r), [b_r], [b_ri])
        dve(lambda: V.tensor_copy(out=rf[:, :n], in_=ri[:, :n]), [b_ri], [b_rf])
        tt(r, r, rf[:, :n], ALU.subtract, [b_r, b_rf], [b_r])
        wrap_half(r, b_r)

    def sincos(r, b_r, sin_out, b_s, cos_out, b_c):
        frac_wrap(r, b_r)
        act(lambda: A.activation(out=sin_out, in_=r, func=AF.Sin, scale=SIN_SCALE), [b_r], [b_s])
        ts(r, r, 0.25, None, ALU.add, ALU.bypass, [b_r], [b_r])
        wrap_half(r, b_r)
        act(lambda: A.activation(out=cos_out, in_=r, func=AF.Sin, scale=SIN_SCALE), [b_r], [b_c])

    iota, b_iota = t_("iota", [128, 512])
    P.dma("sp", iota[:], Cd["iota512"][:, :], writes=[b_iota])
    rmask, b_rmask = t_("rmask", [128, 8])
    P.dma("sp", rmask[:], Cd["rmask"][:, :], writes=[b_rmask])
    colmask, b_colmask = t_("colmask", [128, 8, 128])
    P.dma("sp", colmask[:].rearrange("p a b -> p (a b)"), Cd["colmask"][:, :], writes=[b_colmask])
    pst, b_pst = t_("pst", [128, 128])
    par = {}
    NOFF = [256, 768, 1280, 1792]
    for d in range(2):
        pp = {}
        for nm in ("lamr", "lami", "dt", "rho", "thp", "sn", "cs", "kre", "kim", "w1", "w2", "w3"):
            pp[nm] = t_(f"{nm}{d}", [128, NG])
        pp["Ec"] = t_(f"Ec{d}", [128, 4, NG])
        pp["Es"] = t_(f"Es{d}", [128, 4, NG])
        par[d] = pp
        for nm, src in (("lamr", "s5_lam_re"), ("lami", "s5_lam_im")):
            P.dma("sp", pst[:NG, 0:64], Wd[src][l, d], writes=[b_pst])
            P.dma("sp", pst[:NG, 64:128], Wd[src][l, d], writes=[b_pst])
            tr(psum[0][:, :NG], pst[:NG, :], [b_pst], pbuf[0])
            dve(lambda: V.tensor_copy(out=pp[nm][0][:], in_=psum[0][:, :NG]), [pbuf[0]], [pp[nm][1]])
        dtt, b_dt = pp["dt"]
        P.dma("sp", dtt[:], Wd["s5_log_dt"][l, d:d + 1, :].to_broadcast([128, NG]), writes=[b_dt])
        act(lambda: A.activation(out=dtt[:], in_=dtt[:], func=AF.Exp), [b_dt], [b_dt])
        lamr, b_lamr = pp["lamr"]
        lami, b_lami = pp["lami"]
        rho, b_rho = pp["rho"]
        thp, b_thp = pp["thp"]
        sn, b_sn = pp["sn"]
        cs_, b_cs = pp["cs"]
        kre, b_kre = pp["kre"]
        kim, b_kim = pp["kim"]
        w1, b_w1 = pp["w1"]
        w2, b_w2 = pp["w2"]
        w3, b_w3 = pp["w3"]
        tt(w1[:], lamr[:], dtt[:], ALU.mult, [b_lamr, b_dt], [b_w1])
        act(lambda: A.activation(out=rho[:], in_=w1[:], func=AF.Exp), [b_w1], [b_rho])
        tt(thp[:], lami[:], dtt[:], ALU.mult, [b_lami, b_dt], [b_thp])
        ts(thp[:], thp[:], 1.0 / TWO_PI, None, ALU.mult, ALU.bypass, [b_thp], [b_thp])
        dve(lambda: V.tensor_copy(out=w1[:], in_=thp[:]), [b_thp], [b_w1])
        sincos(w1[:], b_w1, sn[:], b_sn, cs_[:], b_cs)
        tt(w1[:], rho[:], cs_[:], ALU.mult, [b_rho, b_cs], [b_w1])
        ts(w1[:], w1[:], -1.0, None, ALU.add, ALU.bypass, [b_w1], [b_w1])
        tt(w2[:], rho[:], sn[:], ALU.mult, [b_rho, b_sn], [b_w2])
        tt(w3[:], lamr[:], lamr[:], ALU.mult, [b_lamr], [b_w3])
        tt(kre[:], lami[:], lami[:], ALU.mult, [b_lami], [b_kre])
        tt(w3[:], w3[:], kre[:], ALU.add, [b_w3, b_kre], [b_w3])
        dve(lambda: V.reciprocal(out=w3[:], in_=w3[:]), [b_w3], [b_w3])
        tt(kre[:], w1[:], lamr[:], ALU.mult, [b_w1, b_lamr], [b_kre])
        tt(kim[:], w2[:], lami[:], ALU.mult, [b_w2, b_lami], [b_kim])
        tt(kre[:], kre[:], kim[:], ALU.add, [b_kre, b_kim], [b_kre])
        tt(kre[:], kre[:], w3[:], ALU.mult, [b_kre, b_w3], [b_kre])
        tt(kim[:], w2[:], lamr[:], ALU.mult, [b_w2, b_lamr], [b_kim])
        tt(w2[:], w1[:], lami[:], ALU.mult, [b_w1, b_lami], [b_w2])
        tt(kim[:], kim[:], w2[:], ALU.subtract, [b_kim, b_w2], [b_kim])
        tt(kim[:], kim[:], w3[:], ALU.mult, [b_kim, b_w3], [b_kim])
        Ec, b_Ec = pp["Ec"]
        Es, b_Es = pp["Es"]
        for k, noff in enumerate(NOFF):
            ts(w1[:], thp[:], float(noff), None, ALU.mult, ALU.bypass, [b_thp], [b_w1])
            sincos(w1[:], b_w1, Es[:, k, :], b_Es, Ec[:, k, :], b_Ec)

    bre, b_bre = t_("bre", [128, NG, 16])
    bim, b_bim = t_("bim", [128, NG, 16])
    for (tile_, b_, src) in ((bre, b_bre, "s5_b_re"), (bim, b_bim, "s5_b_im")):
        v_ = Wd[src][l].rearrange("g p c -> p g c")
        P.dma("sp", tile_[0:64], v_, writes=[b_])
        P.dma("sp", tile_[64:128], v_, writes=[b_])
    Pre, b_Pre = t_("Pre", [128, NG, 16])
    Pim, b_Pim = t_("Pim", [128, NG, 16])
    pw, b_pw = t_("pw", [128, NG, 16])
    X1, b_X1 = t_("X1", [128, NG * 16])
    X2, b_X2 = t_("X2", [128, NG * 16])
    TB = [t_(f"TB{i}", [128, 128]) for i in range(2)]
    TW = [t_(f"TW{i}", [128, 128]) for i in range(2)]
    cst, b_cst = t_("cstg", [128, 2, 128])
    LB = [t_(f"LB{i}", [128, 128], F32R) for i in range(2)]
    WG = [t_(f"WG{i}", [128, 128], F32R) for i in range(2)]

    suo, b_suo = t_("suo", [128, T], F32R)
    yacc, b_yacc = t_("yacc", [128, T])
    yg = [t_(f"yg{i}", [128, T], F32R) for i in range(6)]
    cosm, b_cosm = t_("cosm", [128, 512])
    sinm, b_sinm = t_("sinm", [128, 512])
    cn, b_cn = t_("cn", [128, 512])
    sn_, b_sn_ = t_("snk", [128, 512])
    tq, b_tq = t_("tq", [128, 512])
    m1, b_m1 = t_("m1", [128, 512])
    vv, b_vv = t_("vv", [128, 512])
    zz, b_zz = t_("zz", [128, 512])
    P1, b_P1 = t_("P1", [128, 512], F32R)
    P2, b_P2 = t_("P2", [128, 512], F32R)
    rhot, b_rhot = t_("rhot", [128, 512])
    ones5, b_ones5 = t_("ones5", [128, 512])
    st, b_st = t_("st", [128, 1])
    dve(lambda: V.memset(ones5[:], 1.0), [], [b_ones5])

    TILES = {0: [(0, 256), (256, 512), (768, 512), (1280, 512), (1792, 512)],
             1: [(0, 256), (1792, 512), (1280, 512), (768, 512), (256, 512)]}

    for o in range(6):
        P.dma("pool", suo[:], E.suT[o * 128:(o + 1) * 128, :], reads=[bd["suT"]], writes=[b_suo])
        P.dma("sp", yacc[:], E.suT[o * 128:(o + 1) * 128, :], reads=[bd["suT"]], writes=[b_yacc])
        ts(yacc[:], yacc[:], E.vcol("s5_d", l * 6 + o), None, ALU.mult, ALU.bypass, [b_yacc, E.b_vecT], [b_yacc])
        for d in range(2):
            pp = par[d]
            kre, b_kre = pp["kre"]
            kim, b_kim = pp["kim"]
            kreb = kre[:].unsqueeze(2).to_broadcast([128, NG, 16])
            kimb = kim[:].unsqueeze(2).to_broadcast([128, NG, 16])
            tt(Pre[:], bre[:], kreb, ALU.mult, [b_bre, b_kre], [b_Pre])
            tt(pw[:], bim[:], kimb, ALU.mult, [b_bim, b_kim], [b_pw])
            tt(Pre[:], Pre[:], pw[:], ALU.subtract, [b_Pre, b_pw], [b_Pre])
            tt(Pim[:], bim[:], kreb, ALU.mult, [b_bim, b_kre], [b_Pim])
            tt(pw[:], bre[:], kimb, ALU.mult, [b_bre, b_kim], [b_pw])
            tt(Pim[:], Pim[:], pw[:], ALU.add, [b_Pim, b_pw], [b_Pim])
            Pre2 = Pre[:].rearrange("p g c -> p (g c)")
            Pim2 = Pim[:].rearrange("p g c -> p (g c)")
            oc = slice(o * 128, (o + 1) * 128)
            dve(lambda: V.tensor_copy(out=X1[0:64, oc], in_=Pre2[0:64, oc]), [b_Pre], [b_X1])
            dve(lambda: V.tensor_copy(out=X1[64:128, oc], in_=Pim2[64:128, oc]), [b_Pim], [b_X1])
            dve(lambda: V.tensor_copy(out=X2[0:64, oc], in_=Pim2[0:64, oc]), [b_Pim], [b_X2])
            ts(X2[64:128, oc], Pre2[64:128, oc], -1.0, None, ALU.mult, ALU.bypass, [b_Pre], [b_X2])
            for i, (X_, b_X) in enumerate(((X1, b_X1), (X2, b_X2))):
                tr(psum[i][:, :128], X_[:, oc], [b_X], pbuf[i])
                dve(lambda: V.tensor_copy(out=TB[i][0][:], in_=psum[i][:, :128]), [pbuf[i]], [TB[i][1]])
            crv = Wd["s5_c_re"][l, d].rearrange("g c p -> (g c) p")[o * 128:(o + 1) * 128, :]
            civ = Wd["s5_c_im"][l, d].rearrange("g c p -> (g c) p")[o * 128:(o + 1) * 128, :]
            P.dma("sp", cst[:, 0, 0:64], crv, writes=[b_cst])
            P.dma("sp", cst[:, 0, 64:128], civ, writes=[b_cst])
            P.dma("sp", cst[:, 1, 0:64], civ, writes=[b_cst])
            P.dma("sp", cst[:, 1, 64:128], crv, writes=[b_cst])
            for i in range(2):
                tr(psum[2 + i][:, :128], cst[:, i, :], [b_cst], pbuf[2 + i])
            dve(lambda: V.tensor_copy(out=TW[0][0][0:64, :], in_=psum[2][0:64, :128]), [pbuf[2]], [TW[0][1]])
            ts(TW[0][0][64:128, :], psum[2][64:128, :128], -1.0, None, ALU.mult, ALU.bypass, [pbuf[2]], [TW[0][1]])
            ts(TW[1][0][:], psum[3][:, :128], -1.0, None, ALU.mult, ALU.bypass, [pbuf[3]], [TW[1][1]])
            rho, b_rho = pp["rho"]
            thp, b_thp = pp["thp"]
            Ec, b_Ec = pp["Ec"]
            Es, b_Es = pp["Es"]
            for gl in range(8):
                g = o * 8 + gl
                for i in range(2):
                    ts(LB[i][0][:], TB[i][0][:], rmask[:, gl:gl + 1], None, ALU.mult, ALU.bypass,
                       [TB[i][1], b_rmask], [LB[i][1]])
                    tt(WG[i][0][:], TW[i][0][:], colmask[:, gl, :], ALU.mult, [TW[i][1], b_colmask], [WG[i][1]])
                ts(tq[:], iota[:], thp[:, g:g + 1], None, ALU.mult, ALU.bypass, [b_iota, b_thp], [b_tq])
                sincos(tq[:], b_tq, sinm[:], b_sinm, cosm[:], b_cosm)
                ts(rhot[:], ones5[:], rho[:, g:g + 1], None, ALU.mult, ALU.bypass, [b_ones5, b_rho], [b_rhot])
                for k, (f0, n) in enumerate(TILES[d]):
                    rv = (lambda ap: ap[:, ::-1]) if d == 1 else (lambda ap: ap)
                    mm(psum[4][:, :n], LB[0][0][:], suo[:, f0:f0 + n], True, True, [LB[0][1], b_suo], pbuf[4])
                    mm(psum[5][:, :n], LB[1][0][:], suo[:, f0:f0 + n], True, True, [LB[1][1], b_suo], pbuf[5])
                    if k == 0:
                        cnk, b_cnk, snk, b_snk = cosm, b_cosm, sinm, b_sinm
                    else:
                        ec = Ec[:, k - 1, g:g + 1]
                        es = Es[:, k - 1, g:g + 1]
                        ts(tq[:], sinm[:], es, None, ALU.mult, ALU.bypass, [b_sinm, b_Es], [b_tq])
                        dve(lambda: V.scalar_tensor_tensor(out=cn[:], in0=cosm[:], scalar=ec, in1=tq[:],
                                                           op0=ALU.mult, op1=ALU.subtract),
                            [b_cosm, b_Ec, b_tq], [b_cn])
                        ts(tq[:], cosm[:], es, None, ALU.mult, ALU.bypass, [b_cosm, b_Es], [b_tq])
                        dve(lambda: V.scalar_tensor_tensor(out=sn_[:], in0=sinm[:], scalar=ec, in1=tq[:],
                                                           op0=ALU.mult, op1=ALU.add),
                            [b_sinm, b_Ec, b_tq], [b_sn_])
                        cnk, b_cnk, snk, b_snk = cn, b_cn, sn_, b_sn_
                    cv = rv(cnk[:, :n])
                    sv = rv(snk[:, :n])
                    tt(m1[:, :n], psum[4][:, :n], cv, ALU.mult, [pbuf[4], b_cnk], [b_m1])
                    tt(vv[:, :n], psum[5][:, :n], sv, ALU.mult, [pbuf[5], b_snk], [b_vv])
                    tt(vv[:, :n], vv[:, :n], m1[:, :n], ALU.add, [b_vv, b_m1], [b_vv])
                    init = 0.0 if k == 0 else st[:, 0:1]
                    dve(lambda: V.tensor_tensor_scan(out=rv(zz[:, :n]), data0=rhot[:, :n], data1=rv(vv[:, :n]),
                                                     initial=init, op0=ALU.mult, op1=ALU.add),
                        [b_rhot, b_vv, b_st], [b_zz])
                    lastc = n - 1 if d == 0 else 0
                    act(lambda: A.copy(out=st[:], in_=zz[:, lastc:lastc + 1]), [b_zz], [b_st])
                    tt(P1[:, :n], zz[:, :n], cv, ALU.mult, [b_zz, b_cnk], [b_P1])
                    tt(P2[:, :n], zz[:, :n], sv, ALU.mult, [b_zz, b_snk], [b_P2])
                    mm(psum[6][:, :n], WG[0][0][:], P1[:, :n], True, False, [WG[0][1], b_P1], pbuf[6])
                    mm(psum[6][:, :n], WG[1][0][:], P2[:, :n], False, True, [WG[1][1], b_P2], pbuf[6])
                    tt(yacc[:, f0:f0 + n], psum[6][:, :n], yacc[:, f0:f0 + n], ALU.add, [pbuf[6], b_yacc], [b_yacc])
        ygt, b_yg = yg[o]
        for (t0, n) in TBLK:
            xa = yacc[:, t0:t0 + n]
            act(lambda: A.activation(out=m1[:, :n], in_=xa, func=AF.Square), [b_yacc], [b_m1])
            ts(m1[:, :n], m1[:, :n], 0.044715, 1.0, ALU.mult, ALU.add, [b_m1], [b_m1])
            tt(m1[:, :n], m1[:, :n], xa, ALU.mult, [b_m1, b_yacc], [b_m1])
            act(lambda: A.activation(out=vv[:, :n], in_=m1[:, :n], func=AF.Sigmoid,
                                     scale=2.0 * math.sqrt(2.0 / math.pi)), [b_m1], [b_vv])
            tt(ygt[:, t0:t0 + n], xa, vv[:, :n], ALU.mult, [b_yacc, b_vv], [b_yg])
    wgl, b_wgl = t_("wgl", [128, 6, 128], F32R)
    wv = Wd["w_s5_glu"][l].rearrange("(k p) c -> p k c", p=128)
    kk = 0
    for fo in range(6):
        P.dma("pool", wgl[:], wv[:, :, fo * 128:(fo + 1) * 128], writes=[b_wgl])
        for (t0, n) in TBLK:
            pb = kk % 2
            kk += 1
            for k in range(6):
                mm(psum[pb][:, :n], wgl[:, k, :], yg[k][0][:, t0:t0 + n], k == 0, k == 5,
                   [b_wgl, yg[k][1]], pbuf[pb])
            sgt, b_sgt = (m1, b_m1) if pb == 0 else (vv, b_vv)
            ott, b_ott = (zz, b_zz) if pb == 0 else (tq, b_tq)
            act(lambda: A.activation(out=sgt[:, :n], in_=psum[pb][:, :n], func=AF.Sigmoid), [pbuf[pb]], [b_sgt])
            tt(ott[:, :n], yg[fo][0][:, t0:t0 + n].bitcast(F32), sgt[:, :n], ALU.mult, [yg[fo][1], b_sgt], [b_ott])
            P.dma("sp", E.os5T[fo * 128:(fo + 1) * 128, t0:t0 + n], ott[:, :n], reads=[b_ott], writes=[bd["os5T"]])


def phase_att(nc, P, ph, l, env):
    E = NS(env)
    V, A = nc.vector, nc.scalar
    sb, psum, pbuf, bd = E.sb, E.psum, E.pbuf, E.bd
    mm, dve, act = E.mm, E.dve, E.act
    L = f"T{l}"
    kT = sb(L + "kT", [128, T], F32R, scope=ph)
    b_kT = Buf("kT")
    vv = sb(L + "vv", [128, NT, 128], F32R, scope=ph)
    b_vv = Buf("vv")
    qb = [sb(L + f"qb{i}", [128, 512], F32R, scope=ph) for i in range(2)]
    b_qb = [Buf(f"qb{i}") for i in range(2)]
    pt = [sb(L + f"pt{i}", [128, 512], F32R, scope=ph) for i in range(3)]
    b_pt = [Buf(f"pt{i}") for i in range(3)]
    rden = sb(L + "rden", [128, 512], scope=ph)
    b_rden = Buf("rden")
    ob = [sb(L + f"ob{i}", [128, 512], scope=ph) for i in range(2)]
    b_ob = [Buf(f"ob{i}") for i in range(2)]
    k = 0
    for g in range(AKV_H):
        P.dma("pool", kT[:], E.kaT[g * 128:(g + 1) * 128, :], reads=[bd["kaT"]], writes=[b_kT])
        P.dma("pool", vv[:], E.va[:, g * 128:(g + 1) * 128].rearrange("(j p) d -> p j d", p=128),
              reads=[bd["va"]], writes=[b_vv])
        for hh in range(4):
            h = g * 4 + hh
            for (t0, n) in TBLK:
                key_tiles = [0, 1] if t0 == 0 else list(range(NT))
                s = k % 2
                po, pd = 4 + 2 * s, 5 + 2 * s
                k += 1
                P.dma("pool", qb[s][:, :n], E.qaT[h * 128:(h + 1) * 128, t0:t0 + n],
                      reads=[bd["qaT"]], writes=[b_qb[s]])
                for idx, j in enumerate(key_tiles):
                    ps = idx % 3
                    last = idx == len(key_tiles) - 1
                    mm(psum[ps][:, :n], kT[:, j * 128:(j + 1) * 128], qb[s][:, :n], True, True,
                       [b_kT, b_qb[s]], pbuf[ps])
                    act(lambda: A.activation(out=pt[ps][:, :n], in_=psum[ps][:, :n], func=AF.Exp,
                                             scale=HD ** -0.5), [pbuf[ps]], [b_pt[ps]])
                    mm(psum[po][:, :n], vv[:, j, :], pt[ps][:, :n], idx == 0, last,
                       [b_vv, b_pt[ps]], pbuf[po])
                    mm(psum[pd][:, :n], E.onesr[:], pt[ps][:, :n], idx == 0, last,
                       [E.b_onesr, b_pt[ps]], pbuf[pd])
                dve(lambda: V.reciprocal(out=rden[:, :n], in_=psum[pd][:, :n]), [pbuf[pd]], [b_rden])
                dve(lambda: V.tensor_tensor(out=ob[s][:, :n], in0=psum[po][:, :n], in1=rden[:, :n],
                                            op=ALU.mult), [pbuf[po], b_rden], [b_ob[s]])
                P.dma("sp", E.oattT[h * 128:(h + 1) * 128, t0:t0 + n], ob[s][:, :n],
                      reads=[b_ob[s]], writes=[bd["oattT"]])


def phase_C(nc, P, ph, l, env):
    E = NS(env)
    V, A = nc.vector, nc.scalar
    sb, psum, pbuf, Wd, bd = E.sb, E.psum, E.pbuf, E.Wd, E.bd
    mm, dve, act = E.mm, E.dve, E.act
    L = f"C{l}"
    R1 = sb(L + "R1", [128, KT, 512], F32R, scope=ph)
    R2 = sb(L + "R2", [128, KT, 512], F32, scope=ph)
    R3 = sb(L + "R3", [128, 11, 512], F32R, scope=ph)
    R4 = sb(L + "R4", [128, KT, 512], F32R, scope=ph)
    O2 = sb(L + "O2", [128, 11, 512], F32R, scope=ph)
    bO2 = [Buf(f"O2_{i}") for i in range(11)]
    bR1 = [Buf(f"R1_{i}") for i in range(KT)]
    bR2 = [Buf(f"R2_{i}") for i in range(KT)]
    bR3 = [Buf(f"R3_{i}") for i in range(11)]
    bR4 = [Buf(f"R4_{i}") for i in range(KT)]
    NW = 4
    wb = [sb(L + f"wb{i}", [128, 2048], F32R, scope=ph) for i in range(NW)]
    b_wb = [Buf(f"wb{i}") for i in range(NW)]
    SC = [sb(L + f"sc{i}", [128, 512], scope=ph) for i in range(7)]
    bSC = [Buf(f"sc{i}") for i in range(7)]
    xs, b_xs = SC[0:2], bSC[0:2]
    sg, b_sg = SC[0:3], bSC[0:3]
    tmp, b_tmp = SC[3:5], bSC[3:5]
    sqs, b_sqs = SC[0:2], bSC[0:2]
    mean, b_mean = SC[2], bSC[2]
    rstd, b_rstd = SC[5], bSC[5]
    msq, b_msq = SC[6], bSC[6]
    cnt = {"w": 0, "x": 0, "t": 0, "q": 0}
    w_in_v = Wd["w_in"][l].rearrange("(kt p) c -> p kt c", p=128)
    wpg_v = Wd["w_proj_gla"][l].rearrange("(k p) c -> p k c", p=128)
    wps_v = Wd["w_proj_s5"][l].rearrange("(k p) c -> p k c", p=128)
    wpa_v = Wd["w_proj_attn"][l].rearrange("(k p) c -> p k c", p=128)
    wo_v = Wd["w_out"][l].rearrange("(k p) c -> p k c", p=128)
    wfi_v = Wd["w_ffn_in"][l].rearrange("(k p) c -> p k c", p=128)

    def wload(src3, nk):
        s = cnt["w"] % NW
        cnt["w"] += 1
        view = wb[s][:, :nk * 128].rearrange("p (k c) -> p k c", c=128)
        P.dma("pool", view, src3, writes=[b_wb[s]])
        return view, b_wb[s]

    def otile(i, n):
        if i < 11:
            return R3[:, i, :n], bR3[i]
        return O2[:, i - 11, :n], bO2[i - 11]

    def layer_norm(n, wname, bname, post):
        for f in range(KT):
            s = f % 2
            act(lambda: A.activation(out=sqs[s][:, :n], in_=R2[:, f, :n], func=AF.Square),
                [bR2[f]], [b_sqs[s]])
            mm(psum[6][:, :n], E.onesf[:], R2[:, f, :n], f == 0, f == KT - 1, [E.b_onesf, bR2[f]], pbuf[6])
            mm(psum[7][:, :n], E.onesf[:], sqs[s][:, :n], f == 0, f == KT - 1, [E.b_onesf, b_sqs[s]], pbuf[7])
        act(lambda: A.mul(out=mean[:, :n], in_=psum[6][:, :n], mul=1.0 / D), [pbuf[6]], [b_mean])
        dve(lambda: V.tensor_tensor(out=msq[:, :n], in0=mean[:, :n], in1=mean[:, :n], op=ALU.mult),
            [b_mean], [b_msq])
        dve(lambda: V.scalar_tensor_tensor(out=msq[:, :n], in0=psum[7][:, :n], scalar=1.0 / D,
                                           in1=msq[:, :n], op0=ALU.mult, op1=ALU.subtract),
            [pbuf[7], b_msq], [b_msq])
        act(lambda: A.activation(out=rstd[:, :n], in_=msq[:, :n], func=AF.Sqrt, bias=E.epsc[:, 0:1],
                                 scale=1.0), [b_msq, E.b_epsc], [b_rstd])
        dve(lambda: V.reciprocal(out=rstd[:, :n], in_=rstd[:, :n]), [b_rstd], [b_rstd])
        for f in range(KT):
            s = cnt["t"] % 2
            cnt["t"] += 1
            dve(lambda: V.tensor_tensor(out=tmp[s][:, :n], in0=R2[:, f, :n], in1=mean[:, :n],
                                        op=ALU.subtract), [bR2[f], b_mean], [b_tmp[s]])
            dve(lambda: V.tensor_tensor(out=tmp[s][:, :n], in0=tmp[s][:, :n], in1=rstd[:, :n],
                                        op=ALU.mult), [b_tmp[s], b_rstd], [b_tmp[s]])
            act(lambda: A.activation(out=R2[:, f, :n], in_=tmp[s][:, :n], func=AF.Identity,
                                     scale=E.vcol(wname, l * 16 + f), bias=E.vcol(bname, l * 16 + f)),
                [b_tmp[s], E.b_vecT], [bR2[f]])
            post(f)

    for bi, (t0, n) in enumerate(TBLK):
        col = 1 if bi == 0 else 0
        for ft in range(KT):
            s = cnt["x"] % 2
            cnt["x"] += 1
            P.dma("sp", xs[s][:, :n], E.xT_v[:, ft, t0:t0 + n], reads=[E.b_xT[bi]], writes=[b_xs[s]])
            dve(lambda: V.tensor_scalar(out=R1[:, ft, :n], in0=xs[s][:, :n],
                                        scalar1=E.mcol(l, 1, ft, col), scalar2=E.mcol(l, 0, ft, col),
                                        op0=ALU.mult, op1=ALU.add),
                [b_xs[s], E.b_modT], [bR1[ft]])
        P.dma("pool", R3[:, 0:8, :n], E.fv(E.oglaT)[:, :, t0:t0 + n], reads=[bd["oglaT"]], writes=bR3[0:8])
        P.dma("pool", R3[:, 8:11, :n], E.fv(E.os5T)[:, 0:3, t0:t0 + n], reads=[bd["os5T"]], writes=bR3[8:11])
        P.dma("pool", O2[:, 0:3, :n], E.fv(E.os5T)[:, 3:6, t0:t0 + n],
              reads=[bd["os5T"]], writes=bO2[0:3])
        P.dma("pool", O2[:, 3:11, :n], E.fv(E.oattT)[:, :, t0:t0 + n],
              reads=[bd["oattT"]], writes=bO2[3:11])
        for f in range(KT):
            for br in range(3):
                wv_, bw = wload(w_in_v[:, :, OFF_BG + br * D + f * 128:OFF_BG + br * D + (f + 1) * 128], KT)
                for kt in range(KT):
                    mm(psum[br][:, :n], wv_[:, kt, :], R1[:, kt, :n], kt == 0, kt == KT - 1,
                       [bw, bR1[kt]], pbuf[br])
                act(lambda: A.activation(out=sg[br][:, :n], in_=psum[br][:, :n], func=AF.Sigmoid),
                    [pbuf[br]], [b_sg[br]])
            for br, (pv, nk, o0) in enumerate(((wpg_v, 8, 0), (wps_v, 6, 8), (wpa_v, 8, 14))):
                wv_, bw = wload(pv[:, :, f * 128:(f + 1) * 128], nk)
                for kk in range(nk):
                    oap, ob_ = otile(o0 + kk, n)
                    mm(psum[3 + br][:, :n], wv_[:, kk, :], oap, kk == 0, kk == nk - 1, [bw, ob_], pbuf[3 + br])
            dve(lambda: V.tensor_tensor(out=tmp[0][:, :n], in0=psum[3][:, :n], in1=sg[0][:, :n], op=ALU.mult),
                [pbuf[3], b_sg[0]], [b_tmp[0]])
            dve(lambda: V.tensor_tensor(out=tmp[1][:, :n], in0=psum[4][:, :n], in1=sg[1][:, :n], op=ALU.mult),
                [pbuf[4], b_sg[1]], [b_tmp[1]])
            dve(lambda: V.tensor_tensor(out=tmp[0][:, :n], in0=tmp[0][:, :n], in1=tmp[1][:, :n], op=ALU.add),
                [b_tmp[0], b_tmp[1]], [b_tmp[0]])
            dve(lambda: V.tensor_tensor(out=tmp[1][:, :n], in0=psum[5][:, :n], in1=sg[2][:, :n], op=ALU.mult),
                [pbuf[5], b_sg[2]], [b_tmp[1]])
            dve(lambda: V.tensor_tensor(out=R4[:, f, :n], in0=tmp[0][:, :n], in1=tmp[1][:, :n], op=ALU.add),
                [b_tmp[0], b_tmp[1]], [bR4[f]])
        for f in range(KT):
            wv_, bw = wload(wo_v[:, :, f * 128:(f + 1) * 128], KT)
            pb = f % 2
            for kt in range(KT):
                mm(psum[pb][:, :n], wv_[:, kt, :], R4[:, kt, :n], kt == 0, kt == KT - 1, [bw, bR4[kt]], pbuf[pb])
            s = cnt["x"] % 2
            cnt["x"] += 1
            P.dma("sp", xs[s][:, :n], E.xT_v[:, f, t0:t0 + n], reads=[E.b_xT[bi]], writes=[b_xs[s]])
            act(lambda: A.mul(out=R2[:, f, :n], in_=xs[s][:, :n], mul=DN_ALPHA), [b_xs[s]], [bR2[f]])
            dve(lambda: V.scalar_tensor_tensor(out=R2[:, f, :n], in0=psum[pb][:, :n],
                                               scalar=E.mcol(l, 2, f, col), in1=R2[:, f, :n],
                                               op0=ALU.mult, op1=ALU.add),
                [pbuf[pb], bR2[f], E.b_modT], [bR2[f]])

        def post1(f):
            act(lambda: A.activation(out=R1[:, f, :n], in_=R2[:, f, :n], func=AF.Identity,
                                     scale=E.mcol(l, 4, f, col), bias=E.mcol(l, 3, f, col)),
                [bR2[f], E.b_modT], [bR1[f]])
        layer_norm(n, "ln1_w", "ln1_b", post1)
        for q in range(4):
            for j in range(11):
                jj = q * 11 + j
                pa, pbk = (0, 1) if j % 2 == 0 else (2, 3)
                wa, bwa = wload(wfi_v[:, :, jj * 128:(jj + 1) * 128], KT)
                for kt in range(KT):
                    mm(psum[pa][:, :n], wa[:, kt, :], R1[:, kt, :n], kt == 0, kt == KT - 1, [bwa, bR1[kt]], pbuf[pa])
                wb_, bwb = wload(wfi_v[:, :, DFF + jj * 128:DFF + (jj + 1) * 128], KT)
                for kt in range(KT):
                    mm(psum[pbk][:, :n], wb_[:, kt, :], R1[:, kt, :n], kt == 0, kt == KT - 1, [bwb, bR1[kt]], pbuf[pbk])
                s = j % 2
                act(lambda: A.activation(out=sg[s][:, :n], in_=psum[pa][:, :n], func=AF.Silu),
                    [pbuf[pa]], [b_sg[s]])
                dve(lambda: V.tensor_tensor(out=R3[:, j, :n], in0=psum[pbk][:, :n], in1=sg[s][:, :n], op=ALU.mult),
                    [pbuf[pbk], b_sg[s]], [bR3[j]])
            wfo_v = Wd["w_ffn_out"][l][q * 1408:(q + 1) * 1408, :].rearrange("(k p) c -> p k c", p=128)
            for f in range(KT):
                wv_, bw = wload(wfo_v[:, :, f * 128:(f + 1) * 128], 11)
                pb = 4 + f % 2
                for j in range(11):
                    mm(psum[pb][:, :n], wv_[:, j, :], R3[:, j, :n], j == 0, j == 10, [bw, bR3[j]], pbuf[pb])
                if q == 0:
                    dve(lambda: V.tensor_copy(out=R4[:, f, :n], in_=psum[pb][:, :n]),
                        [pbuf[pb]], [bR4[f]])
                else:
                    dve(lambda: V.tensor_tensor(out=R4[:, f, :n], in0=psum[pb][:, :n],
                                                in1=R4[:, f, :n].bitcast(F32), op=ALU.add),
                        [pbuf[pb], bR4[f]], [bR4[f]])
        for f in range(KT):
            act(lambda: A.mul(out=R2[:, f, :n], in_=R2[:, f, :n], mul=DN_ALPHA), [bR2[f]], [bR2[f]])
            dve(lambda: V.scalar_tensor_tensor(out=R2[:, f, :n], in0=R4[:, f, :n].bitcast(F32),
                                               scalar=E.mcol(l, 5, f, col), in1=R2[:, f, :n],
                                               op0=ALU.mult, op1=ALU.add),
                [bR4[f], bR2[f], E.b_modT], [bR2[f]])

        def post2(f):
            P.dma("sp", E.xT_v[:, f, t0:t0 + n], R2[:, f, :n], reads=[bR2[f]], writes=[E.b_xT[bi]])
        layer_norm(n, "ln2_w", "ln2_b", post2)


_CONSTS = None


def _consts():
    global _CONSTS
    if _CONSTS is None:
        c = {}
        c["ident"] = np.eye(128, dtype=np.float32)
        c["ones128"] = np.ones((128, 128), np.float32)
        pr = np.zeros((128, 128), np.float32)
        for i in range(64):
            pr[2 * i + 1, 2 * i] = -1.0
            pr[2 * i, 2 * i + 1] = 1.0
        c["prot"] = pr
        t = np.arange(T)
        c["cmask"] = np.broadcast_to((t % CH != 0).astype(np.float32), (128, T)).copy()
        jj, ii = np.meshgrid(np.arange(64), np.arange(64), indexing="ij")
        c["trif"] = (ii >= jj).astype(np.float32)
        c["trib"] = (ii <= jj).astype(np.float32)
        r = np.arange(128)
        c["rmask"] = (r[:, None] // 16 == np.arange(8)[None, :]).astype(np.float32)
        cm = (np.arange(128)[None, :] // 16 == np.arange(8)[:, None]).astype(np.float32)
        c["colmask"] = np.broadcast_to(cm.reshape(1, 8 * 128), (128, 8 * 128)).copy()
        c["iota512"] = np.broadcast_to(np.arange(512, dtype=np.float32), (128, 512)).copy()
        tl = np.arange(NLAT)
        pos = np.zeros((128, NLAT), np.float32)
        pos[:64, :] = (tl // 64)[None, :]
        pos[64:, :] = (tl % 64)[None, :]
        c["ropepos"] = pos
        d = np.arange(128)
        c["fidx"] = ((d % 64) // 2).astype(np.float32).reshape(128, 1)
        _CONSTS = c
    return _CONSTS


def make_in_maps(inputs, n_cores=8):
    c = _consts()
    f = lambda a: np.ascontiguousarray(a, dtype=np.float32)
    x = f(inputs["x"])
    ctx = f(inputs["ctx"])
    cvec = f(inputs["c"])
    c_ctx = f(inputs["c_ctx"])
    shared = {n: f(inputs[n]) for n, _ in WEIGHT_NAMES}
    shared.update(c)
    in_maps = []
    for core in range(n_cores):
        b = core % 4
        m = {"x": x[b], "ctx": ctx[b], "cc": np.stack([cvec[b], c_ctx], 0)}
        m.update(shared)
        in_maps.append(m)
    return in_maps


def kernel(**inputs):
    n_cores = 8
    nc = build_program()
    in_maps = make_in_maps(inputs, n_cores)
    res = run_bass_kernel_spmd(nc, in_maps, core_ids=list(range(n_cores)))
    out = np.stack([res.results[b]["out"] for b in range(4)], axis=0)
    return out.astype(np.float32)
```

```python
import contextlib
import math

import numpy as np
import concourse.bass as bass
import concourse.mybir as mybir
from concourse.bass_utils import run_bass_kernel_spmd

F32 = mybir.dt.float32
F32R = mybir.dt.float32r
I32 = mybir.dt.int32
AF = mybir.ActivationFunctionType
ALU = mybir.AluOpType

D = 2048
NLAT = 2048
NCTX = 256
T = NCTX + NLAT
DEPTH = 4
GLA_H, GLA_DK, GLA_DV = 4, 128, 256
GLA_TAU = 16.0
CH = 64
NCH = T // CH
S5_W, S5_NG, S5_P = 768, 48, 64
AQ_H, AKV_H, HD = 8, 2, 128
DFF = 5632
IN_W = 11536
OFF_GQ, OFF_GK, OFF_GV, OFF_GR, OFF_GLR, OFF_SU, OFF_AQ, OFF_AK, OFF_AV, OFF_BG = (
    0, 512, 1024, 2048, 3072, 3088, 3856, 4880, 5136, 5392)
DN_ALPHA = (2 * DEPTH) ** 0.25
EPS = 1e-6
KT = D // 128
NT = T // 128
TWO_PI = 2.0 * math.pi
SIN_SCALE = TWO_PI * 0.999999
TBLK = [(0, 256), (256, 512), (768, 512), (1280, 512), (1792, 512)]

EPOCH_LIM = 30000


def _tkey(tok):
    return (tok[0], tok[1]) if tok[0] == "c" else (tok[0], tok[1], tok[2])


def _tval(tok):
    return (tok[2], tok[3]) if tok[0] == "c" else tok[3]


class Buf:
    __slots__ = ("name", "lw", "rd", "multi")

    def __init__(self, name, multi=False):
        self.name = name
        self.lw = {}
        self.rd = {}
        self.multi = multi


def _merge(dct, tok):
    k = _tkey(tok)
    cur = dct.get(k)
    if cur is None or _tval(cur) < _tval(tok):
        dct[k] = tok


class Prog:
    def __init__(self, nc, es):
        self.nc = nc
        self.es = es
        self.eng = {"pe": nc.tensor, "act": nc.scalar, "dve": nc.vector,
                    "pool": nc.gpsimd, "sp": nc.sync}
        self.csem = {}
        self.cnt = {}
        self.epoch = {}
        for e in ("pe", "act", "dve", "pool"):
            self.csem[e] = [es.enter_context(nc.semaphore(f"c_{e}_0"))]
            self.cnt[e] = 0
            self.epoch[e] = 0
        self.seen = {e: {} for e in self.eng}
        self.dsem = {}
        self.dval = {}
        self.dlast = {}
        self.drr = {}
        for q, n in (("sp", 24), ("pool", 24)):
            self.dsem[q] = [es.enter_context(nc.semaphore(f"d_{q}_{i}")) for i in range(n)]
            self.dval[q] = [0] * n
            self.dlast[q] = [None] * n
            self.drr[q] = 0
        self.n_inst = 0

    def _wait(self, e, tok):
        if tok is None:
            return
        seen = self.seen[e]
        k = _tkey(tok)
        v = _tval(tok)
        if tok[0] == "c":
            if tok[1] == e and e == "pe":
                return
            if tok[1] == e and e in ("dve", "act") and (
                    tok[2] < self.epoch[e] or self.cnt[e] - tok[3] >= 2):
                return
            cur = seen.get(k)
            if cur is not None and cur >= v:
                return
            self.eng[e].wait_ge(self.csem[tok[1]][tok[2]], tok[3])
        else:
            if seen.get(k, 0) >= v:
                return
            self.eng[e].wait_ge(self.dsem[tok[1]][tok[2]], tok[3])
        seen[k] = v
        self.n_inst += 1

    def _deps(self, reads, writes):
        deps = []
        for b in reads:
            deps.extend(b.lw.values())
        for b in writes:
            if not b.multi:
                deps.extend(b.lw.values())
            deps.extend(b.rd.values())
        return deps

    def _commit(self, tok, reads, writes):
        for b in reads:
            _merge(b.rd, tok)
        for b in writes:
            if b.multi:
                _merge(b.lw, tok)
            else:
                b.lw = {_tkey(tok): tok}
                b.rd = {}

    def op(self, e, fn, reads=(), writes=()):
        for d in self._deps(reads, writes):
            self._wait(e, d)
        if self.cnt[e] >= EPOCH_LIM:
            self.epoch[e] += 1
            self.cnt[e] = 0
            self.csem[e].append(self.es.enter_context(
                self.nc.semaphore(f"c_{e}_{self.epoch[e]}")))
        ins = fn()
        self.cnt[e] += 1
        ins.then_inc(self.csem[e][self.epoch[e]], 1)
        tok = ("c", e, self.epoch[e], self.cnt[e])
        self._commit(tok, reads, writes)
        self.n_inst += 1
        return tok

    def dma(self, q, out, in_, reads=(), writes=(), **kw):
        e = q
        for d in self._deps(reads, writes):
            self._wait(e, d)
        si = self.drr[q]
        self.drr[q] = (si + 1) % len(self.dsem[q])
        self._wait(e, self.dlast[q][si])
        ins = self.eng[e].dma_start(out=out, in_=in_, **kw)
        self.dval[q][si] += 16
        ins.then_inc(self.dsem[q][si], 16)
        tok = ("d", q, si, self.dval[q][si])
        self.dlast[q][si] = tok
        self._commit(tok, reads, writes)
        self.n_inst += 1
        return tok

    def barrier(self):
        toks = []
        for e in ("pe", "act", "dve", "pool"):
            if self.cnt[e] > 0 or self.epoch[e] > 0:
                toks.append(("c", e, self.epoch[e], self.cnt[e]))
        for q in self.dsem:
            for t in self.dlast[q]:
                if t is not None:
                    toks.append(t)
        for e in self.eng:
            for t in toks:
                if t[0] == "c" and t[1] == e:
                    continue
                self._wait(e, t)


WEIGHT_NAMES = [
    ("w_ada", [DEPTH, D, 6 * D]), ("b_ada", [DEPTH, 6 * D]), ("w_in", [DEPTH, D, IN_W]),
    ("w_gla_gate", [DEPTH, 2, 16, 512]), ("b_gla_gate", [DEPTH, 2, 512]),
    ("gla_norm_w", [DEPTH, 1024]),
    ("s5_lam_re", [DEPTH, 2, 48, 64]), ("s5_lam_im", [DEPTH, 2, 48, 64]),
    ("s5_log_dt", [DEPTH, 2, 48]),
    ("s5_b_re", [DEPTH, 48, 64, 16]), ("s5_b_im", [DEPTH, 48, 64, 16]),
    ("s5_c_re", [DEPTH, 2, 48, 16, 64]), ("s5_c_im", [DEPTH, 2, 48, 16, 64]),
    ("s5_d", [DEPTH, 768]), ("w_s5_glu", [DEPTH, 768, 768]),
    ("q_norm_w", [DEPTH, 128]), ("k_norm_w", [DEPTH, 128]),
    ("w_proj_gla", [DEPTH, 1024, D]), ("w_proj_s5", [DEPTH, 768, D]),
    ("w_proj_attn", [DEPTH, 1024, D]), ("w_out", [DEPTH, D, D]),
    ("ln1_w", [DEPTH, D]), ("ln1_b", [DEPTH, D]), ("ln2_w", [DEPTH, D]), ("ln2_b", [DEPTH, D]),
    ("w_ffn_in", [DEPTH, D, 2 * DFF]), ("w_ffn_out", [DEPTH, DFF, D]),
]

CONST_SHAPES = {
    "ident": [128, 128], "ones128": [128, 128], "prot": [128, 128],
    "cmask": [128, T], "trif": [64, 64], "trib": [64, 64],
    "rmask": [128, 8], "colmask": [128, 8 * 128], "iota512": [128, 512],
    "ropepos": [128, NLAT], "fidx": [128, 1],
}

STAGES = ["init", "ada", "A", "gla", "s5", "att", "C"]


def build_program(debug=False, upto="all", nlayers=DEPTH):
    nc = bass.Bass("TRN2", target_bir_lowering=False)
    stage_on = {s: True for s in STAGES}
    if upto != "all":
        cut = STAGES.index(upto)
        for i, s in enumerate(STAGES):
            stage_on[s] = i <= cut

    def din(name, shape, dt=F32):
        return nc.dram_tensor(name, list(shape), dt, kind="ExternalInput").ap()

    skind = "ExternalOutput" if debug else "Internal"

    def dscr(name, shape, dt=F32):
        return nc.dram_tensor(name, list(shape), dt, kind=skind).ap()

    x_in = din("x", [NLAT, D])
    ctx_in = din("ctx", [NCTX, D])
    cc_in = din("cc", [2, D])
    Wd = {n: din(n, s) for n, s in WEIGHT_NAMES}
    Cd = {n: din(n, s) for n, s in CONST_SHAPES.items()}
    out_d = nc.dram_tensor("out", [NLAT, D], F32, kind="ExternalOutput").ap()

    xT = dscr("xT", [D, T])
    rope_d = dscr("rope_d", [2, 128, T])
    qgT = dscr("qgT", [512, T])
    kgT = dscr("kgT", [512, T])
    vg = dscr("vg", [T, 1024])
    grt = dscr("grt", [T, 1024])
    glrT = dscr("glrT", [16, T])
    suT = dscr("suT", [768, T])
    qaT = dscr("qaT", [1024, T])
    kaT = dscr("kaT", [256, T])
    va = dscr("va", [T, 256])
    oglaT = dscr("oglaT", [1024, T])
    os5T = dscr("os5T", [768, T])
    oattT = dscr("oattT", [1024, T])
    if debug:
        dbg_mod = dscr("dbg_mod", [128, DEPTH * 96 * 2])
        dbg_cs = dscr("dbg_cs", [128, T])
        dbg_y2 = dscr("dbg_y2", [128, T])
        dbg_oraw = dscr("dbg_oraw", [64, NCH * 256])
    xT_v = xT.rearrange("(ft p) t -> p ft t", p=128)

    def fv(ap_2d):
        return ap_2d.rearrange("(k p) t -> p k t", p=128)

    with contextlib.ExitStack() as es:
        P = Prog(nc, es)

        def sb(name, shape, dt=F32, scope=None):
            return (scope or es).enter_context(nc.sbuf_tensor(name, list(shape), dt))

        psum = [es.enter_context(nc.psum_tensor(f"ps{i}", [128, 512], F32)) for i in range(8)]
        pbuf = [Buf(f"ps{i}") for i in range(8)]

        b_xT = [Buf(f"xT{i}") for i in range(len(TBLK))]
        bd = {n: Buf(n, multi=True) for n in
              ["rope", "qgT", "kgT", "vg", "grt", "glrT", "suT", "qaT", "kaT", "va",
               "oglaT", "os5T", "oattT"]}

        def xT_bufs(t0, n):
            return [b_xT[i] for i, (s, m) in enumerate(TBLK) if s < t0 + n and t0 < s + m]

        V = nc.vector
        A = nc.scalar

        def mm(out, lhsT, rhs, start, stop, reads, pb):
            return P.op("pe", lambda: nc.tensor.matmul(out, lhsT=lhsT, rhs=rhs, start=start, stop=stop),
                        reads=reads, writes=[pb])

        def tr(out, in_, reads, pb):
            kk_ = in_.shape[0]
            return P.op("pe", lambda: nc.tensor.transpose(out, in_, ident[:kk_, :kk_]),
                        reads=list(reads) + [b_ident], writes=[pb])

        def dve(fn, reads, writes):
            return P.op("dve", fn, reads=reads, writes=writes)

        def act(fn, reads, writes):
            return P.op("act", fn, reads=reads, writes=writes)

        ident = sb("ident_sb", [128, 128])
        b_ident = Buf("ident")
        P.dma("sp", ident[:], Cd["ident"][:, :], writes=[b_ident])
        onesf = sb("onesf", [128, 128])
        b_onesf = Buf("onesf")
        P.dma("sp", onesf[:], Cd["ones128"][:, :], writes=[b_onesf])
        onesr = sb("onesr", [128, 128], F32R)
        b_onesr = Buf("onesr")
        P.dma("pool", onesr[:], Cd["ones128"][:, :], writes=[b_onesr])
        prot = sb("prot_sb", [128, 128], F32R)
        b_prot = Buf("prot")
        P.dma("pool", prot[:], Cd["prot"][:, :], writes=[b_prot])
        epsc = sb("epsc", [128, 1])
        b_epsc = Buf("epsc")
        dve(lambda: V.memset(epsc[:], EPS), [], [b_epsc])

        VC = {}
        ncol = 0
        for nm, n in (("ln1_w", DEPTH * 16), ("ln1_b", DEPTH * 16), ("ln2_w", DEPTH * 16),
                      ("ln2_b", DEPTH * 16), ("b_ada", DEPTH * 96), ("q_norm_w", DEPTH),
                      ("k_norm_w", DEPTH), ("b_gla_gate", DEPTH * 8), ("s5_d", DEPTH * 6)):
            VC[nm] = ncol
            ncol += n
        vecT = sb("vecT", [128, ncol])
        b_vecT = Buf("vecT")
        modT = sb("modT", [128, DEPTH, 96, 2])
        b_modT = Buf("modT")

        def vcol(nm, idx):
            c = VC[nm] + idx
            return vecT[:, c:c + 1]

        with contextlib.ExitStack() as ph:
            stg = [sb(f"pstg{i}", [128, 128], scope=ph) for i in range(2)]
            b_stg = [Buf(f"pstg{i}") for i in range(2)]
            k = 0

            def load_T(src_rows, R, c0):
                nonlocal k
                s = k % 2
                k += 1
                P.dma("sp", stg[s][:R, :], src_rows, writes=[b_stg[s]])
                tr(psum[s][:, :R], stg[s][:R, :], [b_stg[s]], pbuf[s])
                dve(lambda: V.tensor_copy(out=vecT[:, c0:c0 + R], in_=psum[s][:, :R]),
                    [pbuf[s]], [b_vecT])

            for nm in ("ln1_w", "ln1_b", "ln2_w", "ln2_b"):
                load_T(Wd[nm].rearrange("l (k p) -> (l k) p", p=128), DEPTH * 16, VC[nm])
            bav = Wd["b_ada"].rearrange("l (k p) -> (l k) p", p=128)
            for i in range(3):
                load_T(bav[i * 128:(i + 1) * 128, :], 128, VC["b_ada"] + i * 128)
            load_T(Wd["q_norm_w"], DEPTH, VC["q_norm_w"])
            load_T(Wd["k_norm_w"], DEPTH, VC["k_norm_w"])
            load_T(Wd["b_gla_gate"].rearrange("l d (k p) -> (l d k) p", p=128), DEPTH * 8,
                   VC["b_gla_gate"])
            load_T(Wd["s5_d"].rearrange("l (k p) -> (l k) p", p=128), DEPTH * 6, VC["s5_d"])
            c0 = VC["b_gla_gate"]
            dve(lambda: V.tensor_scalar(out=vecT[:, c0:c0 + DEPTH * 8], in0=vecT[:, c0:c0 + DEPTH * 8],
                                        scalar1=-1.0, scalar2=None, op0=ALU.mult, op1=ALU.bypass),
                [b_vecT], [b_vecT])

            fidx = sb("fidx_sb", [128, 1], scope=ph)
            b_fidx = Buf("fidx")
            P.dma("sp", fidx[:], Cd["fidx"][:, :], writes=[b_fidx])
            invr = sb("invr", [128, 1], scope=ph)
            b_invr = Buf("invr")
            act(lambda: A.activation(out=invr[:], in_=fidx[:], func=AF.Exp,
                                     scale=-math.log(10000.0) / 32.0), [b_fidx], [b_invr])
            dve(lambda: V.tensor_scalar(out=invr[:], in0=invr[:], scalar1=1.0 / TWO_PI, scalar2=None,
                                        op0=ALU.mult, op1=ALU.bypass), [b_invr], [b_invr])
            rp = sb("rp", [128, 512], scope=ph)
            rr = sb("rr", [128, 512], scope=ph)
            ri = sb("ri", [128, 512], I32, scope=ph)
            rf = sb("rf", [128, 512], scope=ph)
            ro = [sb(f"ro{i}", [128, 512], scope=ph) for i in range(2)]
            b_rp, b_rr, b_ri, b_rf = Buf("rp"), Buf("rr"), Buf("ri"), Buf("rf")
            b_ro = [Buf("ro0"), Buf("ro1")]

            def frac_wrap(r_t, b_r, n):
                dve(lambda: V.tensor_copy(out=ri[:, :n], in_=r_t[:, :n]), [b_r], [b_ri])
                dve(lambda: V.tensor_copy(out=rf[:, :n], in_=ri[:, :n]), [b_ri], [b_rf])
                dve(lambda: V.tensor_tensor(out=r_t[:, :n], in0=r_t[:, :n], in1=rf[:, :n],
                                            op=ALU.subtract), [b_r, b_rf], [b_r])
                wrap_half(r_t, b_r, n)

            def wrap_half(r_t, b_r, n):
                dve(lambda: V.tensor_single_scalar(out=rf[:, :n], in_=r_t[:, :n], scalar=0.5,
                                                   op=ALU.is_gt), [b_r], [b_rf])
                dve(lambda: V.tensor_tensor(out=r_t[:, :n], in0=r_t[:, :n], in1=rf[:, :n],
                                            op=ALU.subtract), [b_r, b_rf], [b_r])
                dve(lambda: V.tensor_single_scalar(out=rf[:, :n], in_=r_t[:, :n], scalar=-0.5,
                                                   op=ALU.is_lt), [b_r], [b_rf])
                dve(lambda: V.tensor_tensor(out=r_t[:, :n], in0=r_t[:, :n], in1=rf[:, :n],
                                            op=ALU.add), [b_r, b_rf], [b_r])

            if stage_on["A"]:
                dve(lambda: V.memset(ro[0][:, :NCTX], 1.0), [], [b_ro[0]])
                dve(lambda: V.memset(ro[1][:, :NCTX], 0.0), [], [b_ro[1]])
                P.dma("sp", rope_d[0, :, 0:NCTX], ro[0][:, :NCTX], reads=[b_ro[0]], writes=[bd["rope"]])
                P.dma("sp", rope_d[1, :, 0:NCTX], ro[1][:, :NCTX], reads=[b_ro[1]], writes=[bd["rope"]])
                for cch in range(4):
                    P.dma("sp", rp[:], Cd["ropepos"][:, cch * 512:(cch + 1) * 512], writes=[b_rp])
                    dve(lambda: V.tensor_scalar(out=rr[:], in0=rp[:], scalar1=invr[:, 0:1], scalar2=None,
                                                op0=ALU.mult, op1=ALU.bypass), [b_rp, b_invr], [b_rr])
                    frac_wrap(rr, b_rr, 512)
                    act(lambda: A.activation(out=ro[1][:], in_=rr[:], func=AF.Sin, scale=SIN_SCALE),
                        [b_rr], [b_ro[1]])
                    P.dma("sp", rope_d[1, :, NCTX + cch * 512:NCTX + (cch + 1) * 512], ro[1][:],
                          reads=[b_ro[1]], writes=[bd["rope"]])
                    dve(lambda: V.tensor_scalar(out=rr[:], in0=rr[:], scalar1=0.25, scalar2=None,
                                                op0=ALU.add, op1=ALU.bypass), [b_rr], [b_rr])
                    wrap_half(rr, b_rr, 512)
                    act(lambda: A.activation(out=ro[0][:], in_=rr[:], func=AF.Sin, scale=SIN_SCALE),
                        [b_rr], [b_ro[0]])
                    P.dma("sp", rope_d[0, :, NCTX + cch * 512:NCTX + (cch + 1) * 512], ro[0][:],
                          reads=[b_ro[0]], writes=[bd["rope"]])

            if stage_on["ada"]:
                cst = sb("cst", [32, 128], scope=ph)
                b_cst = Buf("cst")
                siluT = sb("siluT", [128, 16, 2], scope=ph)
                b_siluT = Buf("siluT")
                P.dma("sp", cst[:], cc_in.rearrange("j (k p) -> (j k) p", p=128), writes=[b_cst])
                tr(psum[2][:, :32], cst[:, :], [b_cst], pbuf[2])
                act(lambda: A.activation(out=siluT[:].rearrange("p k j -> p j k"),
                                         in_=psum[2][:, :32].rearrange("p (j k) -> p j k", j=2),
                                         func=AF.Silu), [pbuf[2]], [b_siluT])
                siluR = sb("siluR", [128, 16, 2], F32R, scope=ph)
                b_siluR = Buf("siluR")
                dve(lambda: V.tensor_copy(out=siluR[:], in_=siluT[:]), [b_siluT], [b_siluR])
                wad = [sb(f"wad{i}", [128, 16, 512], F32R, scope=ph) for i in range(2)]
                b_wad = [Buf(f"wad{i}") for i in range(2)]
                brow = sb("brow", [2, 6 * D], scope=ph)
                b_brow = Buf("brow")
                mrow = [sb(f"mrow{i}", [2, 512], scope=ph) for i in range(2)]
                b_mrow = [Buf(f"mrow{i}") for i in range(2)]
                kk = 0
                for l in range(nlayers):
                    wv = Wd["w_ada"][l].rearrange("(kt p) c -> p kt c", p=128)
                    P.dma("sp", brow[:], Wd["b_ada"][l:l + 1, :].to_broadcast([2, 6 * D]), writes=[b_brow])
                    for cb in range(24):
                        s = kk % 2
                        pb = 4 + kk % 2
                        kk += 1
                        P.dma("pool", wad[s][:], wv[:, :, cb * 512:(cb + 1) * 512], writes=[b_wad[s]])
                        for kt in range(KT):
                            mm(psum[pb][:2, :], siluR[:, kt, :], wad[s][:, kt, :], kt == 0, kt == KT - 1,
                               [b_wad[s], b_siluR], pbuf[pb])
                        dve(lambda: V.tensor_tensor(out=mrow[s][:], in0=psum[pb][:2, :],
                                                    in1=brow[:, cb * 512:(cb + 1) * 512], op=ALU.add),
                            [pbuf[pb], b_brow], [b_mrow[s]])
                        pt_ = 6 + kk % 2
                        for j in range(4):
                            tr(psum[pt_][:, 2 * j:2 * j + 2], mrow[s][:, j * 128:(j + 1) * 128], [b_mrow[s]], pbuf[pt_])
                        dve(lambda: V.tensor_copy(out=modT[:, l, cb * 4:(cb + 1) * 4, :].rearrange("p f j -> p (f j)"),
                                                  in_=psum[pt_][:, :8]), [pbuf[pt_]], [b_modT])
                    for c0_ in (16, 64):
                        dve(lambda: V.tensor_scalar(out=modT[:, l, c0_:c0_ + 16, :], in0=modT[:, l, c0_:c0_ + 16, :],
                                                    scalar1=1.0, scalar2=None, op0=ALU.add, op1=ALU.bypass),
                            [b_modT], [b_modT])
                if debug:
                    P.dma("sp", dbg_mod, modT[:].rearrange("p l f j -> p (l f j)"), reads=[b_modT], writes=[])
        P.barrier()

        def mcol(l, chunk, f, col):
            return modT[:, l, chunk * 16 + f, col:col + 1]

        with contextlib.ExitStack() as ph:
            xin = [sb(f"xin{i}", [128, D], scope=ph) for i in range(2)]
            b_xin = [Buf(f"xin{i}") for i in range(2)]
            stg = [sb(f"stg{i}", [128, 4, 128], scope=ph) for i in range(2)]
            b_stg = [Buf(f"stg{i}") for i in range(2)]
            k = 0
            for ti in range(NT):
                src = ctx_in[ti * 128:(ti + 1) * 128, :] if ti < 2 else \
                    x_in[(ti - 2) * 128:(ti - 1) * 128, :]
                s = ti % 2
                P.dma("sp", xin[s][:], src, writes=[b_xin[s]])
                for g4 in range(4):
                    pb = k % 2
                    for j in range(4):
                        ft = g4 * 4 + j
                        tr(psum[pb][:, j * 128:(j + 1) * 128], xin[s][:, ft * 128:(ft + 1) * 128],
                           [b_xin[s]], pbuf[pb])
                    ss = k % 2
                    dve(lambda: V.tensor_copy(out=stg[ss][:].rearrange("p a b -> p (a b)"), in_=psum[pb][:]),
                        [pbuf[pb]], [b_stg[ss]])
                    P.dma("sp", xT_v[:, g4 * 4:(g4 + 1) * 4, ti * 128:(ti + 1) * 128], stg[ss][:],
                          reads=[b_stg[ss]], writes=xT_bufs(ti * 128, 128))
                    k += 1
        P.barrier()

        for l in range(nlayers):
            if stage_on["A"]:
                with contextlib.ExitStack() as ph:
                    phase_A(nc, P, ph, l, locals())
                P.barrier()
            if stage_on["gla"]:
                with contextlib.ExitStack() as ph:
                    phase_gla(nc, P, ph, l, locals())
                P.barrier()
            if stage_on["s5"]:
                with contextlib.ExitStack() as ph:
                    phase_s5(nc, P, ph, l, locals())
                P.barrier()
            if stage_on["att"]:
                with contextlib.ExitStack() as ph:
                    phase_att(nc, P, ph, l, locals())
                P.barrier()
            if stage_on["C"]:
                with contextlib.ExitStack() as ph:
                    phase_C(nc, P, ph, l, locals())
                P.barrier()

        with contextlib.ExitStack() as ph:
            xf = [sb(f"xf{i}", [128, KT, 128], scope=ph) for i in range(2)]
            b_xf = [Buf(f"xf{i}") for i in range(2)]
            ost = [sb(f"ost{i}", [128, D], scope=ph) for i in range(2)]
            b_ost = [Buf(f"ost{i}") for i in range(2)]
            k = 0
            toks = []
            for ti in range(2, NT):
                s = ti % 2
                P.dma("sp", xf[s][:], xT_v[:, :, ti * 128:(ti + 1) * 128],
                      reads=xT_bufs(ti * 128, 128), writes=[b_xf[s]])
                for g4 in range(4):
                    pb = k % 2
                    for j in range(4):
                        ft = g4 * 4 + j
                        tr(psum[pb][:, j * 128:(j + 1) * 128], xf[s][:, ft, :], [b_xf[s]], pbuf[pb])
                    dve(lambda: V.tensor_copy(out=ost[s][:, g4 * 512:(g4 + 1) * 512], in_=psum[pb][:]),
                        [pbuf[pb]], [b_ost[s]])
                    k += 1
                toks.append(P.dma("sp", out_d[(ti - 2) * 128:(ti - 1) * 128, :], ost[s][:],
                                  reads=[b_ost[s]], writes=[]))
            for t in toks:
                P._wait("sp", t)
        print("instructions:", P.n_inst)
    return nc


class NS:
    def __init__(self, d):
        self.__dict__.update(d)


def phase_A(nc, P, ph, l, env):
    E = NS(env)
    V, A = nc.vector, nc.scalar
    sb, psum, pbuf, Wd, bd = E.sb, E.psum, E.pbuf, E.Wd, E.bd
    mm, tr, dve, act = E.mm, E.tr, E.dve, E.act
    L = f"A{l}"
    h1 = sb(L + "h1", [128, KT, 512], F32R, scope=ph)
    b_h1 = [Buf(f"h1_{i}") for i in range(KT)]
    xs = [sb(L + f"xs{i}", [128, 512], scope=ph) for i in range(3)]
    b_xs = [Buf(f"xs{i}") for i in range(3)]
    NW = 3
    wb = [sb(L + f"wb{i}", [128, 4096], F32R, scope=ph) for i in range(NW)]
    b_wb = [Buf(f"wb{i}") for i in range(NW)]
    wglr = sb(L + "wglr", [128, KT, 16], scope=ph)
    b_wglr = Buf("wglr")
    ost = [sb(L + f"ost{i}", [128, 512], scope=ph) for i in range(4)]
    b_ost = [Buf(f"ost{i}") for i in range(4)]
    cosT = sb(L + "cosT", [128, T], scope=ph)
    sinT = sb(L + "sinT", [128, T], scope=ph)
    b_rope = Buf("ropesb")
    sq = sb(L + "sq", [128, 512], F32R, scope=ph)
    b_sq = Buf("sq")
    rs = sb(L + "rs", [128, 512], scope=ph)
    b_rs = Buf("rs")
    qn = sb(L + "qn", [128, 512], F32R, scope=ph)
    b_qn = Buf("qn")
    t1 = sb(L + "t1", [128, 512], scope=ph)
    b_t1 = Buf("t1")
    t2 = sb(L + "t2", [128, 512], scope=ph)
    b_t2 = Buf("t2")
    cnt = {"w": 0, "o": 0, "x": 0, "p": 0}

    P.dma("sp", cosT[:], E.rope_d[0], reads=[bd["rope"]], writes=[b_rope])
    P.dma("sp", sinT[:], E.rope_d[1], reads=[bd["rope"]], writes=[b_rope])
    P.dma("sp", wglr[:], Wd["w_in"][l].rearrange("(kt p) c -> p kt c", p=128)[:, :, OFF_GLR:OFF_GLR + 16],
          writes=[b_wglr])
    wv = Wd["w_in"][l].rearrange("(kt p) c -> p kt c", p=128)

    def wload(c0, ncols):
        s = cnt["w"] % NW
        cnt["w"] += 1
        view = wb[s][:, :KT * ncols].rearrange("p (k c) -> p k c", c=ncols)
        P.dma("pool", view, wv[:, :, c0:c0 + ncols], writes=[b_wb[s]])
        return view, b_wb[s]

    def nps():
        p = cnt["p"] % 4
        cnt["p"] += 1
        return p

    def nost():
        s = cnt["o"] % 4
        cnt["o"] += 1
        return s

    for bi, (t0, n) in enumerate(TBLK):
        col = 1 if bi == 0 else 0
        for ft in range(KT):
            s = cnt["x"] % 3
            cnt["x"] += 1
            P.dma("sp", xs[s][:, :n], E.xT_v[:, ft, t0:t0 + n], reads=[E.b_xT[bi]], writes=[b_xs[s]])
            dve(lambda: V.tensor_scalar(out=h1[:, ft, :n], in0=xs[s][:, :n],
                                        scalar1=E.mcol(l, 1, ft, col), scalar2=E.mcol(l, 0, ft, col),
                                        op0=ALU.mult, op1=ALU.add),
                [b_xs[s], E.b_modT], [b_h1[ft]])

        def form1(c0, ntile, post):
            for cb in range(0, ntile, 2):
                nb = min(2, ntile - cb)
                view, bw = wload(c0 + cb * 128, nb * 128)
                for j in range(nb):
                    p = nps()
                    for kt in range(KT):
                        mm(psum[p][:, :n], view[:, kt, j * 128:(j + 1) * 128], h1[:, kt, :n],
                           kt == 0, kt == KT - 1, [bw, b_h1[kt]], pbuf[p])
                    post(cb + j, p)

        def store_fm(dst, name, fi, src_tile, b_src):
            P.dma("sp", dst[fi * 128:(fi + 1) * 128, t0:t0 + n], src_tile[:, :n],
                  reads=[b_src], writes=[bd[name]])

        def post_copy(dst, name, scale=None):
            def f(fi, p):
                s = nost()
                if scale is None:
                    act(lambda: A.copy(out=ost[s][:, :n], in_=psum[p][:, :n]), [pbuf[p]], [b_ost[s]])
                else:
                    act(lambda: A.mul(out=ost[s][:, :n], in_=psum[p][:, :n], mul=scale), [pbuf[p]], [b_ost[s]])
                store_fm(dst, name, fi, ost[s], b_ost[s])
            return f

        def post_qk(dst, name, wname):
            def f(fi, p):
                act(lambda: A.activation(out=sq[:, :n], in_=psum[p][:, :n], func=AF.Square),
                    [pbuf[p]], [b_sq])
                p2 = 4 + (cnt["p"] % 2)
                mm(psum[p2][:, :n], E.onesr[:], sq[:, :n], True, True, [E.b_onesr, b_sq], pbuf[p2])
                act(lambda: A.activation(out=rs[:, :n], in_=psum[p2][:, :n], func=AF.Sqrt,
                                         bias=E.epsc[:, 0:1], scale=1.0 / HD), [pbuf[p2], E.b_epsc], [b_rs])
                dve(lambda: V.reciprocal(out=rs[:, :n], in_=rs[:, :n]), [b_rs], [b_rs])
                dve(lambda: V.scalar_tensor_tensor(out=qn[:, :n], in0=psum[p][:, :n],
                                                   scalar=E.vcol(wname, l), in1=rs[:, :n],
                                                   op0=ALU.mult, op1=ALU.mult),
                    [pbuf[p], b_rs, E.b_vecT], [b_qn])
                p3 = 6 + (cnt["p"] % 2)
                mm(psum[p3][:, :n], E.prot[:], qn[:, :n], True, True, [E.b_prot, b_qn], pbuf[p3])
                dve(lambda: V.tensor_tensor(out=t1[:, :n], in0=qn[:, :n].bitcast(F32), in1=cosT[:, t0:t0 + n],
                                            op=ALU.mult), [b_qn, b_rope], [b_t1])
                dve(lambda: V.tensor_tensor(out=t2[:, :n], in0=psum[p3][:, :n], in1=sinT[:, t0:t0 + n],
                                            op=ALU.mult), [pbuf[p3], b_rope], [b_t2])
                s = nost()
                dve(lambda: V.tensor_tensor(out=ost[s][:, :n], in0=t1[:, :n], in1=t2[:, :n], op=ALU.add),
                    [b_t1, b_t2], [b_ost[s]])
                store_fm(dst, name, fi, ost[s], b_ost[s])
            return f

        form1(OFF_GQ, 4, post_copy(E.qgT, "qgT", GLA_DK ** -0.5))
        form1(OFF_GK, 4, post_copy(E.kgT, "kgT"))
        form1(OFF_SU, 6, post_copy(E.suT, "suT"))
        form1(OFF_AQ, 8, post_qk(E.qaT, "qaT", "q_norm_w"))
        form1(OFF_AK, 2, post_qk(E.kaT, "kaT", "k_norm_w"))
        p = nps()
        for kt in range(KT):
            mm(psum[p][:16, :n], wglr[:, kt, :], h1[:, kt, :n].bitcast(F32), kt == 0, kt == KT - 1,
               [b_wglr, b_h1[kt]], pbuf[p])
        s = nost()
        act(lambda: A.copy(out=ost[s][:16, :n], in_=psum[p][:16, :n]), [pbuf[p]], [b_ost[s]])
        P.dma("sp", E.glrT[:, t0:t0 + n], ost[s][:16, :n], reads=[b_ost[s]], writes=[bd["glrT"]])

        def form2(c0, ncols, dst, name, silu):
            for cb in range(0, ncols, 256):
                view, bw = wload(c0 + cb, 256)
                for tt in range(n // 128):
                    p = nps()
                    for kt in range(KT):
                        mm(psum[p][:, :256], h1[:, kt, tt * 128:(tt + 1) * 128], view[:, kt, :],
                           kt == 0, kt == KT - 1, [bw, b_h1[kt]], pbuf[p])
                    s = nost()
                    if silu:
                        act(lambda: A.activation(out=ost[s][:, :256], in_=psum[p][:, :256], func=AF.Silu),
                            [pbuf[p]], [b_ost[s]])
                    else:
                        act(lambda: A.copy(out=ost[s][:, :256], in_=psum[p][:, :256]), [pbuf[p]], [b_ost[s]])
                    r0 = t0 + tt * 128
                    P.dma("sp", dst[r0:r0 + 128, cb:cb + 256], ost[s][:, :256],
                          reads=[b_ost[s]], writes=[bd[name]])

        form2(OFF_GV, 1024, E.vg, "vg", False)
        form2(OFF_GR, 1024, E.grt, "grt", True)
        form2(OFF_AV, 256, E.va, "va", False)


def phase_gla(nc, P, ph, l, env):
    E = NS(env)
    V, A = nc.vector, nc.scalar
    sb, psum, pbuf, Wd, bd, Cd = E.sb, E.psum, E.pbuf, E.Wd, E.bd, E.Cd
    mm, tr, dve, act = E.mm, E.tr, E.dve, E.act
    L = f"G{l}"

    def t_(name, shape, dt=F32):
        return sb(L + name, shape, dt, scope=ph), Buf(name)

    cmask, b_cmask = t_("cmask", [128, T])
    trim, b_trim = t_("trim", [64, 2, 64])
    wg, b_wg = t_("wg", [16, 2, 512])
    glr, b_glr = t_("glr", [16, T])
    gnw, b_gnw = t_("gnw", [64, 1024])
    y2, b_y2 = t_("y2", [128, T])
    cs, b_cs = t_("cs", [128, T])
    eb, b_eb = t_("eb", [128, T])
    qe, b_qe = t_("qe", [128, T])
    ke, b_ke = t_("ke", [128, T])
    kd, b_kd = t_("kd", [128, T])
    v_tm, b_v = t_("v_tm", [64, NCH, 256])
    kd_tm, b_kdt = t_("kd_tm", [64, NCH, 128])
    o_acc, b_oacc = t_("o_acc", [64, NCH, 256])
    b_oc = [Buf(f"oacc{c}") for c in range(NCH)]
    S, b_S = t_("S", [128, 256])
    attm = [t_(f"attm{i}", [64, 64]) for i in range(2)]
    grs = [t_(f"grs{i}", [64, 256]) for i in range(2)]
    on = [t_(f"on{i}", [64, 256]) for i in range(2)]
    ots = [t_(f"ots{i}", [128, 2, 64]) for i in range(2)]
    stats, b_stats = t_("stats", [64, 6])
    mv, b_mv = t_("mv", [64, 2])
    rstd, b_rstd = t_("rstd", [64, 1])

    P.dma("sp", cmask[:], Cd["cmask"][:, :], writes=[b_cmask])
    P.dma("sp", trim[:, 0, :], Cd["trif"][:, :], writes=[b_trim])
    P.dma("sp", trim[:, 1, :], Cd["trib"][:, :], writes=[b_trim])
    P.dma("sp", wg[:], Wd["w_gla_gate"][l].rearrange("d r c -> r d c"), writes=[b_wg])
    P.dma("sp", glr[:], E.glrT[:, :], reads=[bd["glrT"]], writes=[b_glr])
    P.dma("sp", gnw[:], Wd["gla_norm_w"][l:l + 1, :].to_broadcast([64, 1024]), writes=[b_gnw])
    kk = 0
    for h in range(GLA_H):
        P.dma("sp", v_tm[:], E.vg[:, h * 256:(h + 1) * 256].rearrange("(c p) f -> p c f", p=64),
              reads=[bd["vg"]], writes=[b_v])
        for d in range(2):
            P.dma("sp", qe[:], E.qgT[h * 128:(h + 1) * 128, :], reads=[bd["qgT"]], writes=[b_qe])
            P.dma("sp", ke[:], E.kgT[h * 128:(h + 1) * 128, :], reads=[bd["kgT"]], writes=[b_ke])
            negb = E.vcol("b_gla_gate", l * 8 + d * 4 + h)
            for bi, (t0, n) in enumerate(TBLK):
                pb = bi % 2
                mm(psum[pb][:, :n], wg[:, d, h * 128:(h + 1) * 128], glr[:, t0:t0 + n], True, True,
                   [b_wg, b_glr], pbuf[pb])
                act(lambda: A.activation(out=y2[:, t0:t0 + n], in_=psum[pb][:, :n], func=AF.Exp,
                                         scale=-1.0, bias=negb), [pbuf[pb], E.b_vecT], [b_y2])
            act(lambda: A.activation(out=y2[:], in_=y2[:], func=AF.Ln, bias=1.0, scale=1.0), [b_y2], [b_y2])
            if d == 0:
                dve(lambda: V.tensor_tensor_scan(out=cs[:], data0=cmask[:], data1=y2[:], initial=0.0,
                                                 op0=ALU.mult, op1=ALU.add), [b_cmask, b_y2], [b_cs])
            else:
                dve(lambda: V.tensor_tensor_scan(out=cs[:, ::-1], data0=cmask[:], data1=y2[:, ::-1],
                                                 initial=0.0, op0=ALU.mult, op1=ALU.add),
                    [b_cmask, b_y2], [b_cs])
            if E.debug and h == 0 and d == 0 and l == 0:
                P.dma("sp", E.dbg_y2, y2[:], reads=[b_y2], writes=[])
                P.dma("sp", E.dbg_cs, cs[:], reads=[b_cs], writes=[])
            act(lambda: A.activation(out=eb[:], in_=cs[:], func=AF.Exp, scale=-1.0 / GLA_TAU), [b_cs], [b_eb])
            act(lambda: A.activation(out=y2[:], in_=cs[:], func=AF.Exp, scale=1.0 / GLA_TAU), [b_cs], [b_y2])
            dve(lambda: V.tensor_tensor(out=qe[:], in0=qe[:], in1=eb[:], op=ALU.mult), [b_qe, b_eb], [b_qe])
            dve(lambda: V.tensor_tensor(out=ke[:], in0=ke[:], in1=y2[:], op=ALU.mult), [b_ke, b_y2], [b_ke])
            lpos = CH - 1 if d == 0 else 0
            eb3 = eb[:].rearrange("p (c i) -> p c i", i=CH)
            dve(lambda: V.tensor_tensor(out=kd[:].rearrange("p (c i) -> p c i", i=CH),
                                        in0=ke[:].rearrange("p (c i) -> p c i", i=CH),
                                        in1=eb3[:, :, lpos:lpos + 1].to_broadcast([128, NCH, CH]),
                                        op=ALU.mult), [b_ke, b_eb], [b_kd])
            for c4 in range(NCH // 4):
                pb = 2 + c4 % 2
                for j in range(4):
                    c = c4 * 4 + j
                    tr(psum[pb][:64, j * 128:(j + 1) * 128], kd[:, c * CH:(c + 1) * CH], [b_kd], pbuf[pb])
                act(lambda: A.copy(out=kd_tm[:, c4 * 4:(c4 + 1) * 4, :].rearrange("p a b -> p (a b)"),
                                   in_=psum[pb][:64, :]), [pbuf[pb]], [b_kdt])
            if E.debug and h == 0 and l == 0 and d == 1:
                P.dma("sp", E.dbg_oraw, o_acc[:].rearrange("p a b -> p (a b)"), reads=b_oc, writes=[])
            dve(lambda: V.memset(S[:], 0.0), [], [b_S])
            order = list(range(NCH)) if d == 0 else [3, 2, 1, 0] + list(range(NCH - 1, 3, -1))
            for c in order:
                sl = slice(c * CH, (c + 1) * CH)
                s = kk % 2
                kk += 1
                am, b_am = attm[s]
                mm(psum[4][:64, :64], ke[:, sl], qe[:, sl], True, True, [b_ke, b_qe], pbuf[4])
                dve(lambda: V.tensor_tensor(out=am[:], in0=psum[4][:64, :64], in1=trim[:, d, :], op=ALU.mult),
                    [pbuf[4], b_trim], [b_am])
                mm(psum[5][:64, :256], am[:], v_tm[:, c, :], True, False, [b_am, b_v], pbuf[5])
                mm(psum[5][:64, :256], qe[:, sl], S[:], False, True, [b_qe, b_S], pbuf[5])
                mm(psum[6][:, :256], kd_tm[:, c, :], v_tm[:, c, :], True, True, [b_kdt, b_v], pbuf[6])
                if d == 0:
                    act(lambda: A.copy(out=o_acc[:, c, :], in_=psum[5][:64, :256]), [pbuf[5]], [b_oc[c]])
                else:
                    dve(lambda: V.tensor_tensor(out=o_acc[:, c, :], in0=psum[5][:64, :256], in1=o_acc[:, c, :],
                                                op=ALU.add), [pbuf[5], b_oc[c]], [b_oc[c]])
                li = c * CH + lpos
                dve(lambda: V.scalar_tensor_tensor(out=S[:], in0=S[:], scalar=eb[:, li:li + 1], in1=psum[6][:, :256],
                                                   op0=ALU.mult, op1=ALU.add), [b_S, b_eb, pbuf[6]], [b_S])
        for c in range(NCH):
            s = c % 2
            ont, b_on = on[s]
            grt_, b_gr = grs[s]
            ott, b_ot = ots[s]
            P.dma("sp", grt_[:], E.grt[c * CH:(c + 1) * CH, h * 256:(h + 1) * 256], reads=[bd["grt"]],
                  writes=[b_gr])
            dve(lambda: V.bn_stats(out=stats[:], in_=o_acc[:, c, :]), [b_oc[c]], [b_stats])
            dve(lambda: V.bn_aggr(out=mv[:], in_=stats[:]), [b_stats], [b_mv])
            act(lambda: A.activation(out=rstd[:], in_=mv[:, 1:2], func=AF.Sqrt, bias=E.epsc[:64, 0:1], scale=1.0),
                [b_mv, E.b_epsc], [b_rstd])
            dve(lambda: V.reciprocal(out=rstd[:], in_=rstd[:]), [b_rstd], [b_rstd])
            dve(lambda: V.tensor_scalar(out=ont[:], in0=o_acc[:, c, :], scalar1=mv[:, 0:1], scalar2=rstd[:, 0:1],
                                        op0=ALU.subtract, op1=ALU.mult), [b_oc[c], b_mv, b_rstd], [b_on])
            dve(lambda: V.tensor_tensor(out=ont[:], in0=ont[:], in1=gnw[:, h * 256:(h + 1) * 256], op=ALU.mult),
                [b_on, b_gnw], [b_on])
            dve(lambda: V.tensor_tensor(out=ont[:], in0=ont[:], in1=grt_[:], op=ALU.mult), [b_on, b_gr], [b_on])
            for half in range(2):
                tr(psum[7][:, half * 64:(half + 1) * 64], ont[:, half * 128:(half + 1) * 128], [b_on], pbuf[7])
            act(lambda: A.copy(out=ott[:].rearrange("p a b -> p (a b)"), in_=psum[7][:, :128]), [pbuf[7]], [b_ot])
            P.dma("sp", E.oglaT[h * 256:(h + 1) * 256, c * CH:(c + 1) * CH].rearrange("(a p) t -> p a t", p=128),
                  ott[:], reads=[b_ot], writes=[bd["oglaT"]])


def phase_s5(nc, P, ph, l, env):
    E = NS(env)
    V, A = nc.vector, nc.scalar
    sb, psum, pbuf, Wd, bd, Cd = E.sb, E.psum, E.pbuf, E.Wd, E.bd, E.Cd
    mm, tr, dve, act = E.mm, E.tr, E.dve, E.act
    L = f"S{l}"
    NG = S5_NG

    def t_(name, shape, dt=F32):
        return sb(L + name, shape, dt, scope=ph), Buf(name)

    def ts(out, in0, s1, s2, op0, op1, rd, wr):
        dve(lambda: V.tensor_scalar(out=out, in0=in0, scalar1=s1, scalar2=s2, op0=op0, op1=op1), rd, wr)

    def tt(out, in0, in1, op, rd, wr):
        dve(lambda: V.tensor_tensor(out=out, in0=in0, in1=in1, op=op), rd, wr)

    ri, b_ri = t_("ri", [128, 512], I32)
    rf, b_rf = t_("rf", [128, 512])

    def wrap_half(r, b_r):
        dve(lambda: V.tensor_single_scalar(out=rf[:, :r.shape[1]], in_=r, scalar=0.5, op=ALU.is_gt), [b_r], [b_rf])
        tt(r, r, rf[:, :r.shape[1]], ALU.subtract, [b_r, b_rf], [b_r])
        dve(lambda: V.tensor_single_scalar(out=rf[:, :r.shape[1]], in_=r, scalar=-0.5, op=ALU.is_lt), [b_r], [b_rf])
        tt(r, r, rf[:, :r.shape[1]], ALU.add, [b_r, b_rf], [b_r])

    def frac_wrap(r, b_r):
        n = r.shape[1]
        dve(lambda: V.tensor_copy(out=ri[:, :n], in_=r), [b_r], [b_ri])
        dve(lambda: V.tensor_copy(out=rf[:, :n], in_=ri[:, :n]), [b_ri], [b_rf])
        tt(r, r, rf[:, :n], ALU.subtract, [b_r, b_rf], [b_r])
        wrap_half(r, b_r)

    def sincos(r, b_r, sin_out, b_s, cos_out, b_c):
        frac_wrap(r, b_r)
        act(lambda: A.activation(out=sin_out, in_=r, func=AF.Sin, scale=SIN_SCALE), [b_r], [b_s])
        ts(r, r, 0.25, None, ALU.add, ALU.bypass, [b_r], [b_r])
        wrap_half(r, b_r)
        act(lambda: A.activation(out=cos_out, in_=r, func=AF.Sin, scale=SIN_SCALE), [b_r], [b_c])

    iota, b_iota = t_("iota", [128, 512])
    P.dma("sp", iota[:], Cd["iota512"][:, :], writes=[b_iota])
    rmask, b_rmask = t_("rmask", [128, 8])
    P.dma("sp", rmask[:], Cd["rmask"][:, :], writes=[b_rmask])
    colmask, b_colmask = t_("colmask", [128, 8, 128])
    P.dma("sp", colmask[:].rearrange("p a b -> p (a b)"), Cd["colmask"][:, :], writes=[b_colmask])
    pst, b_pst = t_("pst", [128, 128])
    par = {}
    NOFF = [256, 768, 1280, 1792]
    for d in range(2):
        pp = {}
        for nm in ("lamr", "lami", "dt", "rho", "thp", "sn", "cs", "kre", "kim", "w1", "w2", "w3"):
            pp[nm] = t_(f"{nm}{d}", [128, NG])
        pp["Ec"] = t_(f"Ec{d}", [128, 4, NG])
        pp["Es"] = t_(f"Es{d}", [128, 4, NG])
        par[d] = pp
        for nm, src in (("lamr", "s5_lam_re"), ("lami", "s5_lam_im")):
            P.dma("sp", pst[:NG, 0:64], Wd[src][l, d], writes=[b_pst])
            P.dma("sp", pst[:NG, 64:128], Wd[src][l, d], writes=[b_pst])
            tr(psum[0][:, :NG], pst[:NG, :], [b_pst], pbuf[0])
            dve(lambda: V.tensor_copy(out=pp[nm][0][:], in_=psum[0][:, :NG]), [pbuf[0]], [pp[nm][1]])
        dtt, b_dt = pp["dt"]
        P.dma("sp", dtt[:], Wd["s5_log_dt"][l, d:d + 1, :].to_broadcast([128, NG]), writes=[b_dt])
        act(lambda: A.activation(out=dtt[:], in_=dtt[:], func=AF.Exp), [b_dt], [b_dt])
        lamr, b_lamr = pp["lamr"]
        lami, b_lami = pp["lami"]
        rho, b_rho = pp["rho"]
        thp, b_thp = pp["thp"]
        sn, b_sn = pp["sn"]
        cs_, b_cs = pp["cs"]
        kre, b_kre = pp["kre"]
        kim, b_kim = pp["kim"]
        w1, b_w1 = pp["w1"]
        w2, b_w2 = pp["w2"]
        w3, b_w3 = pp["w3"]
        tt(w1[:], lamr[:], dtt[:], ALU.mult, [b_lamr, b_dt], [b_w1])
        act(lambda: A.activation(out=rho[:], in_=w1[:], func=AF.Exp), [b_w1], [b_rho])
        tt(thp[:], lami[:], dtt[:], ALU.mult, [b_lami, b_dt], [b_thp])
        ts(thp[:], thp[:], 1.0 / TWO_PI, None, ALU.mult, ALU.bypass, [b_thp], [b_thp])
        dve(lambda: V.tensor_copy(out=w1[:], in_=thp[:]), [b_thp], [b_w1])
        sincos(w1[:], b_w1, sn[:], b_sn, cs_[:], b_cs)
        tt(w1[:], rho[:], cs_[:], ALU.mult, [b_rho, b_cs], [b_w1])
        ts(w1[:], w1[:], -1.0, None, ALU.add, ALU.bypass, [b_w1], [b_w1])
        tt(w2[:], rho[:], sn[:], ALU.mult, [b_rho, b_sn], [b_w2])
        tt(w3[:], lamr[:], lamr[:], ALU.mult, [b_lamr], [b_w3])
        tt(kre[:], lami[:], lami[:], ALU.mult, [b_lami], [b_kre])
        tt(w3[:], w3[:], kre[:], ALU.add, [b_w3, b_kre], [b_w3])
        dve(lambda: V.reciprocal(out=w3[:], in_=w3[:]), [b_w3], [b_w3])
        tt(kre[:], w1[:], lamr[:], ALU.mult, [b_w1, b_lamr], [b_kre])
        tt(kim[:], w2[:], lami[:], ALU.mult, [b_w2, b_lami], [b_kim])
        tt(kre[:], kre[:], kim[:], ALU.add, [b_kre, b_kim], [b_kre])
        tt(kre[:], kre[:], w3[:], ALU.mult, [b_kre, b_w3], [b_kre])
        tt(kim[:], w2[:], lamr[:], ALU.mult, [b_w2, b_lamr], [b_kim])
        tt(w2[:], w1[:], lami[:], ALU.mult, [b_w1, b_lami], [b_w2])
        tt(kim[:], kim[:], w2[:], ALU.subtract, [b_kim, b_w2], [b_kim])
        tt(kim[:], kim[:], w3[:], ALU.mult, [b_kim, b_w3], [b_kim])
        Ec, b_Ec = pp["Ec"]
        Es, b_Es = pp["Es"]
        for k, noff in enumerate(NOFF):
            ts(w1[:], thp[:], float(noff), None, ALU.mult, ALU.bypass, [b_thp], [b_w1])
            sincos(w1[:], b_w1, Es[:, k, :], b_Es, Ec[:, k, :], b_Ec)

    bre, b_bre = t_("bre", [128, NG, 16])
    bim, b_bim = t_("bim", [128, NG, 16])
    for (tile_, b_, src) in ((bre, b_bre, "s5_b_re"), (bim, b_bim, "s5_b_im")):
        v_ = Wd[src][l].rearrange("g p c -> p g c")
        P.dma("sp", tile_[0:64], v_, writes=[b_])
        P.dma("sp", tile_[64:128], v_, writes=[b_])
    Pre, b_Pre = t_("Pre", [128, 8, 16])
    Pim, b_Pim = t_("Pim", [128, 8, 16])
    pw, b_pw = t_("pw", [128, 8, 16])
    X1, b_X1 = t_("X1", [128, 128])
    X2, b_X2 = t_("X2", [128, 128])
    TB = [t_(f"TB{i}", [128, 128]) for i in range(2)]
    TW = [t_(f"TW{i}", [128, 128]) for i in range(2)]
    cst, b_cst = t_("cstg", [128, 2, 128])
    LB = [[t_(f"LB{g}_{i}", [128, 128], F32R) for i in range(2)] for g in range(8)]
    WG = [[t_(f"WG{g}_{i}", [128, 128], F32R) for i in range(2)] for g in range(8)]
    cosm = [t_(f"cosm{g}", [128, 512]) for g in range(8)]
    sinm = [t_(f"sinm{g}", [128, 512]) for g in range(8)]
    NSET = 3
    SETS = []
    for i in range(NSET):
        SETS.append({"cn": t_(f"qcn{i}", [128, 512]), "sn": t_(f"qsn{i}", [128, 512]),
                     "m1": t_(f"qm1{i}", [128, 512]), "vv": t_(f"qvv{i}", [128, 512]),
                     "P1": t_(f"qP1{i}", [128, 512], F32R), "P2": t_(f"qP2{i}", [128, 512], F32R)})
    tq, b_tq = t_("tq", [128, 512])
    st8, b_st8 = t_("st8", [128, 8])
    b_st = [Buf(f"st{g}") for g in range(8)]
    negEs = [t_(f"negEs{d}", [128, 4, NG]) for d in range(2)]
    for d in range(2):
        Es, b_Es = par[d]["Es"]
        ts(negEs[d][0][:].rearrange("p a b -> p (a b)"), Es[:].rearrange("p a b -> p (a b)"), -1.0, None,
           ALU.mult, ALU.bypass, [b_Es], [negEs[d][1]])

    suo, b_suo = t_("suo", [128, T], F32R)
    yacc, b_yacc = t_("yacc", [128, T])
    yg = [t_(f"yg{i}", [128, T], F32R) for i in range(6)]

    TILES = {0: [(0, 256), (256, 512), (768, 512), (1280, 512), (1792, 512)],
             1: [(0, 256), (1792, 512), (1280, 512), (768, 512), (256, 512)]}
    GROUPS3 = [(0, 1, 2), (3, 4, 5), (6, 7)]
    setctr = [0]

    for o in range(6):
        P.dma("pool", suo[:], E.suT[o * 128:(o + 1) * 128, :], reads=[bd["suT"]], writes=[b_suo])
        P.dma("sp", yacc[:], E.suT[o * 128:(o + 1) * 128, :], reads=[bd["suT"]], writes=[b_yacc])
        ts(yacc[:], yacc[:], E.vcol("s5_d", l * 6 + o), None, ALU.mult, ALU.bypass, [b_yacc, E.b_vecT], [b_yacc])
        for d in range(2):
            pp = par[d]
            kre, b_kre = pp["kre"]
            kim, b_kim = pp["kim"]
            gs = slice(o * 8, (o + 1) * 8)
            kreb = kre[:, gs].unsqueeze(2).to_broadcast([128, 8, 16])
            kimb = kim[:, gs].unsqueeze(2).to_broadcast([128, 8, 16])
            tt(Pre[:], bre[:, gs, :], kreb, ALU.mult, [b_bre, b_kre], [b_Pre])
            tt(pw[:], bim[:, gs, :], kimb, ALU.mult, [b_bim, b_kim], [b_pw])
            tt(Pre[:], Pre[:], pw[:], ALU.subtract, [b_Pre, b_pw], [b_Pre])
            tt(Pim[:], bim[:, gs, :], kreb, ALU.mult, [b_bim, b_kre], [b_Pim])
            tt(pw[:], bre[:, gs, :], kimb, ALU.mult, [b_bre, b_kim], [b_pw])
            tt(Pim[:], Pim[:], pw[:], ALU.add, [b_Pim, b_pw], [b_Pim])
            Pre2 = Pre[:].rearrange("p g c -> p (g c)")
            Pim2 = Pim[:].rearrange("p g c -> p (g c)")
            dve(lambda: V.tensor_copy(out=X1[0:64, :], in_=Pre2[0:64, :]), [b_Pre], [b_X1])
            dve(lambda: V.tensor_copy(out=X1[64:128, :], in_=Pim2[64:128, :]), [b_Pim], [b_X1])
            dve(lambda: V.tensor_copy(out=X2[0:64, :], in_=Pim2[0:64, :]), [b_Pim], [b_X2])
            ts(X2[64:128, :], Pre2[64:128, :], -1.0, None, ALU.mult, ALU.bypass, [b_Pre], [b_X2])
            for i, (X_, b_X) in enumerate(((X1, b_X1), (X2, b_X2))):
                tr(psum[i][:, :128], X_[:, :], [b_X], pbuf[i])
                dve(lambda: V.tensor_copy(out=TB[i][0][:], in_=psum[i][:, :128]), [pbuf[i]], [TB[i][1]])
            crv = Wd["s5_c_re"][l, d].rearrange("g c p -> (g c) p")[o * 128:(o + 1) * 128, :]
            civ = Wd["s5_c_im"][l, d].rearrange("g c p -> (g c) p")[o * 128:(o + 1) * 128, :]
            P.dma("sp", cst[:, 0, 0:64], crv, writes=[b_cst])
            P.dma("sp", cst[:, 0, 64:128], civ, writes=[b_cst])
            P.dma("sp", cst[:, 1, 0:64], civ, writes=[b_cst])
            P.dma("sp", cst[:, 1, 64:128], crv, writes=[b_cst])
            for i in range(2):
                tr(psum[2 + i][:, :128], cst[:, i, :], [b_cst], pbuf[2 + i])
            dve(lambda: V.tensor_copy(out=TW[0][0][0:64, :], in_=psum[2][0:64, :128]), [pbuf[2]], [TW[0][1]])
            ts(TW[0][0][64:128, :], psum[2][64:128, :128], -1.0, None, ALU.mult, ALU.bypass, [pbuf[2]], [TW[0][1]])
            ts(TW[1][0][:], psum[3][:, :128], -1.0, None, ALU.mult, ALU.bypass, [pbuf[3]], [TW[1][1]])
            rho, b_rho = pp["rho"]
            thp, b_thp = pp["thp"]
            Ec, b_Ec = pp["Ec"]
            Es, b_Es = pp["Es"]
            nEs, b_nEs = negEs[d]
            for gl in range(8):
                g = o * 8 + gl
                for i in range(2):
                    ts(LB[gl][i][0][:], TB[i][0][:], rmask[:, gl:gl + 1], None, ALU.mult, ALU.bypass,
                       [TB[i][1], b_rmask], [LB[gl][i][1]])
                    tt(WG[gl][i][0][:], TW[i][0][:], colmask[:, gl, :], ALU.mult, [TW[i][1], b_colmask],
                       [WG[gl][i][1]])
                ts(tq[:], iota[:], thp[:, g:g + 1], None, ALU.mult, ALU.bypass, [b_iota, b_thp], [b_tq])
                sincos(tq[:], b_tq, sinm[gl][0][:], sinm[gl][1], cosm[gl][0][:], cosm[gl][1])
            rv = (lambda ap: ap[:, ::-1]) if d == 1 else (lambda ap: ap)
            for k, (f0, n) in enumerate(TILES[d]):
                py = 6 + k % 2
                nmm = [0]
                for grp in GROUPS3:
                    ctx_ = {}
                    for gl in grp:
                        S_ = SETS[setctr[0] % NSET]
                        setctr[0] += 1
                        ctx_[gl] = S_
                    for gl in grp:
                        pa = (gl % 3) * 2
                        mm(psum[pa][:, :n], LB[gl][0][0][:], suo[:, f0:f0 + n], True, True,
                           [LB[gl][0][1], b_suo], pbuf[pa])
                        mm(psum[pa + 1][:, :n], LB[gl][1][0][:], suo[:, f0:f0 + n], True, True,
                           [LB[gl][1][1], b_suo], pbuf[pa + 1])
                    tabs = {}
                    if k == 0:
                        for gl in grp:
                            tabs[gl] = (cosm[gl], sinm[gl])
                    else:
                        for gl in grp:
                            g = o * 8 + gl
                            cn_, sn_ = ctx_[gl]["cn"], ctx_[gl]["sn"]
                            ec = Ec[:, k - 1, g:g + 1]
                            act(lambda: A.activation(out=cn_[0][:], in_=cosm[gl][0][:], func=AF.Identity, scale=ec),
                                [cosm[gl][1], b_Ec], [cn_[1]])
                            act(lambda: A.activation(out=sn_[0][:], in_=sinm[gl][0][:], func=AF.Identity, scale=ec),
                                [sinm[gl][1], b_Ec], [sn_[1]])
                        for gl in grp:
                            g = o * 8 + gl
                            cn_, sn_ = ctx_[gl]["cn"], ctx_[gl]["sn"]
                            dve(lambda: V.scalar_tensor_tensor(out=cn_[0][:], in0=sinm[gl][0][:],
                                                               scalar=nEs[:, k - 1, g:g + 1], in1=cn_[0][:],
                                                               op0=ALU.mult, op1=ALU.add),
                                [sinm[gl][1], b_nEs, cn_[1]], [cn_[1]])
                        for gl in grp:
                            g = o * 8 + gl
                            cn_, sn_ = ctx_[gl]["cn"], ctx_[gl]["sn"]
                            dve(lambda: V.scalar_tensor_tensor(out=sn_[0][:], in0=cosm[gl][0][:],
                                                               scalar=Es[:, k - 1, g:g + 1], in1=sn_[0][:],
                                                               op0=ALU.mult, op1=ALU.add),
                                [cosm[gl][1], b_Es, sn_[1]], [sn_[1]])
                            tabs[gl] = (cn_, sn_)
                    for gl in grp:
                        pa = (gl % 3) * 2
                        m1_, b_m1_ = ctx_[gl]["m1"]
                        tt(m1_[:, :n], psum[pa][:, :n], rv(tabs[gl][0][0][:, :n]), ALU.mult,
                           [pbuf[pa], tabs[gl][0][1]], [b_m1_])
                    for gl in grp:
                        pa = (gl % 3) * 2
                        vv_, b_vv_ = ctx_[gl]["vv"]
                        tt(vv_[:, :n], psum[pa + 1][:, :n], rv(tabs[gl][1][0][:, :n]), ALU.mult,
                           [pbuf[pa + 1], tabs[gl][1][1]], [b_vv_])
                    for gl in grp:
                        m1_, b_m1_ = ctx_[gl]["m1"]
                        vv_, b_vv_ = ctx_[gl]["vv"]
                        P.op("pool", lambda: nc.gpsimd.tensor_tensor(out=vv_[:, :n], in0=vv_[:, :n], in1=m1_[:, :n],
                                                                     op=ALU.add),
                             reads=[b_vv_, b_m1_], writes=[b_vv_])
                    for gl in grp:
                        g = o * 8 + gl
                        m1_, b_m1_ = ctx_[gl]["m1"]
                        vv_, b_vv_ = ctx_[gl]["vv"]
                        init = 0.0 if k == 0 else st8[:, gl:gl + 1]
                        dve(lambda: V.tensor_tensor_scan(out=rv(m1_[:, :n]),
                                                         data0=rho[:, g:g + 1].to_broadcast([128, n]),
                                                         data1=rv(vv_[:, :n]), initial=init,
                                                         op0=ALU.mult, op1=ALU.add),
                            [b_rho, b_vv_, b_st[gl]], [b_m1_])
                    lastc = n - 1 if d == 0 else 0
                    for gl in grp:
                        m1_, b_m1_ = ctx_[gl]["m1"]
                        act(lambda: A.copy(out=st8[:, gl:gl + 1], in_=m1_[:, lastc:lastc + 1]), [b_m1_], [b_st[gl]])
                    for gl in grp:
                        m1_, b_m1_ = ctx_[gl]["m1"]
                        P1_, b_P1_ = ctx_[gl]["P1"]
                        tt(P1_[:, :n], m1_[:, :n], rv(tabs[gl][0][0][:, :n]), ALU.mult, [b_m1_, tabs[gl][0][1]], [b_P1_])
                    for gl in grp:
                        m1_, b_m1_ = ctx_[gl]["m1"]
                        P2_, b_P2_ = ctx_[gl]["P2"]
                        P.op("pool", lambda: nc.gpsimd.tensor_tensor(out=P2_[:, :n], in0=m1_[:, :n],
                                                                     in1=rv(tabs[gl][1][0][:, :n]), op=ALU.mult),
                             reads=[b_m1_, tabs[gl][1][1]], writes=[b_P2_])
                    for gl in grp:
                        P1_, b_P1_ = ctx_[gl]["P1"]
                        P2_, b_P2_ = ctx_[gl]["P2"]
                        mm(psum[py][:, :n], WG[gl][0][0][:], P1_[:, :n], nmm[0] == 0, False, [WG[gl][0][1], b_P1_], pbuf[py])
                        nmm[0] += 1
                        mm(psum[py][:, :n], WG[gl][1][0][:], P2_[:, :n], False, nmm[0] == 15, [WG[gl][1][1], b_P2_], pbuf[py])
                        nmm[0] += 1
                tt(yacc[:, f0:f0 + n], psum[py][:, :n], yacc[:, f0:f0 + n], ALU.add, [pbuf[py], b_yacc], [b_yacc])
        m1, b_m1 = SETS[0]["m1"]
        vv, b_vv = SETS[0]["vv"]
        ygt, b_yg = yg[o]
        for (t0, n) in TBLK:
            xa = yacc[:, t0:t0 + n]
            act(lambda: A.activation(out=m1[:, :n], in_=xa, func=AF.Square), [b_yacc], [b_m1])
            ts(m1[:, :n], m1[:, :n], 0.044715, 1.0, ALU.mult, ALU.add, [b_m1], [b_m1])
            tt(m1[:, :n], m1[:, :n], xa, ALU.mult, [b_m1, b_yacc], [b_m1])
            act(lambda: A.activation(out=vv[:, :n], in_=m1[:, :n], func=AF.Sigmoid,
                                     scale=2.0 * math.sqrt(2.0 / math.pi)), [b_m1], [b_vv])
            tt(ygt[:, t0:t0 + n], xa, vv[:, :n], ALU.mult, [b_yacc, b_vv], [b_yg])
    wgl, b_wgl = t_("wgl", [128, 6, 128], F32R)
    wv = Wd["w_s5_glu"][l].rearrange("(k p) c -> p k c", p=128)
    kk = 0
    for fo in range(6):
        P.dma("pool", wgl[:], wv[:, :, fo * 128:(fo + 1) * 128], writes=[b_wgl])
        for (t0, n) in TBLK:
            pb = kk % 2
            kk += 1
            for k in range(6):
                mm(psum[pb][:, :n], wgl[:, k, :], yg[k][0][:, t0:t0 + n], k == 0, k == 5,
                   [b_wgl, yg[k][1]], pbuf[pb])
            sgt, b_sgt = SETS[0]["m1"] if pb == 0 else SETS[0]["vv"]
            ott, b_ott = SETS[1]["m1"] if pb == 0 else SETS[1]["vv"]
            act(lambda: A.activation(out=sgt[:, :n], in_=psum[pb][:, :n], func=AF.Sigmoid), [pbuf[pb]], [b_sgt])
            tt(ott[:, :n], yg[fo][0][:, t0:t0 + n].bitcast(F32), sgt[:, :n], ALU.mult, [yg[fo][1], b_sgt], [b_ott])
            P.dma("sp", E.os5T[fo * 128:(fo + 1) * 128, t0:t0 + n], ott[:, :n], reads=[b_ott], writes=[bd["os5T"]])


def phase_att(nc, P, ph, l, env):
    E = NS(env)
    V, A = nc.vector, nc.scalar
    sb, psum, pbuf, bd = E.sb, E.psum, E.pbuf, E.bd
    mm, dve, act = E.mm, E.dve, E.act
    L = f"T{l}"
    kT = sb(L + "kT", [128, T], F32R, scope=ph)
    b_kT = Buf("kT")
    vv = sb(L + "vv", [128, NT, 128], F32R, scope=ph)
    b_vv = Buf("vv")
    qb = [sb(L + f"qb{i}", [128, 512], F32R, scope=ph) for i in range(2)]
    b_qb = [Buf(f"qb{i}") for i in range(2)]
    pt = [sb(L + f"pt{i}", [128, 512], F32R, scope=ph) for i in range(3)]
    b_pt = [Buf(f"pt{i}") for i in range(3)]
    rden = sb(L + "rden", [128, 512], scope=ph)
    b_rden = Buf("rden")
    ob = [sb(L + f"ob{i}", [128, 512], scope=ph) for i in range(2)]
    b_ob = [Buf(f"ob{i}") for i in range(2)]
    k = 0
    for g in range(AKV_H):
        P.dma("pool", kT[:], E.kaT[g * 128:(g + 1) * 128, :], reads=[bd["kaT"]], writes=[b_kT])
        P.dma("pool", vv[:], E.va[:, g * 128:(g + 1) * 128].rearrange("(j p) d -> p j d", p=128),
              reads=[bd["va"]], writes=[b_vv])
        for hh in range(4):
            h = g * 4 + hh
            for (t0, n) in TBLK:
                key_tiles = [0, 1] if t0 == 0 else list(range(NT))
                s = k % 2
                po, pd = 4 + 2 * s, 5 + 2 * s
                k += 1
                P.dma("pool", qb[s][:, :n], E.qaT[h * 128:(h + 1) * 128, t0:t0 + n],
                      reads=[bd["qaT"]], writes=[b_qb[s]])
                for idx, j in enumerate(key_tiles):
                    ps = idx % 3
                    last = idx == len(key_tiles) - 1
                    mm(psum[ps][:, :n], kT[:, j * 128:(j + 1) * 128], qb[s][:, :n], True, True,
                       [b_kT, b_qb[s]], pbuf[ps])
                    act(lambda: A.activation(out=pt[ps][:, :n], in_=psum[ps][:, :n], func=AF.Exp,
                                             scale=HD ** -0.5), [pbuf[ps]], [b_pt[ps]])
                    mm(psum[po][:, :n], vv[:, j, :], pt[ps][:, :n], idx == 0, last,
                       [b_vv, b_pt[ps]], pbuf[po])
                    mm(psum[pd][:, :n], E.onesr[:], pt[ps][:, :n], idx == 0, last,
                       [E.b_onesr, b_pt[ps]], pbuf[pd])
                dve(lambda: V.reciprocal(out=rden[:, :n], in_=psum[pd][:, :n]), [pbuf[pd]], [b_rden])
                dve(lambda: V.tensor_tensor(out=ob[s][:, :n], in0=psum[po][:, :n], in1=rden[:, :n],
                                            op=ALU.mult), [pbuf[po], b_rden], [b_ob[s]])
                P.dma("sp", E.oattT[h * 128:(h + 1) * 128, t0:t0 + n], ob[s][:, :n],
                      reads=[b_ob[s]], writes=[bd["oattT"]])


def phase_C(nc, P, ph, l, env):
    E = NS(env)
    V, A = nc.vector, nc.scalar
    sb, psum, pbuf, Wd, bd = E.sb, E.psum, E.pbuf, E.Wd, E.bd
    mm, dve, act = E.mm, E.dve, E.act
    L = f"C{l}"
    R1 = sb(L + "R1", [128, KT, 512], F32R, scope=ph)
    R2 = sb(L + "R2", [128, KT, 512], F32, scope=ph)
    R3 = sb(L + "R3", [128, 11, 512], F32R, scope=ph)
    R4 = sb(L + "R4", [128, KT, 512], F32R, scope=ph)
    O2 = sb(L + "O2", [128, 11, 512], F32R, scope=ph)
    bO2 = [Buf(f"O2_{i}") for i in range(11)]
    bR1 = [Buf(f"R1_{i}") for i in range(KT)]
    bR2 = [Buf(f"R2_{i}") for i in range(KT)]
    bR3 = [Buf(f"R3_{i}") for i in range(11)]
    bR4 = [Buf(f"R4_{i}") for i in range(KT)]
    NW = 4
    wb = [sb(L + f"wb{i}", [128, 2048], F32R, scope=ph) for i in range(NW)]
    b_wb = [Buf(f"wb{i}") for i in range(NW)]
    SC = [sb(L + f"sc{i}", [128, 512], scope=ph) for i in range(7)]
    bSC = [Buf(f"sc{i}") for i in range(7)]
    xs, b_xs = SC[0:2], bSC[0:2]
    sg, b_sg = SC[0:3], bSC[0:3]
    tmp, b_tmp = SC[3:5], bSC[3:5]
    sqs, b_sqs = SC[0:2], bSC[0:2]
    mean, b_mean = SC[2], bSC[2]
    rstd, b_rstd = SC[5], bSC[5]
    msq, b_msq = SC[6], bSC[6]
    cnt = {"w": 0, "x": 0, "t": 0, "q": 0}
    w_in_v = Wd["w_in"][l].rearrange("(kt p) c -> p kt c", p=128)
    wpg_v = Wd["w_proj_gla"][l].rearrange("(k p) c -> p k c", p=128)
    wps_v = Wd["w_proj_s5"][l].rearrange("(k p) c -> p k c", p=128)
    wpa_v = Wd["w_proj_attn"][l].rearrange("(k p) c -> p k c", p=128)
    wo_v = Wd["w_out"][l].rearrange("(k p) c -> p k c", p=128)
    wfi_v = Wd["w_ffn_in"][l].rearrange("(k p) c -> p k c", p=128)

    def wload(src3, nk):
        s = cnt["w"] % NW
        cnt["w"] += 1
        view = wb[s][:, :nk * 128].rearrange("p (k c) -> p k c", c=128)
        P.dma("pool", view, src3, writes=[b_wb[s]])
        return view, b_wb[s]

    def otile(i, n):
        if i < 11:
            return R3[:, i, :n], bR3[i]
        return O2[:, i - 11, :n], bO2[i - 11]

    def layer_norm(n, wname, bname, post):
        for f in range(KT):
            s = f % 2
            act(lambda: A.activation(out=sqs[s][:, :n], in_=R2[:, f, :n], func=AF.Square),
                [bR2[f]], [b_sqs[s]])
            mm(psum[6][:, :n], E.onesf[:], R2[:, f, :n], f == 0, f == KT - 1, [E.b_onesf, bR2[f]], pbuf[6])
            mm(psum[7][:, :n], E.onesf[:], sqs[s][:, :n], f == 0, f == KT - 1, [E.b_onesf, b_sqs[s]], pbuf[7])
        act(lambda: A.mul(out=mean[:, :n], in_=psum[6][:, :n], mul=1.0 / D), [pbuf[6]], [b_mean])
        dve(lambda: V.tensor_tensor(out=msq[:, :n], in0=mean[:, :n], in1=mean[:, :n], op=ALU.mult),
            [b_mean], [b_msq])
        dve(lambda: V.scalar_tensor_tensor(out=msq[:, :n], in0=psum[7][:, :n], scalar=1.0 / D,
                                           in1=msq[:, :n], op0=ALU.mult, op1=ALU.subtract),
            [pbuf[7], b_msq], [b_msq])
        act(lambda: A.activation(out=rstd[:, :n], in_=msq[:, :n], func=AF.Sqrt, bias=E.epsc[:, 0:1],
                                 scale=1.0), [b_msq, E.b_epsc], [b_rstd])
        dve(lambda: V.reciprocal(out=rstd[:, :n], in_=rstd[:, :n]), [b_rstd], [b_rstd])
        for f in range(KT):
            s = cnt["t"] % 2
            cnt["t"] += 1
            dve(lambda: V.tensor_tensor(out=tmp[s][:, :n], in0=R2[:, f, :n], in1=mean[:, :n],
                                        op=ALU.subtract), [bR2[f], b_mean], [b_tmp[s]])
            dve(lambda: V.tensor_tensor(out=tmp[s][:, :n], in0=tmp[s][:, :n], in1=rstd[:, :n],
                                        op=ALU.mult), [b_tmp[s], b_rstd], [b_tmp[s]])
            act(lambda: A.activation(out=R2[:, f, :n], in_=tmp[s][:, :n], func=AF.Identity,
                                     scale=E.vcol(wname, l * 16 + f), bias=E.vcol(bname, l * 16 + f)),
                [b_tmp[s], E.b_vecT], [bR2[f]])
            post(f)

    for bi, (t0, n) in enumerate(TBLK):
        col = 1 if bi == 0 else 0
        for ft in range(KT):
            s = cnt["x"] % 2
            cnt["x"] += 1
            P.dma("sp", xs[s][:, :n], E.xT_v[:, ft, t0:t0 + n], reads=[E.b_xT[bi]], writes=[b_xs[s]])
            dve(lambda: V.tensor_scalar(out=R1[:, ft, :n], in0=xs[s][:, :n],
                                        scalar1=E.mcol(l, 1, ft, col), scalar2=E.mcol(l, 0, ft, col),
                                        op0=ALU.mult, op1=ALU.add),
                [b_xs[s], E.b_modT], [bR1[ft]])
        P.dma("pool", R3[:, 0:8, :n], E.fv(E.oglaT)[:, :, t0:t0 + n], reads=[bd["oglaT"]], writes=bR3[0:8])
        P.dma("pool", R3[:, 8:11, :n], E.fv(E.os5T)[:, 0:3, t0:t0 + n], reads=[bd["os5T"]], writes=bR3[8:11])
        P.dma("pool", O2[:, 0:3, :n], E.fv(E.os5T)[:, 3:6, t0:t0 + n],
              reads=[bd["os5T"]], writes=bO2[0:3])
        P.dma("pool", O2[:, 3:11, :n], E.fv(E.oattT)[:, :, t0:t0 + n],
              reads=[bd["oattT"]], writes=bO2[3:11])
        for f in range(KT):
            for br in range(3):
                wv_, bw = wload(w_in_v[:, :, OFF_BG + br * D + f * 128:OFF_BG + br * D + (f + 1) * 128], KT)
                for kt in range(KT):
                    mm(psum[br][:, :n], wv_[:, kt, :], R1[:, kt, :n], kt == 0, kt == KT - 1,
                       [bw, bR1[kt]], pbuf[br])
                act(lambda: A.activation(out=sg[br][:, :n], in_=psum[br][:, :n], func=AF.Sigmoid),
                    [pbuf[br]], [b_sg[br]])
            for br, (pv, nk, o0) in enumerate(((wpg_v, 8, 0), (wps_v, 6, 8), (wpa_v, 8, 14))):
                wv_, bw = wload(pv[:, :, f * 128:(f + 1) * 128], nk)
                for kk in range(nk):
                    oap, ob_ = otile(o0 + kk, n)
                    mm(psum[3 + br][:, :n], wv_[:, kk, :], oap, kk == 0, kk == nk - 1, [bw, ob_], pbuf[3 + br])
            dve(lambda: V.tensor_tensor(out=tmp[0][:, :n], in0=psum[3][:, :n], in1=sg[0][:, :n], op=ALU.mult),
                [pbuf[3], b_sg[0]], [b_tmp[0]])
            dve(lambda: V.tensor_tensor(out=tmp[1][:, :n], in0=psum[4][:, :n], in1=sg[1][:, :n], op=ALU.mult),
                [pbuf[4], b_sg[1]], [b_tmp[1]])
            dve(lambda: V.tensor_tensor(out=tmp[0][:, :n], in0=tmp[0][:, :n], in1=tmp[1][:, :n], op=ALU.add),
                [b_tmp[0], b_tmp[1]], [b_tmp[0]])
            dve(lambda: V.tensor_tensor(out=tmp[1][:, :n], in0=psum[5][:, :n], in1=sg[2][:, :n], op=ALU.mult),
                [pbuf[5], b_sg[2]], [b_tmp[1]])
            dve(lambda: V.tensor_tensor(out=R4[:, f, :n], in0=tmp[0][:, :n], in1=tmp[1][:, :n], op=ALU.add),
                [b_tmp[0], b_tmp[1]], [bR4[f]])
        for f in range(KT):
            wv_, bw = wload(wo_v[:, :, f * 128:(f + 1) * 128], KT)
            pb = f % 2
            for kt in range(KT):
                mm(psum[pb][:, :n], wv_[:, kt, :], R4[:, kt, :n], kt == 0, kt == KT - 1, [bw, bR4[kt]], pbuf[pb])
            s = cnt["x"] % 2
            cnt["x"] += 1
            P.dma("sp", xs[s][:, :n], E.xT_v[:, f, t0:t0 + n], reads=[E.b_xT[bi]], writes=[b_xs[s]])
            act(lambda: A.mul(out=R2[:, f, :n], in_=xs[s][:, :n], mul=DN_ALPHA), [b_xs[s]], [bR2[f]])
            dve(lambda: V.scalar_tensor_tensor(out=R2[:, f, :n], in0=psum[pb][:, :n],
                                               scalar=E.mcol(l, 2, f, col), in1=R2[:, f, :n],
                                               op0=ALU.mult, op1=ALU.add),
                [pbuf[pb], bR2[f], E.b_modT], [bR2[f]])

        def post1(f):
            act(lambda: A.activation(out=R1[:, f, :n], in_=R2[:, f, :n], func=AF.Identity,
                                     scale=E.mcol(l, 4, f, col), bias=E.mcol(l, 3, f, col)),
                [bR2[f], E.b_modT], [bR1[f]])
        layer_norm(n, "ln1_w", "ln1_b", post1)
        for q in range(4):
            for j in range(11):
                jj = q * 11 + j
                pa, pbk = (0, 1) if j % 2 == 0 else (2, 3)
                wa, bwa = wload(wfi_v[:, :, jj * 128:(jj + 1) * 128], KT)
                for kt in range(KT):
                    mm(psum[pa][:, :n], wa[:, kt, :], R1[:, kt, :n], kt == 0, kt == KT - 1, [bwa, bR1[kt]], pbuf[pa])
                wb_, bwb = wload(wfi_v[:, :, DFF + jj * 128:DFF + (jj + 1) * 128], KT)
                for kt in range(KT):
                    mm(psum[pbk][:, :n], wb_[:, kt, :], R1[:, kt, :n], kt == 0, kt == KT - 1, [bwb, bR1[kt]], pbuf[pbk])
                s = j % 2
                act(lambda: A.activation(out=sg[s][:, :n], in_=psum[pa][:, :n], func=AF.Silu),
                    [pbuf[pa]], [b_sg[s]])
                dve(lambda: V.tensor_tensor(out=R3[:, j, :n], in0=psum[pbk][:, :n], in1=sg[s][:, :n], op=ALU.mult),
                    [pbuf[pbk], b_sg[s]], [bR3[j]])
            wfo_v = Wd["w_ffn_out"][l][q * 1408:(q + 1) * 1408, :].rearrange("(k p) c -> p k c", p=128)
            for f in range(KT):
                wv_, bw = wload(wfo_v[:, :, f * 128:(f + 1) * 128], 11)
                pb = 4 + f % 2
                for j in range(11):
                    mm(psum[pb][:, :n], wv_[:, j, :], R3[:, j, :n], j == 0, j == 10, [bw, bR3[j]], pbuf[pb])
                if q == 0:
                    dve(lambda: V.tensor_copy(out=R4[:, f, :n], in_=psum[pb][:, :n]),
                        [pbuf[pb]], [bR4[f]])
                else:
                    dve(lambda: V.tensor_tensor(out=R4[:, f, :n], in0=psum[pb][:, :n],
                                                in1=R4[:, f, :n].bitcast(F32), op=ALU.add),
                        [pbuf[pb], bR4[f]], [bR4[f]])
        for f in range(KT):
            act(lambda: A.mul(out=R2[:, f, :n], in_=R2[:, f, :n], mul=DN_ALPHA), [bR2[f]], [bR2[f]])
            dve(lambda: V.scalar_tensor_tensor(out=R2[:, f, :n], in0=R4[:, f, :n].bitcast(F32),
                                               scalar=E.mcol(l, 5, f, col), in1=R2[:, f, :n],
                                               op0=ALU.mult, op1=ALU.add),
                [bR4[f], bR2[f], E.b_modT], [bR2[f]])

        def post2(f):
            P.dma("sp", E.xT_v[:, f, t0:t0 + n], R2[:, f, :n], reads=[bR2[f]], writes=[E.b_xT[bi]])
        layer_norm(n, "ln2_w", "ln2_b", post2)


_CONSTS = None


def _consts():
    global _CONSTS
    if _CONSTS is None:
        c = {}
        c["ident"] = np.eye(128, dtype=np.float32)
        c["ones128"] = np.ones((128, 128), np.float32)
        pr = np.zeros((128, 128), np.float32)
        for i in range(64):
            pr[2 * i + 1, 2 * i] = -1.0
            pr[2 * i, 2 * i + 1] = 1.0
        c["prot"] = pr
        t = np.arange(T)
        c["cmask"] = np.broadcast_to((t % CH != 0).astype(np.float32), (128, T)).copy()
        jj, ii = np.meshgrid(np.arange(64), np.arange(64), indexing="ij")
        c["trif"] = (ii >= jj).astype(np.float32)
        c["trib"] = (ii <= jj).astype(np.float32)
        r = np.arange(128)
        c["rmask"] = (r[:, None] // 16 == np.arange(8)[None, :]).astype(np.float32)
        cm = (np.arange(128)[None, :] // 16 == np.arange(8)[:, None]).astype(np.float32)
        c["colmask"] = np.broadcast_to(cm.reshape(1, 8 * 128), (128, 8 * 128)).copy()
        c["iota512"] = np.broadcast_to(np.arange(512, dtype=np.float32), (128, 512)).copy()
        tl = np.arange(NLAT)
        pos = np.zeros((128, NLAT), np.float32)
        pos[:64, :] = (tl // 64)[None, :]
        pos[64:, :] = (tl % 64)[None, :]
        c["ropepos"] = pos
        d = np.arange(128)
        c["fidx"] = ((d % 64) // 2).astype(np.float32).reshape(128, 1)
        _CONSTS = c
    return _CONSTS


def make_in_maps(inputs, n_cores=8):
    c = _consts()
    f = lambda a: np.ascontiguousarray(a, dtype=np.float32)
    x = f(inputs["x"])
    ctx = f(inputs["ctx"])
    cvec = f(inputs["c"])
    c_ctx = f(inputs["c_ctx"])
    shared = {n: f(inputs[n]) for n, _ in WEIGHT_NAMES}
    shared.update(c)
    in_maps = []
    for core in range(n_cores):
        b = core % 4
        m = {"x": x[b], "ctx": ctx[b], "cc": np.stack([cvec[b], c_ctx], 0)}
        m.update(shared)
        in_maps.append(m)
    return in_maps


def kernel(**inputs):
    n_cores = 8
    nc = build_program()
    in_maps = make_in_maps(inputs, n_cores)
    res = run_bass_kernel_spmd(nc, in_maps, core_ids=list(range(n_cores)))
    out = np.stack([res.results[b]["out"] for b in range(4)], axis=0)
    return out.astype(np.float32)
```

```python
import contextlib
import math

import numpy as np
import concourse.bass as bass
import concourse.mybir as mybir
from concourse.bass_utils import run_bass_kernel_spmd

F32 = mybir.dt.float32
F32R = mybir.dt.float32r
I32 = mybir.dt.int32
AF = mybir.ActivationFunctionType
ALU = mybir.AluOpType

D = 2048
NLAT = 2048
NCTX = 256
T = NCTX + NLAT
DEPTH = 4
GLA_H, GLA_DK, GLA_DV = 4, 128, 256
GLA_TAU = 16.0
CH = 64
NCH = T // CH
S5_W, S5_NG, S5_P = 768, 48, 64
AQ_H, AKV_H, HD = 8, 2, 128
DFF = 5632
IN_W = 11536
OFF_GQ, OFF_GK, OFF_GV, OFF_GR, OFF_GLR, OFF_SU, OFF_AQ, OFF_AK, OFF_AV, OFF_BG = (
    0, 512, 1024, 2048, 3072, 3088, 3856, 4880, 5136, 5392)
DN_ALPHA = (2 * DEPTH) ** 0.25
EPS = 1e-6
KT = D // 128
NT = T // 128
TWO_PI = 2.0 * math.pi
SIN_SCALE = TWO_PI * 0.999999
TBLK = [(0, 256), (256, 512), (768, 512), (1280, 512), (1792, 512)]

EPOCH_LIM = 30000


def _tkey(tok):
    return (tok[0], tok[1]) if tok[0] == "c" else (tok[0], tok[1], tok[2])


def _tval(tok):
    return (tok[2], tok[3]) if tok[0] == "c" else tok[3]


class Buf:
    __slots__ = ("name", "lw", "rd", "multi")

    def __init__(self, name, multi=False):
        self.name = name
        self.lw = {}
        self.rd = {}
        self.multi = multi


def _merge(dct, tok):
    k = _tkey(tok)
    cur = dct.get(k)
    if cur is None or _tval(cur) < _tval(tok):
        dct[k] = tok


class Prog:
    def __init__(self, nc, es):
        self.nc = nc
        self.es = es
        self.eng = {"pe": nc.tensor, "act": nc.scalar, "dve": nc.vector,
                    "pool": nc.gpsimd, "sp": nc.sync}
        self.csem = {}
        self.cnt = {}
        self.epoch = {}
        for e in ("pe", "act", "dve", "pool"):
            self.csem[e] = [es.enter_context(nc.semaphore(f"c_{e}_0"))]
            self.cnt[e] = 0
            self.epoch[e] = 0
        self.seen = {e: {} for e in self.eng}
        self.dsem = {}
        self.dval = {}
        self.dlast = {}
        self.drr = {}
        for q, n in (("sp", 24), ("pool", 24)):
            self.dsem[q] = [es.enter_context(nc.semaphore(f"d_{q}_{i}")) for i in range(n)]
            self.dval[q] = [0] * n
            self.dlast[q] = [None] * n
            self.drr[q] = 0
        self.n_inst = 0

    def _wait(self, e, tok):
        if tok is None:
            return
        seen = self.seen[e]
        k = _tkey(tok)
        v = _tval(tok)
        if tok[0] == "c":
            if tok[1] == e and e == "pe":
                return
            if tok[1] == e and e in ("dve", "act") and (
                    tok[2] < self.epoch[e] or self.cnt[e] - tok[3] >= 2):
                return
            cur = seen.get(k)
            if cur is not None and cur >= v:
                return
            self.eng[e].wait_ge(self.csem[tok[1]][tok[2]], tok[3])
        else:
            if seen.get(k, 0) >= v:
                return
            self.eng[e].wait_ge(self.dsem[tok[1]][tok[2]], tok[3])
        seen[k] = v
        self.n_inst += 1

    def _deps(self, reads, writes):
        deps = []
        for b in reads:
            deps.extend(b.lw.values())
        for b in writes:
            if not b.multi:
                deps.extend(b.lw.values())
            deps.extend(b.rd.values())
        return deps

    def _commit(self, tok, reads, writes):
        for b in reads:
            _merge(b.rd, tok)
        for b in writes:
            if b.multi:
                _merge(b.lw, tok)
            else:
                b.lw = {_tkey(tok): tok}
                b.rd = {}

    def op(self, e, fn, reads=(), writes=()):
        for d in self._deps(reads, writes):
            self._wait(e, d)
        if self.cnt[e] >= EPOCH_LIM:
            self.epoch[e] += 1
            self.cnt[e] = 0
            self.csem[e].append(self.es.enter_context(
                self.nc.semaphore(f"c_{e}_{self.epoch[e]}")))
        ins = fn()
        self.cnt[e] += 1
        ins.then_inc(self.csem[e][self.epoch[e]], 1)
        tok = ("c", e, self.epoch[e], self.cnt[e])
        self._commit(tok, reads, writes)
        self.n_inst += 1
        return tok

    def dma(self, q, out, in_, reads=(), writes=(), **kw):
        e = q
        for d in self._deps(reads, writes):
            self._wait(e, d)
        si = self.drr[q]
        self.drr[q] = (si + 1) % len(self.dsem[q])
        self._wait(e, self.dlast[q][si])
        ins = self.eng[e].dma_start(out=out, in_=in_, **kw)
        self.dval[q][si] += 16
        ins.then_inc(self.dsem[q][si], 16)
        tok = ("d", q, si, self.dval[q][si])
        self.dlast[q][si] = tok
        self._commit(tok, reads, writes)
        self.n_inst += 1
        return tok

    def barrier(self):
        toks = []
        for e in ("pe", "act", "dve", "pool"):
            if self.cnt[e] > 0 or self.epoch[e] > 0:
                toks.append(("c", e, self.epoch[e], self.cnt[e]))
        for q in self.dsem:
            for t in self.dlast[q]:
                if t is not None:
                    toks.append(t)
        for e in self.eng:
            for t in toks:
                if t[0] == "c" and t[1] == e:
                    continue
                self._wait(e, t)


WEIGHT_NAMES = [
    ("w_ada", [DEPTH, D, 6 * D]), ("b_ada", [DEPTH, 6 * D]), ("w_in", [DEPTH, D, IN_W]),
    ("w_gla_gate", [DEPTH, 2, 16, 512]), ("b_gla_gate", [DEPTH, 2, 512]),
    ("gla_norm_w", [DEPTH, 1024]),
    ("s5_lam_re", [DEPTH, 2, 48, 64]), ("s5_lam_im", [DEPTH, 2, 48, 64]),
    ("s5_log_dt", [DEPTH, 2, 48]),
    ("s5_b_re", [DEPTH, 48, 64, 16]), ("s5_b_im", [DEPTH, 48, 64, 16]),
    ("s5_c_re", [DEPTH, 2, 48, 16, 64]), ("s5_c_im", [DEPTH, 2, 48, 16, 64]),
    ("s5_d", [DEPTH, 768]), ("w_s5_glu", [DEPTH, 768, 768]),
    ("q_norm_w", [DEPTH, 128]), ("k_norm_w", [DEPTH, 128]),
    ("w_proj_gla", [DEPTH, 1024, D]), ("w_proj_s5", [DEPTH, 768, D]),
    ("w_proj_attn", [DEPTH, 1024, D]), ("w_out", [DEPTH, D, D]),
    ("ln1_w", [DEPTH, D]), ("ln1_b", [DEPTH, D]), ("ln2_w", [DEPTH, D]), ("ln2_b", [DEPTH, D]),
    ("w_ffn_in", [DEPTH, D, 2 * DFF]), ("w_ffn_out", [DEPTH, DFF, D]),
]

CONST_SHAPES = {
    "ident": [128, 128], "ones128": [128, 128], "prot": [128, 128],
    "cmask": [128, T], "trif": [64, 64], "trib": [64, 64],
    "rmask": [128, 8], "colmask": [128, 8 * 128], "iota512": [128, 512],
    "ropepos": [128, NLAT], "fidx": [128, 1], "psw": [128, 128],
}

STAGES = ["init", "ada", "A", "gla", "s5", "att", "C"]


def build_program(debug=False, upto="all", nlayers=DEPTH):
    nc = bass.Bass("TRN2", target_bir_lowering=False)
    stage_on = {s: True for s in STAGES}
    if upto != "all":
        cut = STAGES.index(upto)
        for i, s in enumerate(STAGES):
            stage_on[s] = i <= cut

    def din(name, shape, dt=F32):
        return nc.dram_tensor(name, list(shape), dt, kind="ExternalInput").ap()

    skind = "ExternalOutput" if debug else "Internal"

    def dscr(name, shape, dt=F32):
        return nc.dram_tensor(name, list(shape), dt, kind=skind).ap()

    x_in = din("x", [NLAT, D])
    ctx_in = din("ctx", [NCTX, D])
    cc_in = din("cc", [2, D])
    Wd = {n: din(n, s) for n, s in WEIGHT_NAMES}
    Cd = {n: din(n, s) for n, s in CONST_SHAPES.items()}
    out_d = nc.dram_tensor("out", [NLAT, D], F32, kind="ExternalOutput").ap()

    xT = dscr("xT", [D, T])
    rope_d = dscr("rope_d", [2, 128, T])
    qgT = dscr("qgT", [512, T])
    kgT = dscr("kgT", [512, T])
    vg = dscr("vg", [T, 1024])
    grt = dscr("grt", [T, 1024])
    glrT = dscr("glrT", [16, T])
    suT = dscr("suT", [768, T])
    qaT = dscr("qaT", [1024, T])
    kaT = dscr("kaT", [256, T])
    va = dscr("va", [T, 256])
    oglaT = dscr("oglaT", [1024, T])
    os5T = dscr("os5T", [768, T])
    oattT = dscr("oattT", [1024, T])
    if debug:
        dbg_mod = dscr("dbg_mod", [128, DEPTH * 96 * 2])
        dbg_cs = dscr("dbg_cs", [128, T])
        dbg_y2 = dscr("dbg_y2", [128, T])
        dbg_oraw = dscr("dbg_oraw", [64, NCH * 256])
    xT_v = xT.rearrange("(ft p) t -> p ft t", p=128)

    def fv(ap_2d):
        return ap_2d.rearrange("(k p) t -> p k t", p=128)

    with contextlib.ExitStack() as es:
        P = Prog(nc, es)

        def sb(name, shape, dt=F32, scope=None):
            return (scope or es).enter_context(nc.sbuf_tensor(name, list(shape), dt))

        psum = [es.enter_context(nc.psum_tensor(f"ps{i}", [128, 512], F32)) for i in range(8)]
        pbuf = [Buf(f"ps{i}") for i in range(8)]

        b_xT = [Buf(f"xT{i}") for i in range(len(TBLK))]
        bd = {n: Buf(n, multi=True) for n in
              ["rope", "qgT", "kgT", "vg", "grt", "glrT", "suT", "qaT", "kaT", "va",
               "oglaT", "os5T", "oattT"]}

        def xT_bufs(t0, n):
            return [b_xT[i] for i, (s, m) in enumerate(TBLK) if s < t0 + n and t0 < s + m]

        V = nc.vector
        A = nc.scalar

        def mm(out, lhsT, rhs, start, stop, reads, pb):
            return P.op("pe", lambda: nc.tensor.matmul(out, lhsT=lhsT, rhs=rhs, start=start, stop=stop),
                        reads=reads, writes=[pb])

        def tr(out, in_, reads, pb):
            kk_ = in_.shape[0]
            return P.op("pe", lambda: nc.tensor.transpose(out, in_, ident[:kk_, :kk_]),
                        reads=list(reads) + [b_ident], writes=[pb])

        def dve(fn, reads, writes):
            return P.op("dve", fn, reads=reads, writes=writes)

        def act(fn, reads, writes):
            return P.op("act", fn, reads=reads, writes=writes)

        ident = sb("ident_sb", [128, 128])
        b_ident = Buf("ident")
        P.dma("sp", ident[:], Cd["ident"][:, :], writes=[b_ident])
        onesf = sb("onesf", [128, 128])
        b_onesf = Buf("onesf")
        P.dma("sp", onesf[:], Cd["ones128"][:, :], writes=[b_onesf])
        onesr = sb("onesr", [128, 128], F32R)
        b_onesr = Buf("onesr")
        P.dma("pool", onesr[:], Cd["ones128"][:, :], writes=[b_onesr])
        prot = sb("prot_sb", [128, 128], F32R)
        b_prot = Buf("prot")
        P.dma("pool", prot[:], Cd["prot"][:, :], writes=[b_prot])
        epsc = sb("epsc", [128, 1])
        b_epsc = Buf("epsc")
        dve(lambda: V.memset(epsc[:], EPS), [], [b_epsc])

        VC = {}
        ncol = 0
        for nm, n in (("ln1_w", DEPTH * 16), ("ln1_b", DEPTH * 16), ("ln2_w", DEPTH * 16),
                      ("ln2_b", DEPTH * 16), ("b_ada", DEPTH * 96), ("q_norm_w", DEPTH),
                      ("k_norm_w", DEPTH), ("b_gla_gate", DEPTH * 8), ("s5_d", DEPTH * 6)):
            VC[nm] = ncol
            ncol += n
        vecT = sb("vecT", [128, ncol])
        b_vecT = Buf("vecT")
        modT = sb("modT", [128, DEPTH, 96, 2])
        b_modT = Buf("modT")

        def vcol(nm, idx):
            c = VC[nm] + idx
            return vecT[:, c:c + 1]

        with contextlib.ExitStack() as ph:
            stg = [sb(f"pstg{i}", [128, 128], scope=ph) for i in range(2)]
            b_stg = [Buf(f"pstg{i}") for i in range(2)]
            k = 0

            def load_T(src_rows, R, c0):
                nonlocal k
                s = k % 2
                k += 1
                P.dma("sp", stg[s][:R, :], src_rows, writes=[b_stg[s]])
                tr(psum[s][:, :R], stg[s][:R, :], [b_stg[s]], pbuf[s])
                dve(lambda: V.tensor_copy(out=vecT[:, c0:c0 + R], in_=psum[s][:, :R]),
                    [pbuf[s]], [b_vecT])

            for nm in ("ln1_w", "ln1_b", "ln2_w", "ln2_b"):
                load_T(Wd[nm].rearrange("l (k p) -> (l k) p", p=128), DEPTH * 16, VC[nm])
            bav = Wd["b_ada"].rearrange("l (k p) -> (l k) p", p=128)
            for i in range(3):
                load_T(bav[i * 128:(i + 1) * 128, :], 128, VC["b_ada"] + i * 128)
            load_T(Wd["q_norm_w"], DEPTH, VC["q_norm_w"])
            load_T(Wd["k_norm_w"], DEPTH, VC["k_norm_w"])
            load_T(Wd["b_gla_gate"].rearrange("l d (k p) -> (l d k) p", p=128), DEPTH * 8,
                   VC["b_gla_gate"])
            load_T(Wd["s5_d"].rearrange("l (k p) -> (l k) p", p=128), DEPTH * 6, VC["s5_d"])
            c0 = VC["b_gla_gate"]
            dve(lambda: V.tensor_scalar(out=vecT[:, c0:c0 + DEPTH * 8], in0=vecT[:, c0:c0 + DEPTH * 8],
                                        scalar1=-1.0, scalar2=None, op0=ALU.mult, op1=ALU.bypass),
                [b_vecT], [b_vecT])

            fidx = sb("fidx_sb", [128, 1], scope=ph)
            b_fidx = Buf("fidx")
            P.dma("sp", fidx[:], Cd["fidx"][:, :], writes=[b_fidx])
            invr = sb("invr", [128, 1], scope=ph)
            b_invr = Buf("invr")
            act(lambda: A.activation(out=invr[:], in_=fidx[:], func=AF.Exp,
                                     scale=-math.log(10000.0) / 32.0), [b_fidx], [b_invr])
            dve(lambda: V.tensor_scalar(out=invr[:], in0=invr[:], scalar1=1.0 / TWO_PI, scalar2=None,
                                        op0=ALU.mult, op1=ALU.bypass), [b_invr], [b_invr])
            rp = sb("rp", [128, 512], scope=ph)
            rr = sb("rr", [128, 512], scope=ph)
            ri = sb("ri", [128, 512], I32, scope=ph)
            rf = sb("rf", [128, 512], scope=ph)
            ro = [sb(f"ro{i}", [128, 512], scope=ph) for i in range(2)]
            b_rp, b_rr, b_ri, b_rf = Buf("rp"), Buf("rr"), Buf("ri"), Buf("rf")
            b_ro = [Buf("ro0"), Buf("ro1")]

            def frac_wrap(r_t, b_r, n):
                dve(lambda: V.tensor_copy(out=ri[:, :n], in_=r_t[:, :n]), [b_r], [b_ri])
                dve(lambda: V.tensor_copy(out=rf[:, :n], in_=ri[:, :n]), [b_ri], [b_rf])
                dve(lambda: V.tensor_tensor(out=r_t[:, :n], in0=r_t[:, :n], in1=rf[:, :n],
                                            op=ALU.subtract), [b_r, b_rf], [b_r])
                wrap_half(r_t, b_r, n)

            def wrap_half(r_t, b_r, n):
                dve(lambda: V.tensor_single_scalar(out=rf[:, :n], in_=r_t[:, :n], scalar=0.5,
                                                   op=ALU.is_gt), [b_r], [b_rf])
                dve(lambda: V.tensor_tensor(out=r_t[:, :n], in0=r_t[:, :n], in1=rf[:, :n],
                                            op=ALU.subtract), [b_r, b_rf], [b_r])
                dve(lambda: V.tensor_single_scalar(out=rf[:, :n], in_=r_t[:, :n], scalar=-0.5,
                                                   op=ALU.is_lt), [b_r], [b_rf])
                dve(lambda: V.tensor_tensor(out=r_t[:, :n], in0=r_t[:, :n], in1=rf[:, :n],
                                            op=ALU.add), [b_r, b_rf], [b_r])

            if stage_on["A"]:
                dve(lambda: V.memset(ro[0][:, :NCTX], 1.0), [], [b_ro[0]])
                dve(lambda: V.memset(ro[1][:, :NCTX], 0.0), [], [b_ro[1]])
                P.dma("sp", rope_d[0, :, 0:NCTX], ro[0][:, :NCTX], reads=[b_ro[0]], writes=[bd["rope"]])
                P.dma("sp", rope_d[1, :, 0:NCTX], ro[1][:, :NCTX], reads=[b_ro[1]], writes=[bd["rope"]])
                for cch in range(4):
                    P.dma("sp", rp[:], Cd["ropepos"][:, cch * 512:(cch + 1) * 512], writes=[b_rp])
                    dve(lambda: V.tensor_scalar(out=rr[:], in0=rp[:], scalar1=invr[:, 0:1], scalar2=None,
                                                op0=ALU.mult, op1=ALU.bypass), [b_rp, b_invr], [b_rr])
                    frac_wrap(rr, b_rr, 512)
                    act(lambda: A.activation(out=ro[1][:], in_=rr[:], func=AF.Sin, scale=SIN_SCALE),
                        [b_rr], [b_ro[1]])
                    P.dma("sp", rope_d[1, :, NCTX + cch * 512:NCTX + (cch + 1) * 512], ro[1][:],
                          reads=[b_ro[1]], writes=[bd["rope"]])
                    dve(lambda: V.tensor_scalar(out=rr[:], in0=rr[:], scalar1=0.25, scalar2=None,
                                                op0=ALU.add, op1=ALU.bypass), [b_rr], [b_rr])
                    wrap_half(rr, b_rr, 512)
                    act(lambda: A.activation(out=ro[0][:], in_=rr[:], func=AF.Sin, scale=SIN_SCALE),
                        [b_rr], [b_ro[0]])
                    P.dma("sp", rope_d[0, :, NCTX + cch * 512:NCTX + (cch + 1) * 512], ro[0][:],
                          reads=[b_ro[0]], writes=[bd["rope"]])

            if stage_on["ada"]:
                cst = sb("cst", [32, 128], scope=ph)
                b_cst = Buf("cst")
                siluT = sb("siluT", [128, 16, 2], scope=ph)
                b_siluT = Buf("siluT")
                P.dma("sp", cst[:], cc_in.rearrange("j (k p) -> (j k) p", p=128), writes=[b_cst])
                tr(psum[2][:, :32], cst[:, :], [b_cst], pbuf[2])
                act(lambda: A.activation(out=siluT[:].rearrange("p k j -> p j k"),
                                         in_=psum[2][:, :32].rearrange("p (j k) -> p j k", j=2),
                                         func=AF.Silu), [pbuf[2]], [b_siluT])
                siluR = sb("siluR", [128, 16, 2], F32R, scope=ph)
                b_siluR = Buf("siluR")
                dve(lambda: V.tensor_copy(out=siluR[:], in_=siluT[:]), [b_siluT], [b_siluR])
                wad = [sb(f"wad{i}", [128, 16, 512], F32R, scope=ph) for i in range(2)]
                b_wad = [Buf(f"wad{i}") for i in range(2)]
                brow = sb("brow", [2, 6 * D], scope=ph)
                b_brow = Buf("brow")
                mrow = [sb(f"mrow{i}", [2, 512], scope=ph) for i in range(2)]
                b_mrow = [Buf(f"mrow{i}") for i in range(2)]
                kk = 0
                for l in range(nlayers):
                    wv = Wd["w_ada"][l].rearrange("(kt p) c -> p kt c", p=128)
                    P.dma("sp", brow[:], Wd["b_ada"][l:l + 1, :].to_broadcast([2, 6 * D]), writes=[b_brow])
                    for cb in range(24):
                        s = kk % 2
                        pb = 4 + kk % 2
                        kk += 1
                        P.dma("pool", wad[s][:], wv[:, :, cb * 512:(cb + 1) * 512], writes=[b_wad[s]])
                        for kt in range(KT):
                            mm(psum[pb][:2, :], siluR[:, kt, :], wad[s][:, kt, :], kt == 0, kt == KT - 1,
                               [b_wad[s], b_siluR], pbuf[pb])
                        dve(lambda: V.tensor_tensor(out=mrow[s][:], in0=psum[pb][:2, :],
                                                    in1=brow[:, cb * 512:(cb + 1) * 512], op=ALU.add),
                            [pbuf[pb], b_brow], [b_mrow[s]])
                        pt_ = 6 + kk % 2
                        for j in range(4):
                            tr(psum[pt_][:, 2 * j:2 * j + 2], mrow[s][:, j * 128:(j + 1) * 128], [b_mrow[s]], pbuf[pt_])
                        dve(lambda: V.tensor_copy(out=modT[:, l, cb * 4:(cb + 1) * 4, :].rearrange("p f j -> p (f j)"),
                                                  in_=psum[pt_][:, :8]), [pbuf[pt_]], [b_modT])
                    for c0_ in (16, 64):
                        dve(lambda: V.tensor_scalar(out=modT[:, l, c0_:c0_ + 16, :], in0=modT[:, l, c0_:c0_ + 16, :],
                                                    scalar1=1.0, scalar2=None, op0=ALU.add, op1=ALU.bypass),
                            [b_modT], [b_modT])
                if debug:
                    P.dma("sp", dbg_mod, modT[:].rearrange("p l f j -> p (l f j)"), reads=[b_modT], writes=[])
        P.barrier()

        def mcol(l, chunk, f, col):
            return modT[:, l, chunk * 16 + f, col:col + 1]

        with contextlib.ExitStack() as ph:
            xin = [sb(f"xin{i}", [128, D], scope=ph) for i in range(2)]
            b_xin = [Buf(f"xin{i}") for i in range(2)]
            stg = [sb(f"stg{i}", [128, 4, 128], scope=ph) for i in range(2)]
            b_stg = [Buf(f"stg{i}") for i in range(2)]
            k = 0
            for ti in range(NT):
                src = ctx_in[ti * 128:(ti + 1) * 128, :] if ti < 2 else \
                    x_in[(ti - 2) * 128:(ti - 1) * 128, :]
                s = ti % 2
                P.dma("sp", xin[s][:], src, writes=[b_xin[s]])
                for g4 in range(4):
                    pb = k % 2
                    for j in range(4):
                        ft = g4 * 4 + j
                        tr(psum[pb][:, j * 128:(j + 1) * 128], xin[s][:, ft * 128:(ft + 1) * 128],
                           [b_xin[s]], pbuf[pb])
                    ss = k % 2
                    dve(lambda: V.tensor_copy(out=stg[ss][:].rearrange("p a b -> p (a b)"), in_=psum[pb][:]),
                        [pbuf[pb]], [b_stg[ss]])
                    P.dma("sp", xT_v[:, g4 * 4:(g4 + 1) * 4, ti * 128:(ti + 1) * 128], stg[ss][:],
                          reads=[b_stg[ss]], writes=xT_bufs(ti * 128, 128))
                    k += 1
        P.barrier()

        for l in range(nlayers):
            if stage_on["A"]:
                with contextlib.ExitStack() as ph:
                    phase_A(nc, P, ph, l, locals())
                P.barrier()
            if stage_on["gla"]:
                with contextlib.ExitStack() as ph:
                    phase_gla(nc, P, ph, l, locals())
                P.barrier()
            if stage_on["s5"]:
                with contextlib.ExitStack() as ph:
                    phase_s5(nc, P, ph, l, locals())
                P.barrier()
            if stage_on["att"]:
                with contextlib.ExitStack() as ph:
                    phase_att(nc, P, ph, l, locals())
                P.barrier()
            if stage_on["C"]:
                with contextlib.ExitStack() as ph:
                    phase_C(nc, P, ph, l, locals())
                P.barrier()

        with contextlib.ExitStack() as ph:
            xf = [sb(f"xf{i}", [128, KT, 128], scope=ph) for i in range(2)]
            b_xf = [Buf(f"xf{i}") for i in range(2)]
            ost = [sb(f"ost{i}", [128, D], scope=ph) for i in range(2)]
            b_ost = [Buf(f"ost{i}") for i in range(2)]
            k = 0
            toks = []
            for ti in range(2, NT):
                s = ti % 2
                P.dma("sp", xf[s][:], xT_v[:, :, ti * 128:(ti + 1) * 128],
                      reads=xT_bufs(ti * 128, 128), writes=[b_xf[s]])
                for g4 in range(4):
                    pb = k % 2
                    for j in range(4):
                        ft = g4 * 4 + j
                        tr(psum[pb][:, j * 128:(j + 1) * 128], xf[s][:, ft, :], [b_xf[s]], pbuf[pb])
                    dve(lambda: V.tensor_copy(out=ost[s][:, g4 * 512:(g4 + 1) * 512], in_=psum[pb][:]),
                        [pbuf[pb]], [b_ost[s]])
                    k += 1
                toks.append(P.dma("sp", out_d[(ti - 2) * 128:(ti - 1) * 128, :], ost[s][:],
                                  reads=[b_ost[s]], writes=[]))
            for t in toks:
                P._wait("sp", t)
        print("instructions:", P.n_inst)
    return nc


class NS:
    def __init__(self, d):
        self.__dict__.update(d)


def phase_A(nc, P, ph, l, env):
    E = NS(env)
    V, A = nc.vector, nc.scalar
    sb, psum, pbuf, Wd, bd = E.sb, E.psum, E.pbuf, E.Wd, E.bd
    mm, tr, dve, act = E.mm, E.tr, E.dve, E.act
    L = f"A{l}"
    h1 = sb(L + "h1", [128, KT, 512], F32R, scope=ph)
    b_h1 = [Buf(f"h1_{i}") for i in range(KT)]
    xs = [sb(L + f"xs{i}", [128, 512], scope=ph) for i in range(3)]
    b_xs = [Buf(f"xs{i}") for i in range(3)]
    NW = 3
    wb = [sb(L + f"wb{i}", [128, 4096], F32R, scope=ph) for i in range(NW)]
    b_wb = [Buf(f"wb{i}") for i in range(NW)]
    wglr = sb(L + "wglr", [128, KT, 16], scope=ph)
    b_wglr = Buf("wglr")
    ost = [sb(L + f"ost{i}", [128, 512], scope=ph) for i in range(4)]
    b_ost = [Buf(f"ost{i}") for i in range(4)]
    cosT = sb(L + "cosT", [128, T], scope=ph)
    sinT = sb(L + "sinT", [128, T], scope=ph)
    b_rope = Buf("ropesb")
    sq = sb(L + "sq", [128, 512], F32R, scope=ph)
    b_sq = Buf("sq")
    rs = sb(L + "rs", [128, 512], scope=ph)
    b_rs = Buf("rs")
    qn = sb(L + "qn", [128, 512], F32R, scope=ph)
    b_qn = Buf("qn")
    t1 = sb(L + "t1", [128, 512], scope=ph)
    b_t1 = Buf("t1")
    t2 = sb(L + "t2", [128, 512], scope=ph)
    b_t2 = Buf("t2")
    cnt = {"w": 0, "o": 0, "x": 0, "p": 0}

    P.dma("sp", cosT[:], E.rope_d[0], reads=[bd["rope"]], writes=[b_rope])
    P.dma("sp", sinT[:], E.rope_d[1], reads=[bd["rope"]], writes=[b_rope])
    P.dma("sp", wglr[:], Wd["w_in"][l].rearrange("(kt p) c -> p kt c", p=128)[:, :, OFF_GLR:OFF_GLR + 16],
          writes=[b_wglr])
    wv = Wd["w_in"][l].rearrange("(kt p) c -> p kt c", p=128)

    def wload(c0, ncols):
        s = cnt["w"] % NW
        cnt["w"] += 1
        view = wb[s][:, :KT * ncols].rearrange("p (k c) -> p k c", c=ncols)
        P.dma("pool", view, wv[:, :, c0:c0 + ncols], writes=[b_wb[s]])
        return view, b_wb[s]

    def nps():
        p = cnt["p"] % 4
        cnt["p"] += 1
        return p

    def nost():
        s = cnt["o"] % 4
        cnt["o"] += 1
        return s

    for bi, (t0, n) in enumerate(TBLK):
        col = 1 if bi == 0 else 0
        for ft in range(KT):
            s = cnt["x"] % 3
            cnt["x"] += 1
            P.dma("sp", xs[s][:, :n], E.xT_v[:, ft, t0:t0 + n], reads=[E.b_xT[bi]], writes=[b_xs[s]])
            dve(lambda: V.tensor_scalar(out=h1[:, ft, :n], in0=xs[s][:, :n],
                                        scalar1=E.mcol(l, 1, ft, col), scalar2=E.mcol(l, 0, ft, col),
                                        op0=ALU.mult, op1=ALU.add),
                [b_xs[s], E.b_modT], [b_h1[ft]])

        def form1(c0, ntile, post):
            for cb in range(0, ntile, 2):
                nb = min(2, ntile - cb)
                view, bw = wload(c0 + cb * 128, nb * 128)
                for j in range(nb):
                    p = nps()
                    for kt in range(KT):
                        mm(psum[p][:, :n], view[:, kt, j * 128:(j + 1) * 128], h1[:, kt, :n],
                           kt == 0, kt == KT - 1, [bw, b_h1[kt]], pbuf[p])
                    post(cb + j, p)

        def store_fm(dst, name, fi, src_tile, b_src):
            P.dma("sp", dst[fi * 128:(fi + 1) * 128, t0:t0 + n], src_tile[:, :n],
                  reads=[b_src], writes=[bd[name]])

        def post_copy(dst, name, scale=None):
            def f(fi, p):
                s = nost()
                if scale is None:
                    act(lambda: A.copy(out=ost[s][:, :n], in_=psum[p][:, :n]), [pbuf[p]], [b_ost[s]])
                else:
                    act(lambda: A.mul(out=ost[s][:, :n], in_=psum[p][:, :n], mul=scale), [pbuf[p]], [b_ost[s]])
                store_fm(dst, name, fi, ost[s], b_ost[s])
            return f

        def post_qk(dst, name, wname):
            def f(fi, p):
                act(lambda: A.activation(out=sq[:, :n], in_=psum[p][:, :n], func=AF.Square),
                    [pbuf[p]], [b_sq])
                p2 = 4 + (cnt["p"] % 2)
                mm(psum[p2][:, :n], E.onesr[:], sq[:, :n], True, True, [E.b_onesr, b_sq], pbuf[p2])
                act(lambda: A.activation(out=rs[:, :n], in_=psum[p2][:, :n], func=AF.Sqrt,
                                         bias=E.epsc[:, 0:1], scale=1.0 / HD), [pbuf[p2], E.b_epsc], [b_rs])
                dve(lambda: V.reciprocal(out=rs[:, :n], in_=rs[:, :n]), [b_rs], [b_rs])
                dve(lambda: V.scalar_tensor_tensor(out=qn[:, :n], in0=psum[p][:, :n],
                                                   scalar=E.vcol(wname, l), in1=rs[:, :n],
                                                   op0=ALU.mult, op1=ALU.mult),
                    [pbuf[p], b_rs, E.b_vecT], [b_qn])
                p3 = 6 + (cnt["p"] % 2)
                mm(psum[p3][:, :n], E.prot[:], qn[:, :n], True, True, [E.b_prot, b_qn], pbuf[p3])
                dve(lambda: V.tensor_tensor(out=t1[:, :n], in0=qn[:, :n].bitcast(F32), in1=cosT[:, t0:t0 + n],
                                            op=ALU.mult), [b_qn, b_rope], [b_t1])
                dve(lambda: V.tensor_tensor(out=t2[:, :n], in0=psum[p3][:, :n], in1=sinT[:, t0:t0 + n],
                                            op=ALU.mult), [pbuf[p3], b_rope], [b_t2])
                s = nost()
                dve(lambda: V.tensor_tensor(out=ost[s][:, :n], in0=t1[:, :n], in1=t2[:, :n], op=ALU.add),
                    [b_t1, b_t2], [b_ost[s]])
                store_fm(dst, name, fi, ost[s], b_ost[s])
            return f

        form1(OFF_GQ, 4, post_copy(E.qgT, "qgT", GLA_DK ** -0.5))
        form1(OFF_GK, 4, post_copy(E.kgT, "kgT"))
        form1(OFF_SU, 6, post_copy(E.suT, "suT"))
        form1(OFF_AQ, 8, post_qk(E.qaT, "qaT", "q_norm_w"))
        form1(OFF_AK, 2, post_qk(E.kaT, "kaT", "k_norm_w"))
        p = nps()
        for kt in range(KT):
            mm(psum[p][:16, :n], wglr[:, kt, :], h1[:, kt, :n].bitcast(F32), kt == 0, kt == KT - 1,
               [b_wglr, b_h1[kt]], pbuf[p])
        s = nost()
        act(lambda: A.copy(out=ost[s][:16, :n], in_=psum[p][:16, :n]), [pbuf[p]], [b_ost[s]])
        P.dma("sp", E.glrT[:, t0:t0 + n], ost[s][:16, :n], reads=[b_ost[s]], writes=[bd["glrT"]])

        def form2(c0, ncols, dst, name, silu):
            for cb in range(0, ncols, 256):
                view, bw = wload(c0 + cb, 256)
                for tt in range(n // 128):
                    p = nps()
                    for kt in range(KT):
                        mm(psum[p][:, :256], h1[:, kt, tt * 128:(tt + 1) * 128], view[:, kt, :],
                           kt == 0, kt == KT - 1, [bw, b_h1[kt]], pbuf[p])
                    s = nost()
                    if silu:
                        act(lambda: A.activation(out=ost[s][:, :256], in_=psum[p][:, :256], func=AF.Silu),
                            [pbuf[p]], [b_ost[s]])
                    else:
                        act(lambda: A.copy(out=ost[s][:, :256], in_=psum[p][:, :256]), [pbuf[p]], [b_ost[s]])
                    r0 = t0 + tt * 128
                    P.dma("sp", dst[r0:r0 + 128, cb:cb + 256], ost[s][:, :256],
                          reads=[b_ost[s]], writes=[bd[name]])

        form2(OFF_GV, 1024, E.vg, "vg", False)
        form2(OFF_GR, 1024, E.grt, "grt", True)
        form2(OFF_AV, 256, E.va, "va", False)


def phase_gla(nc, P, ph, l, env):
    E = NS(env)
    V, A = nc.vector, nc.scalar
    sb, psum, pbuf, Wd, bd, Cd = E.sb, E.psum, E.pbuf, E.Wd, E.bd, E.Cd
    mm, tr, dve, act = E.mm, E.tr, E.dve, E.act
    L = f"G{l}"

    def t_(name, shape, dt=F32):
        return sb(L + name, shape, dt, scope=ph), Buf(name)

    cmask, b_cmask = t_("cmask", [128, T])
    trim, b_trim = t_("trim", [64, 2, 64])
    wg, b_wg = t_("wg", [16, 2, 512])
    glr, b_glr = t_("glr", [16, T])
    gnw, b_gnw = t_("gnw", [64, 1024])
    y2, b_y2 = t_("y2", [128, T])
    cs, b_cs = t_("cs", [128, T])
    eb, b_eb = t_("eb", [128, T])
    qe, b_qe = t_("qe", [128, T])
    ke, b_ke = t_("ke", [128, T])
    kd, b_kd = t_("kd", [128, T])
    v_tm, b_v = t_("v_tm", [64, NCH, 256])
    kd_tm, b_kdt = t_("kd_tm", [64, NCH, 128])
    o_acc, b_oacc = t_("o_acc", [64, NCH, 256])
    b_oc = [Buf(f"oacc{c}") for c in range(NCH)]
    S3 = [t_(f"S{i}", [128, 256]) for i in range(3)]
    attm = [t_(f"attm{i}", [64, 64]) for i in range(2)]
    grs = [t_(f"grs{i}", [64, 256]) for i in range(2)]
    on = [t_(f"on{i}", [64, 256]) for i in range(2)]
    ots = [t_(f"ots{i}", [128, 2, 64]) for i in range(2)]
    stats, b_stats = t_("stats", [64, 6])
    mv, b_mv = t_("mv", [64, 2])
    rstd, b_rstd = t_("rstd", [64, 1])

    P.dma("sp", cmask[:], Cd["cmask"][:, :], writes=[b_cmask])
    P.dma("sp", trim[:, 0, :], Cd["trif"][:, :], writes=[b_trim])
    P.dma("sp", trim[:, 1, :], Cd["trib"][:, :], writes=[b_trim])
    P.dma("sp", wg[:], Wd["w_gla_gate"][l].rearrange("d r c -> r d c"), writes=[b_wg])
    P.dma("sp", glr[:], E.glrT[:, :], reads=[bd["glrT"]], writes=[b_glr])
    P.dma("sp", gnw[:], Wd["gla_norm_w"][l:l + 1, :].to_broadcast([64, 1024]), writes=[b_gnw])
    kk = 0
    for h in range(GLA_H):
        P.dma("sp", v_tm[:], E.vg[:, h * 256:(h + 1) * 256].rearrange("(c p) f -> p c f", p=64),
              reads=[bd["vg"]], writes=[b_v])
        for d in range(2):
            P.dma("sp", qe[:], E.qgT[h * 128:(h + 1) * 128, :], reads=[bd["qgT"]], writes=[b_qe])
            P.dma("sp", ke[:], E.kgT[h * 128:(h + 1) * 128, :], reads=[bd["kgT"]], writes=[b_ke])
            negb = E.vcol("b_gla_gate", l * 8 + d * 4 + h)
            for bi, (t0, n) in enumerate(TBLK):
                pb = bi % 2
                mm(psum[pb][:, :n], wg[:, d, h * 128:(h + 1) * 128], glr[:, t0:t0 + n], True, True,
                   [b_wg, b_glr], pbuf[pb])
                act(lambda: A.activation(out=y2[:, t0:t0 + n], in_=psum[pb][:, :n], func=AF.Exp,
                                         scale=-1.0, bias=negb), [pbuf[pb], E.b_vecT], [b_y2])
            act(lambda: A.activation(out=y2[:], in_=y2[:], func=AF.Ln, bias=1.0, scale=1.0), [b_y2], [b_y2])
            if d == 0:
                dve(lambda: V.tensor_tensor_scan(out=cs[:], data0=cmask[:], data1=y2[:], initial=0.0,
                                                 op0=ALU.mult, op1=ALU.add), [b_cmask, b_y2], [b_cs])
            else:
                dve(lambda: V.tensor_tensor_scan(out=cs[:, ::-1], data0=cmask[:], data1=y2[:, ::-1],
                                                 initial=0.0, op0=ALU.mult, op1=ALU.add),
                    [b_cmask, b_y2], [b_cs])
            if E.debug and h == 0 and d == 0 and l == 0:
                P.dma("sp", E.dbg_y2, y2[:], reads=[b_y2], writes=[])
                P.dma("sp", E.dbg_cs, cs[:], reads=[b_cs], writes=[])
            act(lambda: A.activation(out=eb[:], in_=cs[:], func=AF.Exp, scale=-1.0 / GLA_TAU), [b_cs], [b_eb])
            act(lambda: A.activation(out=y2[:], in_=cs[:], func=AF.Exp, scale=1.0 / GLA_TAU), [b_cs], [b_y2])
            dve(lambda: V.tensor_tensor(out=qe[:], in0=qe[:], in1=eb[:], op=ALU.mult), [b_qe, b_eb], [b_qe])
            dve(lambda: V.tensor_tensor(out=ke[:], in0=ke[:], in1=y2[:], op=ALU.mult), [b_ke, b_y2], [b_ke])
            lpos = CH - 1 if d == 0 else 0
            eb3 = eb[:].rearrange("p (c i) -> p c i", i=CH)
            dve(lambda: V.tensor_tensor(out=kd[:].rearrange("p (c i) -> p c i", i=CH),
                                        in0=ke[:].rearrange("p (c i) -> p c i", i=CH),
                                        in1=eb3[:, :, lpos:lpos + 1].to_broadcast([128, NCH, CH]),
                                        op=ALU.mult), [b_ke, b_eb], [b_kd])
            for c4 in range(NCH // 4):
                pb = 2 + c4 % 2
                for j in range(4):
                    c = c4 * 4 + j
                    tr(psum[pb][:64, j * 128:(j + 1) * 128], kd[:, c * CH:(c + 1) * CH], [b_kd], pbuf[pb])
                act(lambda: A.copy(out=kd_tm[:, c4 * 4:(c4 + 1) * 4, :].rearrange("p a b -> p (a b)"),
                                   in_=psum[pb][:64, :]), [pbuf[pb]], [b_kdt])
            if E.debug and h == 0 and l == 0 and d == 1:
                P.dma("sp", E.dbg_oraw, o_acc[:].rearrange("p a b -> p (a b)"), reads=b_oc, writes=[])
            dve(lambda: V.memset(S3[0][0][:], 0.0), [], [S3[0][1]])
            order = list(range(NCH)) if d == 0 else [3, 2, 1, 0] + list(range(NCH - 1, 3, -1))

            def stageA(c, i):
                sl = slice(c * CH, (c + 1) * CH)
                mm(psum[i % 2][:64, :64], ke[:, sl], qe[:, sl], True, True, [b_ke, b_qe], pbuf[i % 2])
                mm(psum[2 + i % 2][:, :256], kd_tm[:, c, :], v_tm[:, c, :], True, True, [b_kdt, b_v], pbuf[2 + i % 2])

            def stageB(c, i):
                sl = slice(c * CH, (c + 1) * CH)
                am, b_am = attm[i % 2]
                Sc, b_Sc = S3[i % 3]
                Sn, b_Sn = S3[(i + 1) % 3]
                po = 4 + i % 2
                dve(lambda: V.tensor_tensor(out=am[:], in0=psum[i % 2][:64, :64], in1=trim[:, d, :], op=ALU.mult),
                    [pbuf[i % 2], b_trim], [b_am])
                mm(psum[po][:64, :256], am[:], v_tm[:, c, :], True, False, [b_am, b_v], pbuf[po])
                mm(psum[po][:64, :256], qe[:, sl], Sc[:], False, True, [b_qe, b_Sc], pbuf[po])
                li = c * CH + lpos
                dve(lambda: V.scalar_tensor_tensor(out=Sn[:], in0=Sc[:], scalar=eb[:, li:li + 1],
                                                   in1=psum[2 + i % 2][:, :256], op0=ALU.mult, op1=ALU.add),
                    [b_Sc, b_eb, pbuf[2 + i % 2]], [b_Sn])
                if d == 0:
                    act(lambda: A.copy(out=o_acc[:, c, :], in_=psum[po][:64, :256]), [pbuf[po]], [b_oc[c]])
                else:
                    dve(lambda: V.tensor_tensor(out=o_acc[:, c, :], in0=psum[po][:64, :256], in1=o_acc[:, c, :],
                                                op=ALU.add), [pbuf[po], b_oc[c]], [b_oc[c]])

            stageA(order[0], 0)
            for i in range(NCH):
                if i + 1 < NCH:
                    stageA(order[i + 1], i + 1)
                stageB(order[i], i)
        for c in range(NCH):
            s = c % 2
            ont, b_on = on[s]
            grt_, b_gr = grs[s]
            ott, b_ot = ots[s]
            P.dma("sp", grt_[:], E.grt[c * CH:(c + 1) * CH, h * 256:(h + 1) * 256], reads=[bd["grt"]],
                  writes=[b_gr])
            dve(lambda: V.bn_stats(out=stats[:], in_=o_acc[:, c, :]), [b_oc[c]], [b_stats])
            dve(lambda: V.bn_aggr(out=mv[:], in_=stats[:]), [b_stats], [b_mv])
            act(lambda: A.activation(out=rstd[:], in_=mv[:, 1:2], func=AF.Sqrt, bias=E.epsc[:64, 0:1], scale=1.0),
                [b_mv, E.b_epsc], [b_rstd])
            dve(lambda: V.reciprocal(out=rstd[:], in_=rstd[:]), [b_rstd], [b_rstd])
            dve(lambda: V.tensor_scalar(out=ont[:], in0=o_acc[:, c, :], scalar1=mv[:, 0:1], scalar2=rstd[:, 0:1],
                                        op0=ALU.subtract, op1=ALU.mult), [b_oc[c], b_mv, b_rstd], [b_on])
            dve(lambda: V.tensor_tensor(out=ont[:], in0=ont[:], in1=gnw[:, h * 256:(h + 1) * 256], op=ALU.mult),
                [b_on, b_gnw], [b_on])
            dve(lambda: V.tensor_tensor(out=ont[:], in0=ont[:], in1=grt_[:], op=ALU.mult), [b_on, b_gr], [b_on])
            for half in range(2):
                tr(psum[7][:, half * 64:(half + 1) * 64], ont[:, half * 128:(half + 1) * 128], [b_on], pbuf[7])
            act(lambda: A.copy(out=ott[:].rearrange("p a b -> p (a b)"), in_=psum[7][:, :128]), [pbuf[7]], [b_ot])
            P.dma("sp", E.oglaT[h * 256:(h + 1) * 256, c * CH:(c + 1) * CH].rearrange("(a p) t -> p a t", p=128),
                  ott[:], reads=[b_ot], writes=[bd["oglaT"]])


def phase_s5(nc, P, ph, l, env):
    E = NS(env)
    V, A = nc.vector, nc.scalar
    sb, psum, pbuf, Wd, bd, Cd = E.sb, E.psum, E.pbuf, E.Wd, E.bd, E.Cd
    mm, tr, dve, act = E.mm, E.tr, E.dve, E.act
    L = f"S{l}"
    NG = S5_NG

    def t_(name, shape, dt=F32):
        return sb(L + name, shape, dt, scope=ph), Buf(name)

    def ts(out, in0, s1, s2, op0, op1, rd, wr):
        dve(lambda: V.tensor_scalar(out=out, in0=in0, scalar1=s1, scalar2=s2, op0=op0, op1=op1), rd, wr)

    def tt(out, in0, in1, op, rd, wr):
        dve(lambda: V.tensor_tensor(out=out, in0=in0, in1=in1, op=op), rd, wr)

    ri, b_ri = t_("ri", [128, 512], I32)
    rf, b_rf = t_("rf", [128, 512])

    def wrap_half(r, b_r):
        dve(lambda: V.tensor_single_scalar(out=rf[:, :r.shape[1]], in_=r, scalar=0.5, op=ALU.is_gt), [b_r], [b_rf])
        tt(r, r, rf[:, :r.shape[1]], ALU.subtract, [b_r, b_rf], [b_r])

    def frac_wrap(r, b_r):
        n = r.shape[1]
        dve(lambda: V.tensor_copy(out=ri[:, :n], in_=r), [b_r], [b_ri])
        dve(lambda: V.tensor_copy(out=rf[:, :n], in_=ri[:, :n]), [b_ri], [b_rf])
        tt(r, r, rf[:, :n], ALU.subtract, [b_r, b_rf], [b_r])
        wrap_half(r, b_r)

    def sincos(r, b_r, sin_out, b_s, cos_out, b_c):
        frac_wrap(r, b_r)
        act(lambda: A.activation(out=sin_out, in_=r, func=AF.Sin, scale=SIN_SCALE), [b_r], [b_s])
        ts(r, r, 0.25, None, ALU.add, ALU.bypass, [b_r], [b_r])
        wrap_half(r, b_r)
        act(lambda: A.activation(out=cos_out, in_=r, func=AF.Sin, scale=SIN_SCALE), [b_r], [b_c])

    iota, b_iota = t_("iota", [128, 512])
    P.dma("sp", iota[:], Cd["iota512"][:, :], writes=[b_iota])
    rmask, b_rmask = t_("rmask", [128, 8])
    P.dma("sp", rmask[:], Cd["rmask"][:, :], writes=[b_rmask])
    colmask, b_colmask = t_("colmask", [128, 8, 128])
    P.dma("sp", colmask[:].rearrange("p a b -> p (a b)"), Cd["colmask"][:, :], writes=[b_colmask])
    pst, b_pst = t_("pst", [128, 128])
    par = {}
    NOFF = [256, 512]
    for d in range(2):
        pp = {}
        for nm in ("lamr", "lami", "dt", "rho", "thp", "sn", "cs", "kre", "kim", "w1", "w2", "w3"):
            pp[nm] = t_(f"{nm}{d}", [128, NG])
        pp["Ec"] = t_(f"Ec{d}", [128, 2, NG])
        pp["Es"] = t_(f"Es{d}", [128, 2, NG])
        par[d] = pp
        for nm, src in (("lamr", "s5_lam_re"), ("lami", "s5_lam_im")):
            P.dma("sp", pst[:NG, 0:64], Wd[src][l, d], writes=[b_pst])
            P.dma("sp", pst[:NG, 64:128], Wd[src][l, d], writes=[b_pst])
            tr(psum[0][:, :NG], pst[:NG, :], [b_pst], pbuf[0])
            dve(lambda: V.tensor_copy(out=pp[nm][0][:], in_=psum[0][:, :NG]), [pbuf[0]], [pp[nm][1]])
        dtt, b_dt = pp["dt"]
        P.dma("sp", dtt[:], Wd["s5_log_dt"][l, d:d + 1, :].to_broadcast([128, NG]), writes=[b_dt])
        act(lambda: A.activation(out=dtt[:], in_=dtt[:], func=AF.Exp), [b_dt], [b_dt])
        lamr, b_lamr = pp["lamr"]
        lami, b_lami = pp["lami"]
        rho, b_rho = pp["rho"]
        thp, b_thp = pp["thp"]
        sn, b_sn = pp["sn"]
        cs_, b_cs = pp["cs"]
        kre, b_kre = pp["kre"]
        kim, b_kim = pp["kim"]
        w1, b_w1 = pp["w1"]
        w2, b_w2 = pp["w2"]
        w3, b_w3 = pp["w3"]
        tt(w1[:], lamr[:], dtt[:], ALU.mult, [b_lamr, b_dt], [b_w1])
        act(lambda: A.activation(out=rho[:], in_=w1[:], func=AF.Exp), [b_w1], [b_rho])
        tt(thp[:], lami[:], dtt[:], ALU.mult, [b_lami, b_dt], [b_thp])
        ts(thp[:], thp[:], 1.0 / TWO_PI, None, ALU.mult, ALU.bypass, [b_thp], [b_thp])
        dve(lambda: V.tensor_copy(out=w1[:], in_=thp[:]), [b_thp], [b_w1])
        sincos(w1[:], b_w1, sn[:], b_sn, cs_[:], b_cs)
        tt(w1[:], rho[:], cs_[:], ALU.mult, [b_rho, b_cs], [b_w1])
        ts(w1[:], w1[:], -1.0, None, ALU.add, ALU.bypass, [b_w1], [b_w1])
        tt(w2[:], rho[:], sn[:], ALU.mult, [b_rho, b_sn], [b_w2])
        tt(w3[:], lamr[:], lamr[:], ALU.mult, [b_lamr], [b_w3])
        tt(kre[:], lami[:], lami[:], ALU.mult, [b_lami], [b_kre])
        tt(w3[:], w3[:], kre[:], ALU.add, [b_w3, b_kre], [b_w3])
        dve(lambda: V.reciprocal(out=w3[:], in_=w3[:]), [b_w3], [b_w3])
        tt(kre[:], w1[:], lamr[:], ALU.mult, [b_w1, b_lamr], [b_kre])
        tt(kim[:], w2[:], lami[:], ALU.mult, [b_w2, b_lami], [b_kim])
        tt(kre[:], kre[:], kim[:], ALU.add, [b_kre, b_kim], [b_kre])
        tt(kre[:], kre[:], w3[:], ALU.mult, [b_kre, b_w3], [b_kre])
        tt(kim[:], w2[:], lamr[:], ALU.mult, [b_w2, b_lamr], [b_kim])
        tt(w2[:], w1[:], lami[:], ALU.mult, [b_w1, b_lami], [b_w2])
        tt(kim[:], kim[:], w2[:], ALU.subtract, [b_kim, b_w2], [b_kim])
        tt(kim[:], kim[:], w3[:], ALU.mult, [b_kim, b_w3], [b_kim])
        Ec, b_Ec = pp["Ec"]
        Es, b_Es = pp["Es"]
        for k, noff in enumerate(NOFF):
            ts(w1[:], thp[:], float(noff), None, ALU.mult, ALU.bypass, [b_thp], [b_w1])
            sincos(w1[:], b_w1, Es[:, k, :], b_Es, Ec[:, k, :], b_Ec)

    bre, b_bre = t_("bre", [128, NG, 16])
    bim, b_bim = t_("bim", [128, NG, 16])
    for (tile_, b_, src) in ((bre, b_bre, "s5_b_re"), (bim, b_bim, "s5_b_im")):
        v_ = Wd[src][l].rearrange("g p c -> p g c")
        P.dma("sp", tile_[0:64], v_, writes=[b_])
        P.dma("sp", tile_[64:128], v_, writes=[b_])
    Pre, b_Pre = t_("Pre", [128, 8, 16])
    Pim, b_Pim = t_("Pim", [128, 8, 16])
    pw, b_pw = t_("pw", [128, 8, 16])
    X1, b_X1 = t_("X1", [128, 128])
    X2, b_X2 = t_("X2", [128, 128])
    TB = [t_(f"TB{i}", [128, 128]) for i in range(2)]
    TW = [t_(f"TW{i}", [128, 128]) for i in range(2)]
    cst, b_cst = t_("cstg", [128, 2, 128])
    LB = [[t_(f"LB{g}_{i}", [128, 128], F32R) for i in range(2)] for g in range(8)]
    WG = [[t_(f"WG{g}_{i}", [128, 128], F32R) for i in range(2)] for g in range(8)]
    cosm = [t_(f"cosm{g}", [128, 512]) for g in range(8)]
    sinm = [t_(f"sinm{g}", [128, 512]) for g in range(8)]
    NSET = 3
    SETS = []
    for i in range(NSET):
        SETS.append({"m1": t_(f"qm1{i}", [128, 512]), "vv": t_(f"qvv{i}", [128, 512]),
                     "P1": t_(f"qP1{i}", [128, 512], F32R), "P2": t_(f"qP2{i}", [128, 512], F32R)})
    tq, b_tq = t_("tq", [128, 512])
    st8, b_st8 = t_("st8", [128, 8])
    b_st = [Buf(f"st{g}") for g in range(8)]
    psw, b_psw = t_("psw", [128, 128])
    P.dma("sp", psw[:], Cd["psw"][:, :], writes=[b_psw])
    init8, b_init8 = t_("init8", [128, 8])
    t8, b_t8 = t_("t8", [128, 8])

    suo, b_suo = t_("suo", [128, T], F32R)
    yacc, b_yacc = t_("yacc", [128, T])
    yg = [t_(f"yg{i}", [128, T], F32R) for i in range(6)]

    TILES = {0: [(0, 256), (256, 512), (768, 512), (1280, 512), (1792, 512)],
             1: [(0, 256), (1792, 512), (1280, 512), (768, 512), (256, 512)]}
    GROUPS3 = [(0, 1, 2), (3, 4, 5), (6, 7)]
    setctr = [0]

    for o in range(6):
        P.dma("pool", suo[:], E.suT[o * 128:(o + 1) * 128, :], reads=[bd["suT"]], writes=[b_suo])
        P.dma("sp", yacc[:], E.suT[o * 128:(o + 1) * 128, :], reads=[bd["suT"]], writes=[b_yacc])
        ts(yacc[:], yacc[:], E.vcol("s5_d", l * 6 + o), None, ALU.mult, ALU.bypass, [b_yacc, E.b_vecT], [b_yacc])
        for d in range(2):
            pp = par[d]
            kre, b_kre = pp["kre"]
            kim, b_kim = pp["kim"]
            gs = slice(o * 8, (o + 1) * 8)
            kreb = kre[:, gs].unsqueeze(2).to_broadcast([128, 8, 16])
            kimb = kim[:, gs].unsqueeze(2).to_broadcast([128, 8, 16])
            tt(Pre[:], bre[:, gs, :], kreb, ALU.mult, [b_bre, b_kre], [b_Pre])
            tt(pw[:], bim[:, gs, :], kimb, ALU.mult, [b_bim, b_kim], [b_pw])
            tt(Pre[:], Pre[:], pw[:], ALU.subtract, [b_Pre, b_pw], [b_Pre])
            tt(Pim[:], bim[:, gs, :], kreb, ALU.mult, [b_bim, b_kre], [b_Pim])
            tt(pw[:], bre[:, gs, :], kimb, ALU.mult, [b_bre, b_kim], [b_pw])
            tt(Pim[:], Pim[:], pw[:], ALU.add, [b_Pim, b_pw], [b_Pim])
            Pre2 = Pre[:].rearrange("p g c -> p (g c)")
            Pim2 = Pim[:].rearrange("p g c -> p (g c)")
            dve(lambda: V.tensor_copy(out=X1[0:64, :], in_=Pre2[0:64, :]), [b_Pre], [b_X1])
            dve(lambda: V.tensor_copy(out=X1[64:128, :], in_=Pim2[64:128, :]), [b_Pim], [b_X1])
            dve(lambda: V.tensor_copy(out=X2[0:64, :], in_=Pim2[0:64, :]), [b_Pim], [b_X2])
            ts(X2[64:128, :], Pre2[64:128, :], -1.0, None, ALU.mult, ALU.bypass, [b_Pre], [b_X2])
            for i, (X_, b_X) in enumerate(((X1, b_X1), (X2, b_X2))):
                tr(psum[i][:, :128], X_[:, :], [b_X], pbuf[i])
                dve(lambda: V.tensor_copy(out=TB[i][0][:], in_=psum[i][:, :128]), [pbuf[i]], [TB[i][1]])
            crv = Wd["s5_c_re"][l, d].rearrange("g c p -> (g c) p")[o * 128:(o + 1) * 128, :]
            civ = Wd["s5_c_im"][l, d].rearrange("g c p -> (g c) p")[o * 128:(o + 1) * 128, :]
            P.dma("sp", cst[:, 0, 0:64], crv, writes=[b_cst])
            P.dma("sp", cst[:, 0, 64:128], civ, writes=[b_cst])
            P.dma("sp", cst[:, 1, 0:64], civ, writes=[b_cst])
            P.dma("sp", cst[:, 1, 64:128], crv, writes=[b_cst])
            for i in range(2):
                tr(psum[2 + i][:, :128], cst[:, i, :], [b_cst], pbuf[2 + i])
            dve(lambda: V.tensor_copy(out=TW[0][0][0:64, :], in_=psum[2][0:64, :128]), [pbuf[2]], [TW[0][1]])
            ts(TW[0][0][64:128, :], psum[2][64:128, :128], -1.0, None, ALU.mult, ALU.bypass, [pbuf[2]], [TW[0][1]])
            ts(TW[1][0][:], psum[3][:, :128], -1.0, None, ALU.mult, ALU.bypass, [pbuf[3]], [TW[1][1]])
            rho, b_rho = pp["rho"]
            thp, b_thp = pp["thp"]
            Ec, b_Ec = pp["Ec"]
            Es, b_Es = pp["Es"]
            for gl in range(8):
                g = o * 8 + gl
                for i in range(2):
                    ts(LB[gl][i][0][:], TB[i][0][:], rmask[:, gl:gl + 1], None, ALU.mult, ALU.bypass,
                       [TB[i][1], b_rmask], [LB[gl][i][1]])
                    tt(WG[gl][i][0][:], TW[i][0][:], colmask[:, gl, :], ALU.mult, [TW[i][1], b_colmask],
                       [WG[gl][i][1]])
                ts(tq[:], iota[:], thp[:, g:g + 1], None, ALU.mult, ALU.bypass, [b_iota, b_thp], [b_tq])
                sincos(tq[:], b_tq, sinm[gl][0][:], sinm[gl][1], cosm[gl][0][:], cosm[gl][1])
            rv = (lambda ap: ap[:, ::-1]) if d == 1 else (lambda ap: ap)
            for k, (f0, n) in enumerate(TILES[d]):
                py = 6 + k % 2
                nmm = [0]
                if k >= 1:
                    pq = 6 + (k + 1) % 2
                    ei = 0 if k == 1 else 1
                    gs8 = slice(o * 8, (o + 1) * 8)
                    mm(psum[pq][:, :8], psw[:], st8[:, :], True, True, [b_psw] + b_st, pbuf[pq])
                    tt(t8[:], psum[pq][:, :8], Es[:, ei, gs8], ALU.mult, [pbuf[pq], b_Es], [b_t8])
                    tt(init8[:], st8[:], Ec[:, ei, gs8], ALU.mult, b_st + [b_Ec], [b_init8])
                    tt(init8[:], init8[:], t8[:], ALU.add, [b_init8, b_t8], [b_init8])
                for grp in GROUPS3:
                    ctx_ = {}
                    for gl in grp:
                        S_ = SETS[setctr[0] % NSET]
                        setctr[0] += 1
                        ctx_[gl] = S_
                    for gl in grp:
                        pa = (gl % 3) * 2
                        mm(psum[pa][:, :n], LB[gl][0][0][:], suo[:, f0:f0 + n], True, True,
                           [LB[gl][0][1], b_suo], pbuf[pa])
                        mm(psum[pa + 1][:, :n], LB[gl][1][0][:], suo[:, f0:f0 + n], True, True,
                           [LB[gl][1][1], b_suo], pbuf[pa + 1])
                    tabs = {gl: (cosm[gl], sinm[gl]) for gl in grp}
                    for gl in grp:
                        pa = (gl % 3) * 2
                        m1_, b_m1_ = ctx_[gl]["m1"]
                        tt(m1_[:, :n], psum[pa][:, :n], rv(tabs[gl][0][0][:, :n]), ALU.mult,
                           [pbuf[pa], tabs[gl][0][1]], [b_m1_])
                    for gl in grp:
                        pa = (gl % 3) * 2
                        vv_, b_vv_ = ctx_[gl]["vv"]
                        tt(vv_[:, :n], psum[pa + 1][:, :n], rv(tabs[gl][1][0][:, :n]), ALU.mult,
                           [pbuf[pa + 1], tabs[gl][1][1]], [b_vv_])
                    for gl in grp:
                        m1_, b_m1_ = ctx_[gl]["m1"]
                        vv_, b_vv_ = ctx_[gl]["vv"]
                        P.op("pool", lambda: nc.gpsimd.tensor_tensor(out=vv_[:, :n], in0=vv_[:, :n], in1=m1_[:, :n],
                                                                     op=ALU.add),
                             reads=[b_vv_, b_m1_], writes=[b_vv_])
                    for gl in grp:
                        g = o * 8 + gl
                        m1_, b_m1_ = ctx_[gl]["m1"]
                        vv_, b_vv_ = ctx_[gl]["vv"]
                        init = 0.0 if k == 0 else init8[:, gl:gl + 1]
                        dve(lambda: V.tensor_tensor_scan(out=rv(m1_[:, :n]),
                                                         data0=rho[:, g:g + 1].to_broadcast([128, n]),
                                                         data1=rv(vv_[:, :n]), initial=init,
                                                         op0=ALU.mult, op1=ALU.add),
                            [b_rho, b_vv_, b_init8], [b_m1_])
                    lastc = n - 1 if d == 0 else 0
                    for gl in grp:
                        m1_, b_m1_ = ctx_[gl]["m1"]
                        act(lambda: A.copy(out=st8[:, gl:gl + 1], in_=m1_[:, lastc:lastc + 1]), [b_m1_], [b_st[gl]])
                    for gl in grp:
                        m1_, b_m1_ = ctx_[gl]["m1"]
                        P1_, b_P1_ = ctx_[gl]["P1"]
                        tt(P1_[:, :n], m1_[:, :n], rv(tabs[gl][0][0][:, :n]), ALU.mult, [b_m1_, tabs[gl][0][1]], [b_P1_])
                    for gl in grp:
                        m1_, b_m1_ = ctx_[gl]["m1"]
                        P2_, b_P2_ = ctx_[gl]["P2"]
                        P.op("pool", lambda: nc.gpsimd.tensor_tensor(out=P2_[:, :n], in0=m1_[:, :n],
                                                                     in1=rv(tabs[gl][1][0][:, :n]), op=ALU.mult),
                             reads=[b_m1_, tabs[gl][1][1]], writes=[b_P2_])
                    for gl in grp:
                        P1_, b_P1_ = ctx_[gl]["P1"]
                        P2_, b_P2_ = ctx_[gl]["P2"]
                        mm(psum[py][:, :n], WG[gl][0][0][:], P1_[:, :n], nmm[0] == 0, False, [WG[gl][0][1], b_P1_], pbuf[py])
                        nmm[0] += 1
                        mm(psum[py][:, :n], WG[gl][1][0][:], P2_[:, :n], False, nmm[0] == 15, [WG[gl][1][1], b_P2_], pbuf[py])
                        nmm[0] += 1
                tt(yacc[:, f0:f0 + n], psum[py][:, :n], yacc[:, f0:f0 + n], ALU.add, [pbuf[py], b_yacc], [b_yacc])
        m1, b_m1 = SETS[0]["m1"]
        vv, b_vv = SETS[0]["vv"]
        ygt, b_yg = yg[o]
        for (t0, n) in TBLK:
            xa = yacc[:, t0:t0 + n]
            act(lambda: A.activation(out=m1[:, :n], in_=xa, func=AF.Square), [b_yacc], [b_m1])
            ts(m1[:, :n], m1[:, :n], 0.044715, 1.0, ALU.mult, ALU.add, [b_m1], [b_m1])
            tt(m1[:, :n], m1[:, :n], xa, ALU.mult, [b_m1, b_yacc], [b_m1])
            act(lambda: A.activation(out=vv[:, :n], in_=m1[:, :n], func=AF.Sigmoid,
                                     scale=2.0 * math.sqrt(2.0 / math.pi)), [b_m1], [b_vv])
            tt(ygt[:, t0:t0 + n], xa, vv[:, :n], ALU.mult, [b_yacc, b_vv], [b_yg])
    wgl, b_wgl = t_("wgl", [128, 6, 128], F32R)
    wv = Wd["w_s5_glu"][l].rearrange("(k p) c -> p k c", p=128)
    kk = 0
    for fo in range(6):
        P.dma("pool", wgl[:], wv[:, :, fo * 128:(fo + 1) * 128], writes=[b_wgl])
        for (t0, n) in TBLK:
            pb = kk % 2
            kk += 1
            for k in range(6):
                mm(psum[pb][:, :n], wgl[:, k, :], yg[k][0][:, t0:t0 + n], k == 0, k == 5,
                   [b_wgl, yg[k][1]], pbuf[pb])
            sgt, b_sgt = SETS[0]["m1"] if pb == 0 else SETS[0]["vv"]
            ott, b_ott = SETS[1]["m1"] if pb == 0 else SETS[1]["vv"]
            act(lambda: A.activation(out=sgt[:, :n], in_=psum[pb][:, :n], func=AF.Sigmoid), [pbuf[pb]], [b_sgt])
            tt(ott[:, :n], yg[fo][0][:, t0:t0 + n].bitcast(F32), sgt[:, :n], ALU.mult, [yg[fo][1], b_sgt], [b_ott])
            P.dma("sp", E.os5T[fo * 128:(fo + 1) * 128, t0:t0 + n], ott[:, :n], reads=[b_ott], writes=[bd["os5T"]])


def phase_att(nc, P, ph, l, env):
    E = NS(env)
    V, A = nc.vector, nc.scalar
    sb, psum, pbuf, bd = E.sb, E.psum, E.pbuf, E.bd
    mm, dve, act = E.mm, E.dve, E.act
    L = f"T{l}"
    kT = sb(L + "kT", [128, T], F32R, scope=ph)
    b_kT = Buf("kT")
    vv = sb(L + "vv", [128, NT, 128], F32R, scope=ph)
    b_vv = Buf("vv")
    qb = [sb(L + f"qb{i}", [128, 512], F32R, scope=ph) for i in range(2)]
    b_qb = [Buf(f"qb{i}") for i in range(2)]
    pt = [sb(L + f"pt{i}", [128, 512], F32R, scope=ph) for i in range(3)]
    b_pt = [Buf(f"pt{i}") for i in range(3)]
    rden = sb(L + "rden", [128, 512], scope=ph)
    b_rden = Buf("rden")
    ob = [sb(L + f"ob{i}", [128, 512], scope=ph) for i in range(2)]
    b_ob = [Buf(f"ob{i}") for i in range(2)]
    k = 0
    for g in range(AKV_H):
        P.dma("pool", kT[:], E.kaT[g * 128:(g + 1) * 128, :], reads=[bd["kaT"]], writes=[b_kT])
        P.dma("pool", vv[:], E.va[:, g * 128:(g + 1) * 128].rearrange("(j p) d -> p j d", p=128),
              reads=[bd["va"]], writes=[b_vv])
        for hh in range(4):
            h = g * 4 + hh
            for (t0, n) in TBLK:
                key_tiles = [0, 1] if t0 == 0 else list(range(NT))
                s = k % 2
                po, pd = 4 + 2 * s, 5 + 2 * s
                k += 1
                P.dma("pool", qb[s][:, :n], E.qaT[h * 128:(h + 1) * 128, t0:t0 + n],
                      reads=[bd["qaT"]], writes=[b_qb[s]])
                for idx, j in enumerate(key_tiles):
                    ps = idx % 3
                    last = idx == len(key_tiles) - 1
                    mm(psum[ps][:, :n], kT[:, j * 128:(j + 1) * 128], qb[s][:, :n], True, True,
                       [b_kT, b_qb[s]], pbuf[ps])
                    act(lambda: A.activation(out=pt[ps][:, :n], in_=psum[ps][:, :n], func=AF.Exp,
                                             scale=HD ** -0.5), [pbuf[ps]], [b_pt[ps]])
                    mm(psum[po][:, :n], vv[:, j, :], pt[ps][:, :n], idx == 0, last,
                       [b_vv, b_pt[ps]], pbuf[po])
                    mm(psum[pd][:, :n], E.onesr[:], pt[ps][:, :n], idx == 0, last,
                       [E.b_onesr, b_pt[ps]], pbuf[pd])
                dve(lambda: V.reciprocal(out=rden[:, :n], in_=psum[pd][:, :n]), [pbuf[pd]], [b_rden])
                dve(lambda: V.tensor_tensor(out=ob[s][:, :n], in0=psum[po][:, :n], in1=rden[:, :n],
                                            op=ALU.mult), [pbuf[po], b_rden], [b_ob[s]])
                P.dma("sp", E.oattT[h * 128:(h + 1) * 128, t0:t0 + n], ob[s][:, :n],
                      reads=[b_ob[s]], writes=[bd["oattT"]])


def phase_C(nc, P, ph, l, env):
    E = NS(env)
    V, A = nc.vector, nc.scalar
    sb, psum, pbuf, Wd, bd = E.sb, E.psum, E.pbuf, E.Wd, E.bd
    mm, dve, act = E.mm, E.dve, E.act
    L = f"C{l}"
    R1 = sb(L + "R1", [128, KT, 512], F32R, scope=ph)
    R2 = sb(L + "R2", [128, KT, 512], F32, scope=ph)
    R3 = sb(L + "R3", [128, 11, 512], F32R, scope=ph)
    R4 = sb(L + "R4", [128, KT, 512], F32R, scope=ph)
    O2 = sb(L + "O2", [128, 11, 512], F32R, scope=ph)
    bO2 = [Buf(f"O2_{i}") for i in range(11)]
    bR1 = [Buf(f"R1_{i}") for i in range(KT)]
    bR2 = [Buf(f"R2_{i}") for i in range(KT)]
    bR3 = [Buf(f"R3_{i}") for i in range(11)]
    bR4 = [Buf(f"R4_{i}") for i in range(KT)]
    NW = 4
    wb = [sb(L + f"wb{i}", [128, 2048], F32R, scope=ph) for i in range(NW)]
    b_wb = [Buf(f"wb{i}") for i in range(NW)]
    SC = [sb(L + f"sc{i}", [128, 512], scope=ph) for i in range(7)]
    bSC = [Buf(f"sc{i}") for i in range(7)]
    xs, b_xs = SC[0:2], bSC[0:2]
    sg, b_sg = SC[0:3], bSC[0:3]
    tmp, b_tmp = SC[3:5], bSC[3:5]
    sqs, b_sqs = SC[0:2], bSC[0:2]
    mean, b_mean = SC[2], bSC[2]
    rstd, b_rstd = SC[5], bSC[5]
    msq, b_msq = SC[6], bSC[6]
    cnt = {"w": 0, "x": 0, "t": 0, "q": 0}
    w_in_v = Wd["w_in"][l].rearrange("(kt p) c -> p kt c", p=128)
    wpg_v = Wd["w_proj_gla"][l].rearrange("(k p) c -> p k c", p=128)
    wps_v = Wd["w_proj_s5"][l].rearrange("(k p) c -> p k c", p=128)
    wpa_v = Wd["w_proj_attn"][l].rearrange("(k p) c -> p k c", p=128)
    wo_v = Wd["w_out"][l].rearrange("(k p) c -> p k c", p=128)
    wfi_v = Wd["w_ffn_in"][l].rearrange("(k p) c -> p k c", p=128)

    def wload(src3, nk):
        s = cnt["w"] % NW
        cnt["w"] += 1
        view = wb[s][:, :nk * 128].rearrange("p (k c) -> p k c", c=128)
        P.dma("pool", view, src3, writes=[b_wb[s]])
        return view, b_wb[s]

    def otile(i, n):
        if i < 11:
            return R3[:, i, :n], bR3[i]
        return O2[:, i - 11, :n], bO2[i - 11]

    def layer_norm(n, wname, bname, post):
        for f in range(KT):
            s = f % 2
            act(lambda: A.activation(out=sqs[s][:, :n], in_=R2[:, f, :n], func=AF.Square),
                [bR2[f]], [b_sqs[s]])
            mm(psum[6][:, :n], E.onesf[:], R2[:, f, :n], f == 0, f == KT - 1, [E.b_onesf, bR2[f]], pbuf[6])
            mm(psum[7][:, :n], E.onesf[:], sqs[s][:, :n], f == 0, f == KT - 1, [E.b_onesf, b_sqs[s]], pbuf[7])
        act(lambda: A.mul(out=mean[:, :n], in_=psum[6][:, :n], mul=1.0 / D), [pbuf[6]], [b_mean])
        dve(lambda: V.tensor_tensor(out=msq[:, :n], in0=mean[:, :n], in1=mean[:, :n], op=ALU.mult),
            [b_mean], [b_msq])
        dve(lambda: V.scalar_tensor_tensor(out=msq[:, :n], in0=psum[7][:, :n], scalar=1.0 / D,
                                           in1=msq[:, :n], op0=ALU.mult, op1=ALU.subtract),
            [pbuf[7], b_msq], [b_msq])
        act(lambda: A.activation(out=rstd[:, :n], in_=msq[:, :n], func=AF.Sqrt, bias=E.epsc[:, 0:1],
                                 scale=1.0), [b_msq, E.b_epsc], [b_rstd])
        dve(lambda: V.reciprocal(out=rstd[:, :n], in_=rstd[:, :n]), [b_rstd], [b_rstd])
        for f in range(KT):
            s = cnt["t"] % 2
            cnt["t"] += 1
            dve(lambda: V.tensor_tensor(out=tmp[s][:, :n], in0=R2[:, f, :n], in1=mean[:, :n],
                                        op=ALU.subtract), [bR2[f], b_mean], [b_tmp[s]])
            dve(lambda: V.tensor_tensor(out=tmp[s][:, :n], in0=tmp[s][:, :n], in1=rstd[:, :n],
                                        op=ALU.mult), [b_tmp[s], b_rstd], [b_tmp[s]])
            act(lambda: A.activation(out=R2[:, f, :n], in_=tmp[s][:, :n], func=AF.Identity,
                                     scale=E.vcol(wname, l * 16 + f), bias=E.vcol(bname, l * 16 + f)),
                [b_tmp[s], E.b_vecT], [bR2[f]])
            post(f)

    for bi, (t0, n) in enumerate(TBLK):
        col = 1 if bi == 0 else 0
        for ft in range(KT):
            s = cnt["x"] % 2
            cnt["x"] += 1
            P.dma("sp", xs[s][:, :n], E.xT_v[:, ft, t0:t0 + n], reads=[E.b_xT[bi]], writes=[b_xs[s]])
            dve(lambda: V.tensor_scalar(out=R1[:, ft, :n], in0=xs[s][:, :n],
                                        scalar1=E.mcol(l, 1, ft, col), scalar2=E.mcol(l, 0, ft, col),
                                        op0=ALU.mult, op1=ALU.add),
                [b_xs[s], E.b_modT], [bR1[ft]])
        P.dma("pool", R3[:, 0:8, :n], E.fv(E.oglaT)[:, :, t0:t0 + n], reads=[bd["oglaT"]], writes=bR3[0:8])
        P.dma("pool", R3[:, 8:11, :n], E.fv(E.os5T)[:, 0:3, t0:t0 + n], reads=[bd["os5T"]], writes=bR3[8:11])
        P.dma("pool", O2[:, 0:3, :n], E.fv(E.os5T)[:, 3:6, t0:t0 + n],
              reads=[bd["os5T"]], writes=bO2[0:3])
        P.dma("pool", O2[:, 3:11, :n], E.fv(E.oattT)[:, :, t0:t0 + n],
              reads=[bd["oattT"]], writes=bO2[3:11])
        for f in range(KT):
            for br in range(3):
                wv_, bw = wload(w_in_v[:, :, OFF_BG + br * D + f * 128:OFF_BG + br * D + (f + 1) * 128], KT)
                for kt in range(KT):
                    mm(psum[br][:, :n], wv_[:, kt, :], R1[:, kt, :n], kt == 0, kt == KT - 1,
                       [bw, bR1[kt]], pbuf[br])
                act(lambda: A.activation(out=sg[br][:, :n], in_=psum[br][:, :n], func=AF.Sigmoid),
                    [pbuf[br]], [b_sg[br]])
            for br, (pv, nk, o0) in enumerate(((wpg_v, 8, 0), (wps_v, 6, 8), (wpa_v, 8, 14))):
                wv_, bw = wload(pv[:, :, f * 128:(f + 1) * 128], nk)
                for kk in range(nk):
                    oap, ob_ = otile(o0 + kk, n)
                    mm(psum[3 + br][:, :n], wv_[:, kk, :], oap, kk == 0, kk == nk - 1, [bw, ob_], pbuf[3 + br])
            dve(lambda: V.tensor_tensor(out=tmp[0][:, :n], in0=psum[3][:, :n], in1=sg[0][:, :n], op=ALU.mult),
                [pbuf[3], b_sg[0]], [b_tmp[0]])
            dve(lambda: V.tensor_tensor(out=tmp[1][:, :n], in0=psum[4][:, :n], in1=sg[1][:, :n], op=ALU.mult),
                [pbuf[4], b_sg[1]], [b_tmp[1]])
            dve(lambda: V.tensor_tensor(out=tmp[0][:, :n], in0=tmp[0][:, :n], in1=tmp[1][:, :n], op=ALU.add),
                [b_tmp[0], b_tmp[1]], [b_tmp[0]])
            dve(lambda: V.tensor_tensor(out=tmp[1][:, :n], in0=psum[5][:, :n], in1=sg[2][:, :n], op=ALU.mult),
                [pbuf[5], b_sg[2]], [b_tmp[1]])
            dve(lambda: V.tensor_tensor(out=R4[:, f, :n], in0=tmp[0][:, :n], in1=tmp[1][:, :n], op=ALU.add),
                [b_tmp[0], b_tmp[1]], [bR4[f]])
        for f in range(KT):
            wv_, bw = wload(wo_v[:, :, f * 128:(f + 1) * 128], KT)
            pb = f % 2
            for kt in range(KT):
                mm(psum[pb][:, :n], wv_[:, kt, :], R4[:, kt, :n], kt == 0, kt == KT - 1, [bw, bR4[kt]], pbuf[pb])
            s = cnt["x"] % 2
            cnt["x"] += 1
            P.dma("sp", xs[s][:, :n], E.xT_v[:, f, t0:t0 + n], reads=[E.b_xT[bi]], writes=[b_xs[s]])
            act(lambda: A.mul(out=R2[:, f, :n], in_=xs[s][:, :n], mul=DN_ALPHA), [b_xs[s]], [bR2[f]])
            dve(lambda: V.scalar_tensor_tensor(out=R2[:, f, :n], in0=psum[pb][:, :n],
                                               scalar=E.mcol(l, 2, f, col), in1=R2[:, f, :n],
                                               op0=ALU.mult, op1=ALU.add),
                [pbuf[pb], bR2[f], E.b_modT], [bR2[f]])

        def post1(f):
            act(lambda: A.activation(out=R1[:, f, :n], in_=R2[:, f, :n], func=AF.Identity,
                                     scale=E.mcol(l, 4, f, col), bias=E.mcol(l, 3, f, col)),
                [bR2[f], E.b_modT], [bR1[f]])
        layer_norm(n, "ln1_w", "ln1_b", post1)
        for q in range(4):
            for j in range(11):
                jj = q * 11 + j
                pa, pbk = (0, 1) if j % 2 == 0 else (2, 3)
                wa, bwa = wload(wfi_v[:, :, jj * 128:(jj + 1) * 128], KT)
                for kt in range(KT):
                    mm(psum[pa][:, :n], wa[:, kt, :], R1[:, kt, :n], kt == 0, kt == KT - 1, [bwa, bR1[kt]], pbuf[pa])
                wb_, bwb = wload(wfi_v[:, :, DFF + jj * 128:DFF + (jj + 1) * 128], KT)
                for kt in range(KT):
                    mm(psum[pbk][:, :n], wb_[:, kt, :], R1[:, kt, :n], kt == 0, kt == KT - 1, [bwb, bR1[kt]], pbuf[pbk])
                s = j % 2
                act(lambda: A.activation(out=sg[s][:, :n], in_=psum[pa][:, :n], func=AF.Silu),
                    [pbuf[pa]], [b_sg[s]])
                dve(lambda: V.tensor_tensor(out=R3[:, j, :n], in0=psum[pbk][:, :n], in1=sg[s][:, :n], op=ALU.mult),
                    [pbuf[pbk], b_sg[s]], [bR3[j]])
            wfo_v = Wd["w_ffn_out"][l][q * 1408:(q + 1) * 1408, :].rearrange("(k p) c -> p k c", p=128)
            for f in range(KT):
                wv_, bw = wload(wfo_v[:, :, f * 128:(f + 1) * 128], 11)
                pb = 4 + f % 2
                for j in range(11):
                    mm(psum[pb][:, :n], wv_[:, j, :], R3[:, j, :n], j == 0, j == 10, [bw, bR3[j]], pbuf[pb])
                if q == 0:
                    dve(lambda: V.tensor_copy(out=R4[:, f, :n], in_=psum[pb][:, :n]),
                        [pbuf[pb]], [bR4[f]])
                else:
                    dve(lambda: V.tensor_tensor(out=R4[:, f, :n], in0=psum[pb][:, :n],
                                                in1=R4[:, f, :n].bitcast(F32), op=ALU.add),
                        [pbuf[pb], bR4[f]], [bR4[f]])
        for f in range(KT):
            act(lambda: A.mul(out=R2[:, f, :n], in_=R2[:, f, :n], mul=DN_ALPHA), [bR2[f]], [bR2[f]])
            dve(lambda: V.scalar_tensor_tensor(out=R2[:, f, :n], in0=R4[:, f, :n].bitcast(F32),
                                               scalar=E.mcol(l, 5, f, col), in1=R2[:, f, :n],
                                               op0=ALU.mult, op1=ALU.add),
                [bR4[f], bR2[f], E.b_modT], [bR2[f]])

        def post2(f):
            P.dma("sp", E.xT_v[:, f, t0:t0 + n], R2[:, f, :n], reads=[bR2[f]], writes=[E.b_xT[bi]])
        layer_norm(n, "ln2_w", "ln2_b", post2)


_CONSTS = None


def _consts():
    global _CONSTS
    if _CONSTS is None:
        c = {}
        c["ident"] = np.eye(128, dtype=np.float32)
        c["ones128"] = np.ones((128, 128), np.float32)
        pr = np.zeros((128, 128), np.float32)
        for i in range(64):
            pr[2 * i + 1, 2 * i] = -1.0
            pr[2 * i, 2 * i + 1] = 1.0
        c["prot"] = pr
        t = np.arange(T)
        c["cmask"] = np.broadcast_to((t % CH != 0).astype(np.float32), (128, T)).copy()
        jj, ii = np.meshgrid(np.arange(64), np.arange(64), indexing="ij")
        c["trif"] = (ii >= jj).astype(np.float32)
        c["trib"] = (ii <= jj).astype(np.float32)
        r = np.arange(128)
        c["rmask"] = (r[:, None] // 16 == np.arange(8)[None, :]).astype(np.float32)
        cm = (np.arange(128)[None, :] // 16 == np.arange(8)[:, None]).astype(np.float32)
        c["colmask"] = np.broadcast_to(cm.reshape(1, 8 * 128), (128, 8 * 128)).copy()
        c["iota512"] = np.broadcast_to(np.arange(512, dtype=np.float32), (128, 512)).copy()
        tl = np.arange(NLAT)
        pos = np.zeros((128, NLAT), np.float32)
        pos[:64, :] = (tl // 64)[None, :]
        pos[64:, :] = (tl % 64)[None, :]
        c["ropepos"] = pos
        d = np.arange(128)
        c["fidx"] = ((d % 64) // 2).astype(np.float32).reshape(128, 1)
        ps_ = np.zeros((128, 128), np.float32)
        for m in range(64):
            ps_[m + 64, m] = -1.0
            ps_[m, m + 64] = 1.0
        c["psw"] = ps_
        _CONSTS = c
    return _CONSTS


def make_in_maps(inputs, n_cores=8):
    c = _consts()
    f = lambda a: np.ascontiguousarray(a, dtype=np.float32)
    x = f(inputs["x"])
    ctx = f(inputs["ctx"])
    cvec = f(inputs["c"])
    c_ctx = f(inputs["c_ctx"])
    shared = {n: f(inputs[n]) for n, _ in WEIGHT_NAMES}
    shared.update(c)
    in_maps = []
    for core in range(n_cores):
        b = core % 4
        m = {"x": x[b], "ctx": ctx[b], "cc": np.stack([cvec[b], c_ctx], 0)}
        m.update(shared)
        in_maps.append(m)
    return in_maps


def kernel(**inputs):
    n_cores = 8
    nc = build_program()
    in_maps = make_in_maps(inputs, n_cores)
    res = run_bass_kernel_spmd(nc, in_maps, core_ids=list(range(n_cores)))
    out = np.stack([res.results[b]["out"] for b in range(4)], axis=0)
    return out.astype(np.float32)
```

```python
import contextlib
import math

import numpy as np
import concourse.bass as bass
import concourse.mybir as mybir
from concourse.bass_utils import run_bass_kernel_spmd

F32 = mybir.dt.float32
F32R = mybir.dt.float32r
I32 = mybir.dt.int32
AF = mybir.ActivationFunctionType
ALU = mybir.AluOpType

D = 2048
NLAT = 2048
NCTX = 256
T = NCTX + NLAT
DEPTH = 4
GLA_H, GLA_DK, GLA_DV = 4, 128, 256
GLA_TAU = 16.0
CH = 64
NCH = T // CH
S5_W, S5_NG, S5_P = 768, 48, 64
AQ_H, AKV_H, HD = 8, 2, 128
DFF = 5632
IN_W = 11536
OFF_GQ, OFF_GK, OFF_GV, OFF_GR, OFF_GLR, OFF_SU, OFF_AQ, OFF_AK, OFF_AV, OFF_BG = (
    0, 512, 1024, 2048, 3072, 3088, 3856, 4880, 5136, 5392)
DN_ALPHA = (2 * DEPTH) ** 0.25
EPS = 1e-6
KT = D // 128
NT = T // 128
TWO_PI = 2.0 * math.pi
SIN_SCALE = TWO_PI * 0.999999
TBLK = [(0, 256), (256, 512), (768, 512), (1280, 512), (1792, 512)]

EPOCH_LIM = 30000


def _tkey(tok):
    return (tok[0], tok[1]) if tok[0] == "c" else (tok[0], tok[1], tok[2])


def _tval(tok):
    return (tok[2], tok[3]) if tok[0] == "c" else tok[3]


class Buf:
    __slots__ = ("name", "lw", "rd", "multi")

    def __init__(self, name, multi=False):
        self.name = name
        self.lw = {}
        self.rd = {}
        self.multi = multi


def _merge(dct, tok):
    k = _tkey(tok)
    cur = dct.get(k)
    if cur is None or _tval(cur) < _tval(tok):
        dct[k] = tok


class Prog:
    def __init__(self, nc, es):
        self.nc = nc
        self.es = es
        self.eng = {"pe": nc.tensor, "act": nc.scalar, "dve": nc.vector,
                    "pool": nc.gpsimd, "sp": nc.sync}
        self.csem = {}
        self.cnt = {}
        self.epoch = {}
        for e in ("pe", "act", "dve", "pool"):
            self.csem[e] = [es.enter_context(nc.semaphore(f"c_{e}_0"))]
            self.cnt[e] = 0
            self.epoch[e] = 0
        self.seen = {e: {} for e in self.eng}
        self.dsem = {}
        self.dval = {}
        self.dlast = {}
        self.drr = {}
        for q, n in (("sp", 24), ("pool", 24)):
            self.dsem[q] = [es.enter_context(nc.semaphore(f"d_{q}_{i}")) for i in range(n)]
            self.dval[q] = [0] * n
            self.dlast[q] = [None] * n
            self.drr[q] = 0
        self.n_inst = 0

    def _wait(self, e, tok):
        if tok is None:
            return
        seen = self.seen[e]
        k = _tkey(tok)
        v = _tval(tok)
        if tok[0] == "c":
            if tok[1] == e and e == "pe":
                return
            if tok[1] == e and e in ("dve", "act") and (
                    tok[2] < self.epoch[e] or self.cnt[e] - tok[3] >= 2):
                return
            cur = seen.get(k)
            if cur is not None and cur >= v:
                return
            self.eng[e].wait_ge(self.csem[tok[1]][tok[2]], tok[3])
        else:
            if seen.get(k, 0) >= v:
                return
            self.eng[e].wait_ge(self.dsem[tok[1]][tok[2]], tok[3])
        seen[k] = v
        self.n_inst += 1

    def _deps(self, reads, writes):
        deps = []
        for b in reads:
            deps.extend(b.lw.values())
        for b in writes:
            if not b.multi:
                deps.extend(b.lw.values())
            deps.extend(b.rd.values())
        return deps

    def _commit(self, tok, reads, writes):
        for b in reads:
            _merge(b.rd, tok)
        for b in writes:
            if b.multi:
                _merge(b.lw, tok)
            else:
                b.lw = {_tkey(tok): tok}
                b.rd = {}

    def op(self, e, fn, reads=(), writes=()):
        for d in self._deps(reads, writes):
            self._wait(e, d)
        if self.cnt[e] >= EPOCH_LIM:
            self.epoch[e] += 1
            self.cnt[e] = 0
            self.csem[e].append(self.es.enter_context(
                self.nc.semaphore(f"c_{e}_{self.epoch[e]}")))
        ins = fn()
        self.cnt[e] += 1
        ins.then_inc(self.csem[e][self.epoch[e]], 1)
        tok = ("c", e, self.epoch[e], self.cnt[e])
        self._commit(tok, reads, writes)
        self.n_inst += 1
        return tok

    def dma(self, q, out, in_, reads=(), writes=(), **kw):
        e = q
        for d in self._deps(reads, writes):
            self._wait(e, d)
        si = self.drr[q]
        self.drr[q] = (si + 1) % len(self.dsem[q])
        self._wait(e, self.dlast[q][si])
        ins = self.eng[e].dma_start(out=out, in_=in_, **kw)
        self.dval[q][si] += 16
        ins.then_inc(self.dsem[q][si], 16)
        tok = ("d", q, si, self.dval[q][si])
        self.dlast[q][si] = tok
        self._commit(tok, reads, writes)
        self.n_inst += 1
        return tok

    def barrier(self):
        toks = []
        for e in ("pe", "act", "dve", "pool"):
            if self.cnt[e] > 0 or self.epoch[e] > 0:
                toks.append(("c", e, self.epoch[e], self.cnt[e]))
        for q in self.dsem:
            for t in self.dlast[q]:
                if t is not None:
                    toks.append(t)
        for e in self.eng:
            for t in toks:
                if t[0] == "c" and t[1] == e:
                    continue
                self._wait(e, t)


WEIGHT_NAMES = [
    ("w_ada", [DEPTH, D, 6 * D]), ("b_ada", [DEPTH, 6 * D]),
    ("w_gla_gate", [DEPTH, 2, 16, 512]), ("b_gla_gate", [DEPTH, 2, 512]),
    ("gla_norm_w", [DEPTH, 1024]),
    ("s5_lam_re", [DEPTH, 2, 48, 64]), ("s5_lam_im", [DEPTH, 2, 48, 64]),
    ("s5_log_dt", [DEPTH, 2, 48]),
    ("s5_b_re", [DEPTH, 48, 64, 16]), ("s5_b_im", [DEPTH, 48, 64, 16]),
    ("s5_c_re", [DEPTH, 2, 48, 16, 64]), ("s5_c_im", [DEPTH, 2, 48, 16, 64]),
    ("s5_d", [DEPTH, 768]), ("w_s5_glu", [DEPTH, 768, 768]),
    ("q_norm_w", [DEPTH, 128]), ("k_norm_w", [DEPTH, 128]),
    ("ln1_w", [DEPTH, D]), ("ln1_b", [DEPTH, D]), ("ln2_w", [DEPTH, D]), ("ln2_b", [DEPTH, D]),
]
TILED_NAMES = [
    ("w_inA", [DEPTH, D, OFF_BG]), ("bg_t", [DEPTH, 48, 128, 16, 128]),
    ("wpg_t", [DEPTH, 16, 128, 8, 128]), ("wps_t", [DEPTH, 16, 128, 6, 128]),
    ("wpa_t", [DEPTH, 16, 128, 8, 128]), ("wo_t", [DEPTH, 16, 128, 16, 128]),
    ("wfi_t", [DEPTH, 88, 128, 16, 128]), ("wfo_t", [DEPTH, 4, 16, 128, 11, 128]),
]

CONST_SHAPES = {
    "ident": [128, 128], "ones128": [128, 128], "prot": [128, 128],
    "cmask": [128, T], "trif": [64, 64], "trib": [64, 64],
    "rmask": [128, 8], "colmask": [128, 8 * 128], "iota512": [128, 512],
    "ropepos": [128, NLAT], "fidx": [128, 1], "psw": [128, 128],
}

STAGES = ["init", "ada", "A", "gla", "s5", "att", "C"]


def build_program(debug=False, upto="all", nlayers=DEPTH):
    nc = bass.Bass("TRN2", target_bir_lowering=False)
    stage_on = {s: True for s in STAGES}
    if upto != "all":
        cut = STAGES.index(upto)
        for i, s in enumerate(STAGES):
            stage_on[s] = i <= cut

    def din(name, shape, dt=F32):
        return nc.dram_tensor(name, list(shape), dt, kind="ExternalInput").ap()

    skind = "ExternalOutput" if debug else "Internal"

    def dscr(name, shape, dt=F32):
        return nc.dram_tensor(name, list(shape), dt, kind=skind).ap()

    x_in = din("x", [NLAT, D])
    ctx_in = din("ctx", [NCTX, D])
    cc_in = din("cc", [2, D])
    Wd = {n: din(n, s) for n, s in WEIGHT_NAMES + TILED_NAMES}
    Cd = {n: din(n, s) for n, s in CONST_SHAPES.items()}
    out_d = nc.dram_tensor("out", [NLAT, D], F32, kind="ExternalOutput").ap()

    xT = dscr("xT", [D, T])
    rope_d = dscr("rope_d", [2, 128, T])
    qgT = dscr("qgT", [512, T])
    kgT = dscr("kgT", [512, T])
    vg = dscr("vg", [T, 1024])
    grt = dscr("grt", [T, 1024])
    glrT = dscr("glrT", [16, T])
    suT = dscr("suT", [768, T])
    qaT = dscr("qaT", [1024, T])
    kaT = dscr("kaT", [256, T])
    va = dscr("va", [T, 256])
    oglaT = dscr("oglaT", [1024, T])
    os5T = dscr("os5T", [768, T])
    oattT = dscr("oattT", [1024, T])
    if debug:
        dbg_mod = dscr("dbg_mod", [128, DEPTH * 96 * 2])
        dbg_cs = dscr("dbg_cs", [128, T])
        dbg_y2 = dscr("dbg_y2", [128, T])
        dbg_oraw = dscr("dbg_oraw", [64, NCH * 256])
    xT_v = xT.rearrange("(ft p) t -> p ft t", p=128)

    def fv(ap_2d):
        return ap_2d.rearrange("(k p) t -> p k t", p=128)

    with contextlib.ExitStack() as es:
        P = Prog(nc, es)

        def sb(name, shape, dt=F32, scope=None):
            return (scope or es).enter_context(nc.sbuf_tensor(name, list(shape), dt))

        psum = [es.enter_context(nc.psum_tensor(f"ps{i}", [128, 512], F32)) for i in range(8)]
        pbuf = [Buf(f"ps{i}") for i in range(8)]

        b_xT = [Buf(f"xT{i}") for i in range(len(TBLK))]
        bd = {n: Buf(n, multi=True) for n in
              ["rope", "qgT", "kgT", "vg", "grt", "glrT", "suT", "qaT", "kaT", "va",
               "oglaT", "os5T", "oattT"]}

        def xT_bufs(t0, n):
            return [b_xT[i] for i, (s, m) in enumerate(TBLK) if s < t0 + n and t0 < s + m]

        V = nc.vector
        A = nc.scalar

        def mm(out, lhsT, rhs, start, stop, reads, pb):
            return P.op("pe", lambda: nc.tensor.matmul(out, lhsT=lhsT, rhs=rhs, start=start, stop=stop),
                        reads=reads, writes=[pb])

        def tr(out, in_, reads, pb):
            kk_ = in_.shape[0]
            return P.op("pe", lambda: nc.tensor.transpose(out, in_, ident[:kk_, :kk_]),
                        reads=list(reads) + [b_ident], writes=[pb])

        def dve(fn, reads, writes):
            return P.op("dve", fn, reads=reads, writes=writes)

        def act(fn, reads, writes):
            return P.op("act", fn, reads=reads, writes=writes)

        ident = sb("ident_sb", [128, 128])
        b_ident = Buf("ident")
        P.dma("sp", ident[:], Cd["ident"][:, :], writes=[b_ident])
        onesf = sb("onesf", [128, 128])
        b_onesf = Buf("onesf")
        P.dma("sp", onesf[:], Cd["ones128"][:, :], writes=[b_onesf])
        onesr = sb("onesr", [128, 128], F32R)
        b_onesr = Buf("onesr")
        P.dma("pool", onesr[:], Cd["ones128"][:, :], writes=[b_onesr])
        prot = sb("prot_sb", [128, 128], F32R)
        b_prot = Buf("prot")
        P.dma("pool", prot[:], Cd["prot"][:, :], writes=[b_prot])
        epsc = sb("epsc", [128, 1])
        b_epsc = Buf("epsc")
        dve(lambda: V.memset(epsc[:], EPS), [], [b_epsc])

        VC = {}
        ncol = 0
        for nm, n in (("ln1_w", DEPTH * 16), ("ln1_b", DEPTH * 16), ("ln2_w", DEPTH * 16),
                      ("ln2_b", DEPTH * 16), ("b_ada", DEPTH * 96), ("q_norm_w", DEPTH),
                      ("k_norm_w", DEPTH), ("b_gla_gate", DEPTH * 8), ("s5_d", DEPTH * 6)):
            VC[nm] = ncol
            ncol += n
        vecT = sb("vecT", [128, ncol])
        b_vecT = Buf("vecT")
        modT = sb("modT", [128, DEPTH, 96, 2])
        b_modT = Buf("modT")

        def vcol(nm, idx):
            c = VC[nm] + idx
            return vecT[:, c:c + 1]

        with contextlib.ExitStack() as ph:
            stg = [sb(f"pstg{i}", [128, 128], scope=ph) for i in range(2)]
            b_stg = [Buf(f"pstg{i}") for i in range(2)]
            k = 0

            def load_T(src_rows, R, c0):
                nonlocal k
                s = k % 2
                k += 1
                P.dma("sp", stg[s][:R, :], src_rows, writes=[b_stg[s]])
                tr(psum[s][:, :R], stg[s][:R, :], [b_stg[s]], pbuf[s])
                dve(lambda: V.tensor_copy(out=vecT[:, c0:c0 + R], in_=psum[s][:, :R]),
                    [pbuf[s]], [b_vecT])

            for nm in ("ln1_w", "ln1_b", "ln2_w", "ln2_b"):
                load_T(Wd[nm].rearrange("l (k p) -> (l k) p", p=128), DEPTH * 16, VC[nm])
            bav = Wd["b_ada"].rearrange("l (k p) -> (l k) p", p=128)
            for i in range(3):
                load_T(bav[i * 128:(i + 1) * 128, :], 128, VC["b_ada"] + i * 128)
            load_T(Wd["q_norm_w"], DEPTH, VC["q_norm_w"])
            load_T(Wd["k_norm_w"], DEPTH, VC["k_norm_w"])
            load_T(Wd["b_gla_gate"].rearrange("l d (k p) -> (l d k) p", p=128), DEPTH * 8,
                   VC["b_gla_gate"])
            load_T(Wd["s5_d"].rearrange("l (k p) -> (l k) p", p=128), DEPTH * 6, VC["s5_d"])
            c0 = VC["b_gla_gate"]
            dve(lambda: V.tensor_scalar(out=vecT[:, c0:c0 + DEPTH * 8], in0=vecT[:, c0:c0 + DEPTH * 8],
                                        scalar1=-1.0, scalar2=None, op0=ALU.mult, op1=ALU.bypass),
                [b_vecT], [b_vecT])

            fidx = sb("fidx_sb", [128, 1], scope=ph)
            b_fidx = Buf("fidx")
            P.dma("sp", fidx[:], Cd["fidx"][:, :], writes=[b_fidx])
            invr = sb("invr", [128, 1], scope=ph)
            b_invr = Buf("invr")
            act(lambda: A.activation(out=invr[:], in_=fidx[:], func=AF.Exp,
                                     scale=-math.log(10000.0) / 32.0), [b_fidx], [b_invr])
            dve(lambda: V.tensor_scalar(out=invr[:], in0=invr[:], scalar1=1.0 / TWO_PI, scalar2=None,
                                        op0=ALU.mult, op1=ALU.bypass), [b_invr], [b_invr])
            rp = sb("rp", [128, 512], scope=ph)
            rr = sb("rr", [128, 512], scope=ph)
            ri = sb("ri", [128, 512], I32, scope=ph)
            rf = sb("rf", [128, 512], scope=ph)
            ro = [sb(f"ro{i}", [128, 512], scope=ph) for i in range(2)]
            b_rp, b_rr, b_ri, b_rf = Buf("rp"), Buf("rr"), Buf("ri"), Buf("rf")
            b_ro = [Buf("ro0"), Buf("ro1")]

            def frac_wrap(r_t, b_r, n):
                dve(lambda: V.tensor_copy(out=ri[:, :n], in_=r_t[:, :n]), [b_r], [b_ri])
                dve(lambda: V.tensor_copy(out=rf[:, :n], in_=ri[:, :n]), [b_ri], [b_rf])
                dve(lambda: V.tensor_tensor(out=r_t[:, :n], in0=r_t[:, :n], in1=rf[:, :n],
                                            op=ALU.subtract), [b_r, b_rf], [b_r])
                wrap_half(r_t, b_r, n)

            def wrap_half(r_t, b_r, n):
                dve(lambda: V.tensor_single_scalar(out=rf[:, :n], in_=r_t[:, :n], scalar=0.5,
                                                   op=ALU.is_gt), [b_r], [b_rf])
                dve(lambda: V.tensor_tensor(out=r_t[:, :n], in0=r_t[:, :n], in1=rf[:, :n],
                                            op=ALU.subtract), [b_r, b_rf], [b_r])
                dve(lambda: V.tensor_single_scalar(out=rf[:, :n], in_=r_t[:, :n], scalar=-0.5,
                                                   op=ALU.is_lt), [b_r], [b_rf])
                dve(lambda: V.tensor_tensor(out=r_t[:, :n], in0=r_t[:, :n], in1=rf[:, :n],
                                            op=ALU.add), [b_r, b_rf], [b_r])

            if stage_on["A"]:
                dve(lambda: V.memset(ro[0][:, :NCTX], 1.0), [], [b_ro[0]])
                dve(lambda: V.memset(ro[1][:, :NCTX], 0.0), [], [b_ro[1]])
                P.dma("sp", rope_d[0, :, 0:NCTX], ro[0][:, :NCTX], reads=[b_ro[0]], writes=[bd["rope"]])
                P.dma("sp", rope_d[1, :, 0:NCTX], ro[1][:, :NCTX], reads=[b_ro[1]], writes=[bd["rope"]])
                for cch in range(4):
                    P.dma("sp", rp[:], Cd["ropepos"][:, cch * 512:(cch + 1) * 512], writes=[b_rp])
                    dve(lambda: V.tensor_scalar(out=rr[:], in0=rp[:], scalar1=invr[:, 0:1], scalar2=None,
                                                op0=ALU.mult, op1=ALU.bypass), [b_rp, b_invr], [b_rr])
                    frac_wrap(rr, b_rr, 512)
                    act(lambda: A.activation(out=ro[1][:], in_=rr[:], func=AF.Sin, scale=SIN_SCALE),
                        [b_rr], [b_ro[1]])
                    P.dma("sp", rope_d[1, :, NCTX + cch * 512:NCTX + (cch + 1) * 512], ro[1][:],
                          reads=[b_ro[1]], writes=[bd["rope"]])
                    dve(lambda: V.tensor_scalar(out=rr[:], in0=rr[:], scalar1=0.25, scalar2=None,
                                                op0=ALU.add, op1=ALU.bypass), [b_rr], [b_rr])
                    wrap_half(rr, b_rr, 512)
                    act(lambda: A.activation(out=ro[0][:], in_=rr[:], func=AF.Sin, scale=SIN_SCALE),
                        [b_rr], [b_ro[0]])
                    P.dma("sp", rope_d[0, :, NCTX + cch * 512:NCTX + (cch + 1) * 512], ro[0][:],
                          reads=[b_ro[0]], writes=[bd["rope"]])

            if stage_on["ada"]:
                cst = sb("cst", [32, 128], scope=ph)
                b_cst = Buf("cst")
                siluT = sb("siluT", [128, 16, 2], scope=ph)
                b_siluT = Buf("siluT")
                P.dma("sp", cst[:], cc_in.rearrange("j (k p) -> (j k) p", p=128), writes=[b_cst])
                tr(psum[2][:, :32], cst[:, :], [b_cst], pbuf[2])
                act(lambda: A.activation(out=siluT[:].rearrange("p k j -> p j k"),
                                         in_=psum[2][:, :32].rearrange("p (j k) -> p j k", j=2),
                                         func=AF.Silu), [pbuf[2]], [b_siluT])
                siluR = sb("siluR", [128, 16, 2], F32R, scope=ph)
                b_siluR = Buf("siluR")
                dve(lambda: V.tensor_copy(out=siluR[:], in_=siluT[:]), [b_siluT], [b_siluR])
                wad = [sb(f"wad{i}", [128, 16, 512], F32R, scope=ph) for i in range(2)]
                b_wad = [Buf(f"wad{i}") for i in range(2)]
                brow = sb("brow", [2, 6 * D], scope=ph)
                b_brow = Buf("brow")
                mrow = [sb(f"mrow{i}", [2, 512], scope=ph) for i in range(2)]
                b_mrow = [Buf(f"mrow{i}") for i in range(2)]
                kk = 0
                for l in range(nlayers):
                    wv = Wd["w_ada"][l].rearrange("(kt p) c -> p kt c", p=128)
                    P.dma("sp", brow[:], Wd["b_ada"][l:l + 1, :].to_broadcast([2, 6 * D]), writes=[b_brow])
                    for cb in range(24):
                        s = kk % 2
                        pb = 4 + kk % 2
                        kk += 1
                        P.dma("pool", wad[s][:], wv[:, :, cb * 512:(cb + 1) * 512], writes=[b_wad[s]])
                        for kt in range(KT):
                            mm(psum[pb][:2, :], siluR[:, kt, :], wad[s][:, kt, :], kt == 0, kt == KT - 1,
                               [b_wad[s], b_siluR], pbuf[pb])
                        dve(lambda: V.tensor_tensor(out=mrow[s][:], in0=psum[pb][:2, :],
                                                    in1=brow[:, cb * 512:(cb + 1) * 512], op=ALU.add),
                            [pbuf[pb], b_brow], [b_mrow[s]])
                        pt_ = 6 + kk % 2
                        for j in range(4):
                            tr(psum[pt_][:, 2 * j:2 * j + 2], mrow[s][:, j * 128:(j + 1) * 128], [b_mrow[s]], pbuf[pt_])
                        dve(lambda: V.tensor_copy(out=modT[:, l, cb * 4:(cb + 1) * 4, :].rearrange("p f j -> p (f j)"),
                                                  in_=psum[pt_][:, :8]), [pbuf[pt_]], [b_modT])
                    for c0_ in (16, 64):
                        dve(lambda: V.tensor_scalar(out=modT[:, l, c0_:c0_ + 16, :], in0=modT[:, l, c0_:c0_ + 16, :],
                                                    scalar1=1.0, scalar2=None, op0=ALU.add, op1=ALU.bypass),
                            [b_modT], [b_modT])
                if debug:
                    P.dma("sp", dbg_mod, modT[:].rearrange("p l f j -> p (l f j)"), reads=[b_modT], writes=[])
        P.barrier()

        def mcol(l, chunk, f, col):
            return modT[:, l, chunk * 16 + f, col:col + 1]

        with contextlib.ExitStack() as ph:
            xin = [sb(f"xin{i}", [128, D], scope=ph) for i in range(2)]
            b_xin = [Buf(f"xin{i}") for i in range(2)]
            stg = [sb(f"stg{i}", [128, 4, 128], scope=ph) for i in range(2)]
            b_stg = [Buf(f"stg{i}") for i in range(2)]
            k = 0
            for ti in range(NT):
                src = ctx_in[ti * 128:(ti + 1) * 128, :] if ti < 2 else \
                    x_in[(ti - 2) * 128:(ti - 1) * 128, :]
                s = ti % 2
                P.dma("sp", xin[s][:], src, writes=[b_xin[s]])
                for g4 in range(4):
                    pb = k % 2
                    for j in range(4):
                        ft = g4 * 4 + j
                        tr(psum[pb][:, j * 128:(j + 1) * 128], xin[s][:, ft * 128:(ft + 1) * 128],
                           [b_xin[s]], pbuf[pb])
                    ss = k % 2
                    dve(lambda: V.tensor_copy(out=stg[ss][:].rearrange("p a b -> p (a b)"), in_=psum[pb][:]),
                        [pbuf[pb]], [b_stg[ss]])
                    P.dma("sp", xT_v[:, g4 * 4:(g4 + 1) * 4, ti * 128:(ti + 1) * 128], stg[ss][:],
                          reads=[b_stg[ss]], writes=xT_bufs(ti * 128, 128))
                    k += 1
        P.barrier()

        for l in range(nlayers):
            if stage_on["A"]:
                with contextlib.ExitStack() as ph:
                    phase_A(nc, P, ph, l, locals())
                P.barrier()
            if stage_on["gla"]:
                with contextlib.ExitStack() as ph:
                    phase_gla(nc, P, ph, l, locals())
                P.barrier()
            if stage_on["s5"]:
                with contextlib.ExitStack() as ph:
                    phase_s5(nc, P, ph, l, locals())
                P.barrier()
            if stage_on["att"]:
                with contextlib.ExitStack() as ph:
                    phase_att(nc, P, ph, l, locals())
                P.barrier()
            if stage_on["C"]:
                with contextlib.ExitStack() as ph:
                    phase_C(nc, P, ph, l, locals())
                P.barrier()

        with contextlib.ExitStack() as ph:
            xf = [sb(f"xf{i}", [128, KT, 128], scope=ph) for i in range(2)]
            b_xf = [Buf(f"xf{i}") for i in range(2)]
            ost = [sb(f"ost{i}", [128, D], scope=ph) for i in range(2)]
            b_ost = [Buf(f"ost{i}") for i in range(2)]
            k = 0
            toks = []
            for ti in range(2, NT):
                s = ti % 2
                P.dma("sp", xf[s][:], xT_v[:, :, ti * 128:(ti + 1) * 128],
                      reads=xT_bufs(ti * 128, 128), writes=[b_xf[s]])
                for g4 in range(4):
                    pb = k % 2
                    for j in range(4):
                        ft = g4 * 4 + j
                        tr(psum[pb][:, j * 128:(j + 1) * 128], xf[s][:, ft, :], [b_xf[s]], pbuf[pb])
                    dve(lambda: V.tensor_copy(out=ost[s][:, g4 * 512:(g4 + 1) * 512], in_=psum[pb][:]),
                        [pbuf[pb]], [b_ost[s]])
                    k += 1
                toks.append(P.dma("sp", out_d[(ti - 2) * 128:(ti - 1) * 128, :], ost[s][:],
                                  reads=[b_ost[s]], writes=[]))
            for t in toks:
                P._wait("sp", t)
        print("instructions:", P.n_inst)
    return nc


class NS:
    def __init__(self, d):
        self.__dict__.update(d)


def phase_A(nc, P, ph, l, env):
    E = NS(env)
    V, A = nc.vector, nc.scalar
    sb, psum, pbuf, Wd, bd = E.sb, E.psum, E.pbuf, E.Wd, E.bd
    mm, tr, dve, act = E.mm, E.tr, E.dve, E.act
    L = f"A{l}"
    h1 = sb(L + "h1", [128, KT, 512], F32R, scope=ph)
    b_h1 = [Buf(f"h1_{i}") for i in range(KT)]
    xs = [sb(L + f"xs{i}", [128, 512], scope=ph) for i in range(3)]
    b_xs = [Buf(f"xs{i}") for i in range(3)]
    NW = 3
    wb = [sb(L + f"wb{i}", [128, 4096], F32R, scope=ph) for i in range(NW)]
    b_wb = [Buf(f"wb{i}") for i in range(NW)]
    wglr = sb(L + "wglr", [128, KT, 16], scope=ph)
    b_wglr = Buf("wglr")
    ost = [sb(L + f"ost{i}", [128, 512], scope=ph) for i in range(4)]
    b_ost = [Buf(f"ost{i}") for i in range(4)]
    cosT = sb(L + "cosT", [128, T], scope=ph)
    sinT = sb(L + "sinT", [128, T], scope=ph)
    b_rope = Buf("ropesb")
    sq = sb(L + "sq", [128, 512], F32R, scope=ph)
    b_sq = Buf("sq")
    rs = sb(L + "rs", [128, 512], scope=ph)
    b_rs = Buf("rs")
    qn = sb(L + "qn", [128, 512], F32R, scope=ph)
    b_qn = Buf("qn")
    t1 = sb(L + "t1", [128, 512], scope=ph)
    b_t1 = Buf("t1")
    t2 = sb(L + "t2", [128, 512], scope=ph)
    b_t2 = Buf("t2")
    cnt = {"w": 0, "o": 0, "x": 0, "p": 0}

    P.dma("sp", cosT[:], E.rope_d[0], reads=[bd["rope"]], writes=[b_rope])
    P.dma("sp", sinT[:], E.rope_d[1], reads=[bd["rope"]], writes=[b_rope])
    P.dma("sp", wglr[:], Wd["w_inA"][l].rearrange("(kt p) c -> p kt c", p=128)[:, :, OFF_GLR:OFF_GLR + 16],
          writes=[b_wglr])
    wv = Wd["w_inA"][l].rearrange("(kt p) c -> p kt c", p=128)

    def wload(c0, ncols):
        s = cnt["w"] % NW
        cnt["w"] += 1
        view = wb[s][:, :KT * ncols].rearrange("p (k c) -> p k c", c=ncols)
        P.dma("pool", view, wv[:, :, c0:c0 + ncols], writes=[b_wb[s]])
        return view, b_wb[s]

    def nps():
        p = cnt["p"] % 4
        cnt["p"] += 1
        return p

    def nost():
        s = cnt["o"] % 4
        cnt["o"] += 1
        return s

    for bi, (t0, n) in enumerate(TBLK):
        col = 1 if bi == 0 else 0
        for ft in range(KT):
            s = cnt["x"] % 3
            cnt["x"] += 1
            P.dma("sp", xs[s][:, :n], E.xT_v[:, ft, t0:t0 + n], reads=[E.b_xT[bi]], writes=[b_xs[s]])
            dve(lambda: V.tensor_scalar(out=h1[:, ft, :n], in0=xs[s][:, :n],
                                        scalar1=E.mcol(l, 1, ft, col), scalar2=E.mcol(l, 0, ft, col),
                                        op0=ALU.mult, op1=ALU.add),
                [b_xs[s], E.b_modT], [b_h1[ft]])

        def form1(c0, ntile, post):
            for cb in range(0, ntile, 2):
                nb = min(2, ntile - cb)
                view, bw = wload(c0 + cb * 128, nb * 128)
                for j in range(nb):
                    p = nps()
                    for kt in range(KT):
                        mm(psum[p][:, :n], view[:, kt, j * 128:(j + 1) * 128], h1[:, kt, :n],
                           kt == 0, kt == KT - 1, [bw, b_h1[kt]], pbuf[p])
                    post(cb + j, p)

        def store_fm(dst, name, fi, src_tile, b_src):
            P.dma("sp", dst[fi * 128:(fi + 1) * 128, t0:t0 + n], src_tile[:, :n],
                  reads=[b_src], writes=[bd[name]])

        def post_copy(dst, name, scale=None):
            def f(fi, p):
                s = nost()
                if scale is None:
                    act(lambda: A.copy(out=ost[s][:, :n], in_=psum[p][:, :n]), [pbuf[p]], [b_ost[s]])
                else:
                    act(lambda: A.mul(out=ost[s][:, :n], in_=psum[p][:, :n], mul=scale), [pbuf[p]], [b_ost[s]])
                store_fm(dst, name, fi, ost[s], b_ost[s])
            return f

        def post_qk(dst, name, wname):
            def f(fi, p):
                act(lambda: A.activation(out=sq[:, :n], in_=psum[p][:, :n], func=AF.Square),
                    [pbuf[p]], [b_sq])
                p2 = 4 + (cnt["p"] % 2)
                mm(psum[p2][:, :n], E.onesr[:], sq[:, :n], True, True, [E.b_onesr, b_sq], pbuf[p2])
                act(lambda: A.activation(out=rs[:, :n], in_=psum[p2][:, :n], func=AF.Sqrt,
                                         bias=E.epsc[:, 0:1], scale=1.0 / HD), [pbuf[p2], E.b_epsc], [b_rs])
                dve(lambda: V.reciprocal(out=rs[:, :n], in_=rs[:, :n]), [b_rs], [b_rs])
                dve(lambda: V.scalar_tensor_tensor(out=qn[:, :n], in0=psum[p][:, :n],
                                                   scalar=E.vcol(wname, l), in1=rs[:, :n],
                                                   op0=ALU.mult, op1=ALU.mult),
                    [pbuf[p], b_rs, E.b_vecT], [b_qn])
                p3 = 6 + (cnt["p"] % 2)
                mm(psum[p3][:, :n], E.prot[:], qn[:, :n], True, True, [E.b_prot, b_qn], pbuf[p3])
                dve(lambda: V.tensor_tensor(out=t1[:, :n], in0=qn[:, :n].bitcast(F32), in1=cosT[:, t0:t0 + n],
                                            op=ALU.mult), [b_qn, b_rope], [b_t1])
                dve(lambda: V.tensor_tensor(out=t2[:, :n], in0=psum[p3][:, :n], in1=sinT[:, t0:t0 + n],
                                            op=ALU.mult), [pbuf[p3], b_rope], [b_t2])
                s = nost()
                dve(lambda: V.tensor_tensor(out=ost[s][:, :n], in0=t1[:, :n], in1=t2[:, :n], op=ALU.add),
                    [b_t1, b_t2], [b_ost[s]])
                store_fm(dst, name, fi, ost[s], b_ost[s])
            return f

        form1(OFF_GQ, 4, post_copy(E.qgT, "qgT", GLA_DK ** -0.5))
        form1(OFF_GK, 4, post_copy(E.kgT, "kgT"))
        form1(OFF_SU, 6, post_copy(E.suT, "suT"))
        form1(OFF_AQ, 8, post_qk(E.qaT, "qaT", "q_norm_w"))
        form1(OFF_AK, 2, post_qk(E.kaT, "kaT", "k_norm_w"))
        p = nps()
        for kt in range(KT):
            mm(psum[p][:16, :n], wglr[:, kt, :], h1[:, kt, :n].bitcast(F32), kt == 0, kt == KT - 1,
               [b_wglr, b_h1[kt]], pbuf[p])
        s = nost()
        act(lambda: A.copy(out=ost[s][:16, :n], in_=psum[p][:16, :n]), [pbuf[p]], [b_ost[s]])
        P.dma("sp", E.glrT[:, t0:t0 + n], ost[s][:16, :n], reads=[b_ost[s]], writes=[bd["glrT"]])

        def form2(c0, ncols, dst, name, silu):
            for cb in range(0, ncols, 256):
                view, bw = wload(c0 + cb, 256)
                for tt in range(n // 128):
                    p = nps()
                    for kt in range(KT):
                        mm(psum[p][:, :256], h1[:, kt, tt * 128:(tt + 1) * 128], view[:, kt, :],
                           kt == 0, kt == KT - 1, [bw, b_h1[kt]], pbuf[p])
                    s = nost()
                    if silu:
                        act(lambda: A.activation(out=ost[s][:, :256], in_=psum[p][:, :256], func=AF.Silu),
                            [pbuf[p]], [b_ost[s]])
                    else:
                        act(lambda: A.copy(out=ost[s][:, :256], in_=psum[p][:, :256]), [pbuf[p]], [b_ost[s]])
                    r0 = t0 + tt * 128
                    P.dma("sp", dst[r0:r0 + 128, cb:cb + 256], ost[s][:, :256],
                          reads=[b_ost[s]], writes=[bd[name]])

        form2(OFF_GV, 1024, E.vg, "vg", False)
        form2(OFF_GR, 1024, E.grt, "grt", True)
        form2(OFF_AV, 256, E.va, "va", False)


def phase_gla(nc, P, ph, l, env):
    E = NS(env)
    V, A = nc.vector, nc.scalar
    sb, psum, pbuf, Wd, bd, Cd = E.sb, E.psum, E.pbuf, E.Wd, E.bd, E.Cd
    mm, tr, dve, act = E.mm, E.tr, E.dve, E.act
    L = f"G{l}"

    def t_(name, shape, dt=F32):
        return sb(L + name, shape, dt, scope=ph), Buf(name)

    cmask, b_cmask = t_("cmask", [128, T])
    trim, b_trim = t_("trim", [64, 2, 64])
    wg, b_wg = t_("wg", [16, 2, 512])
    glr, b_glr = t_("glr", [16, T])
    gnw, b_gnw = t_("gnw", [64, 1024])
    y2, b_y2 = t_("y2", [128, T])
    cs, b_cs = t_("cs", [128, T])
    eb, b_eb = t_("eb", [128, T])
    qe, b_qe = t_("qe", [128, T])
    ke, b_ke = t_("ke", [128, T])
    kd, b_kd = t_("kd", [128, T])
    v_tm, b_v = t_("v_tm", [64, NCH, 256])
    kd_tm, b_kdt = t_("kd_tm", [64, NCH, 128])
    o_acc, b_oacc = t_("o_acc", [64, NCH, 256])
    b_oc = [Buf(f"oacc{c}") for c in range(NCH)]
    S3 = [t_(f"S{i}", [128, 256]) for i in range(3)]
    attm = [t_(f"attm{i}", [64, 64]) for i in range(2)]
    grs = [t_(f"grs{i}", [64, 256]) for i in range(2)]
    on = [t_(f"on{i}", [64, 256]) for i in range(2)]
    ots = [t_(f"ots{i}", [128, 2, 64]) for i in range(2)]
    stats, b_stats = t_("stats", [64, 6])
    mv, b_mv = t_("mv", [64, 2])
    rstd, b_rstd = t_("rstd", [64, 1])

    P.dma("sp", cmask[:], Cd["cmask"][:, :], writes=[b_cmask])
    P.dma("sp", trim[:, 0, :], Cd["trif"][:, :], writes=[b_trim])
    P.dma("sp", trim[:, 1, :], Cd["trib"][:, :], writes=[b_trim])
    P.dma("sp", wg[:], Wd["w_gla_gate"][l].rearrange("d r c -> r d c"), writes=[b_wg])
    P.dma("sp", glr[:], E.glrT[:, :], reads=[bd["glrT"]], writes=[b_glr])
    P.dma("sp", gnw[:], Wd["gla_norm_w"][l:l + 1, :].to_broadcast([64, 1024]), writes=[b_gnw])
    kk = 0
    for h in range(GLA_H):
        P.dma("sp", v_tm[:], E.vg[:, h * 256:(h + 1) * 256].rearrange("(c p) f -> p c f", p=64),
              reads=[bd["vg"]], writes=[b_v])
        for d in range(2):
            P.dma("sp", qe[:], E.qgT[h * 128:(h + 1) * 128, :], reads=[bd["qgT"]], writes=[b_qe])
            P.dma("sp", ke[:], E.kgT[h * 128:(h + 1) * 128, :], reads=[bd["kgT"]], writes=[b_ke])
            negb = E.vcol("b_gla_gate", l * 8 + d * 4 + h)
            for bi, (t0, n) in enumerate(TBLK):
                pb = bi % 2
                mm(psum[pb][:, :n], wg[:, d, h * 128:(h + 1) * 128], glr[:, t0:t0 + n], True, True,
                   [b_wg, b_glr], pbuf[pb])
                act(lambda: A.activation(out=y2[:, t0:t0 + n], in_=psum[pb][:, :n], func=AF.Exp,
                                         scale=-1.0, bias=negb), [pbuf[pb], E.b_vecT], [b_y2])
            act(lambda: A.activation(out=y2[:], in_=y2[:], func=AF.Ln, bias=1.0, scale=1.0), [b_y2], [b_y2])
            if d == 0:
                dve(lambda: V.tensor_tensor_scan(out=cs[:], data0=cmask[:], data1=y2[:], initial=0.0,
                                                 op0=ALU.mult, op1=ALU.add), [b_cmask, b_y2], [b_cs])
            else:
                dve(lambda: V.tensor_tensor_scan(out=cs[:, ::-1], data0=cmask[:], data1=y2[:, ::-1],
                                                 initial=0.0, op0=ALU.mult, op1=ALU.add),
                    [b_cmask, b_y2], [b_cs])
            if E.debug and h == 0 and d == 0 and l == 0:
                P.dma("sp", E.dbg_y2, y2[:], reads=[b_y2], writes=[])
                P.dma("sp", E.dbg_cs, cs[:], reads=[b_cs], writes=[])
            act(lambda: A.activation(out=eb[:], in_=cs[:], func=AF.Exp, scale=-1.0 / GLA_TAU), [b_cs], [b_eb])
            act(lambda: A.activation(out=y2[:], in_=cs[:], func=AF.Exp, scale=1.0 / GLA_TAU), [b_cs], [b_y2])
            dve(lambda: V.tensor_tensor(out=qe[:], in0=qe[:], in1=eb[:], op=ALU.mult), [b_qe, b_eb], [b_qe])
            dve(lambda: V.tensor_tensor(out=ke[:], in0=ke[:], in1=y2[:], op=ALU.mult), [b_ke, b_y2], [b_ke])
            lpos = CH - 1 if d == 0 else 0
            eb3 = eb[:].rearrange("p (c i) -> p c i", i=CH)
            dve(lambda: V.tensor_tensor(out=kd[:].rearrange("p (c i) -> p c i", i=CH),
                                        in0=ke[:].rearrange("p (c i) -> p c i", i=CH),
                                        in1=eb3[:, :, lpos:lpos + 1].to_broadcast([128, NCH, CH]),
                                        op=ALU.mult), [b_ke, b_eb], [b_kd])
            for c4 in range(NCH // 4):
                pb = 2 + c4 % 2
                for j in range(4):
                    c = c4 * 4 + j
                    tr(psum[pb][:64, j * 128:(j + 1) * 128], kd[:, c * CH:(c + 1) * CH], [b_kd], pbuf[pb])
                act(lambda: A.copy(out=kd_tm[:, c4 * 4:(c4 + 1) * 4, :].rearrange("p a b -> p (a b)"),
                                   in_=psum[pb][:64, :]), [pbuf[pb]], [b_kdt])
            if E.debug and h == 0 and l == 0 and d == 1:
                P.dma("sp", E.dbg_oraw, o_acc[:].rearrange("p a b -> p (a b)"), reads=b_oc, writes=[])
            dve(lambda: V.memset(S3[0][0][:], 0.0), [], [S3[0][1]])
            order = list(range(NCH)) if d == 0 else [3, 2, 1, 0] + list(range(NCH - 1, 3, -1))

            def stageA(c, i):
                sl = slice(c * CH, (c + 1) * CH)
                mm(psum[i % 2][:64, :64], ke[:, sl], qe[:, sl], True, True, [b_ke, b_qe], pbuf[i % 2])
                mm(psum[2 + i % 2][:, :256], kd_tm[:, c, :], v_tm[:, c, :], True, True, [b_kdt, b_v], pbuf[2 + i % 2])

            def stageB(c, i):
                sl = slice(c * CH, (c + 1) * CH)
                am, b_am = attm[i % 2]
                Sc, b_Sc = S3[i % 3]
                Sn, b_Sn = S3[(i + 1) % 3]
                po = 4 + i % 2
                dve(lambda: V.tensor_tensor(out=am[:], in0=psum[i % 2][:64, :64], in1=trim[:, d, :], op=ALU.mult),
                    [pbuf[i % 2], b_trim], [b_am])
                mm(psum[po][:64, :256], am[:], v_tm[:, c, :], True, False, [b_am, b_v], pbuf[po])
                mm(psum[po][:64, :256], qe[:, sl], Sc[:], False, True, [b_qe, b_Sc], pbuf[po])
                li = c * CH + lpos
                dve(lambda: V.scalar_tensor_tensor(out=Sn[:], in0=Sc[:], scalar=eb[:, li:li + 1],
                                                   in1=psum[2 + i % 2][:, :256], op0=ALU.mult, op1=ALU.add),
                    [b_Sc, b_eb, pbuf[2 + i % 2]], [b_Sn])
                if d == 0:
                    act(lambda: A.copy(out=o_acc[:, c, :], in_=psum[po][:64, :256]), [pbuf[po]], [b_oc[c]])
                else:
                    dve(lambda: V.tensor_tensor(out=o_acc[:, c, :], in0=psum[po][:64, :256], in1=o_acc[:, c, :],
                                                op=ALU.add), [pbuf[po], b_oc[c]], [b_oc[c]])

            stageA(order[0], 0)
            for i in range(NCH):
                if i + 1 < NCH:
                    stageA(order[i + 1], i + 1)
                stageB(order[i], i)
        for c in range(NCH):
            s = c % 2
            ont, b_on = on[s]
            grt_, b_gr = grs[s]
            ott, b_ot = ots[s]
            P.dma("sp", grt_[:], E.grt[c * CH:(c + 1) * CH, h * 256:(h + 1) * 256], reads=[bd["grt"]],
                  writes=[b_gr])
            dve(lambda: V.bn_stats(out=stats[:], in_=o_acc[:, c, :]), [b_oc[c]], [b_stats])
            dve(lambda: V.bn_aggr(out=mv[:], in_=stats[:]), [b_stats], [b_mv])
            act(lambda: A.activation(out=rstd[:], in_=mv[:, 1:2], func=AF.Sqrt, bias=E.epsc[:64, 0:1], scale=1.0),
                [b_mv, E.b_epsc], [b_rstd])
            dve(lambda: V.reciprocal(out=rstd[:], in_=rstd[:]), [b_rstd], [b_rstd])
            dve(lambda: V.tensor_scalar(out=ont[:], in0=o_acc[:, c, :], scalar1=mv[:, 0:1], scalar2=rstd[:, 0:1],
                                        op0=ALU.subtract, op1=ALU.mult), [b_oc[c], b_mv, b_rstd], [b_on])
            dve(lambda: V.tensor_tensor(out=ont[:], in0=ont[:], in1=gnw[:, h * 256:(h + 1) * 256], op=ALU.mult),
                [b_on, b_gnw], [b_on])
            dve(lambda: V.tensor_tensor(out=ont[:], in0=ont[:], in1=grt_[:], op=ALU.mult), [b_on, b_gr], [b_on])
            for half in range(2):
                tr(psum[7][:, half * 64:(half + 1) * 64], ont[:, half * 128:(half + 1) * 128], [b_on], pbuf[7])
            act(lambda: A.copy(out=ott[:].rearrange("p a b -> p (a b)"), in_=psum[7][:, :128]), [pbuf[7]], [b_ot])
            P.dma("sp", E.oglaT[h * 256:(h + 1) * 256, c * CH:(c + 1) * CH].rearrange("(a p) t -> p a t", p=128),
                  ott[:], reads=[b_ot], writes=[bd["oglaT"]])


def phase_s5(nc, P, ph, l, env):
    E = NS(env)
    V, A = nc.vector, nc.scalar
    sb, psum, pbuf, Wd, bd, Cd = E.sb, E.psum, E.pbuf, E.Wd, E.bd, E.Cd
    mm, tr, dve, act = E.mm, E.tr, E.dve, E.act
    L = f"S{l}"
    NG = S5_NG

    def t_(name, shape, dt=F32):
        return sb(L + name, shape, dt, scope=ph), Buf(name)

    def ts(out, in0, s1, s2, op0, op1, rd, wr):
        dve(lambda: V.tensor_scalar(out=out, in0=in0, scalar1=s1, scalar2=s2, op0=op0, op1=op1), rd, wr)

    def tt(out, in0, in1, op, rd, wr):
        dve(lambda: V.tensor_tensor(out=out, in0=in0, in1=in1, op=op), rd, wr)

    ri, b_ri = t_("ri", [128, 512], I32)
    rf, b_rf = t_("rf", [128, 512])

    def wrap_half(r, b_r):
        dve(lambda: V.tensor_single_scalar(out=rf[:, :r.shape[1]], in_=r, scalar=0.5, op=ALU.is_gt), [b_r], [b_rf])
        tt(r, r, rf[:, :r.shape[1]], ALU.subtract, [b_r, b_rf], [b_r])

    def frac_wrap(r, b_r):
        n = r.shape[1]
        dve(lambda: V.tensor_copy(out=ri[:, :n], in_=r), [b_r], [b_ri])
        dve(lambda: V.tensor_copy(out=rf[:, :n], in_=ri[:, :n]), [b_ri], [b_rf])
        tt(r, r, rf[:, :n], ALU.subtract, [b_r, b_rf], [b_r])
        wrap_half(r, b_r)

    def sincos(r, b_r, sin_out, b_s, cos_out, b_c):
        frac_wrap(r, b_r)
        act(lambda: A.activation(out=sin_out, in_=r, func=AF.Sin, scale=SIN_SCALE), [b_r], [b_s])
        ts(r, r, 0.25, None, ALU.add, ALU.bypass, [b_r], [b_r])
        wrap_half(r, b_r)
        act(lambda: A.activation(out=cos_out, in_=r, func=AF.Sin, scale=SIN_SCALE), [b_r], [b_c])

    iota, b_iota = t_("iota", [128, 512])
    P.dma("sp", iota[:], Cd["iota512"][:, :], writes=[b_iota])
    rmask, b_rmask = t_("rmask", [128, 8])
    P.dma("sp", rmask[:], Cd["rmask"][:, :], writes=[b_rmask])
    colmask, b_colmask = t_("colmask", [128, 8, 128])
    P.dma("sp", colmask[:].rearrange("p a b -> p (a b)"), Cd["colmask"][:, :], writes=[b_colmask])
    pst, b_pst = t_("pst", [128, 128])
    par = {}
    NOFF = [256, 512]
    for d in range(2):
        pp = {}
        for nm in ("lamr", "lami", "dt", "rho", "thp", "sn", "cs", "kre", "kim", "w1", "w2", "w3"):
            pp[nm] = t_(f"{nm}{d}", [128, NG])
        pp["Ec"] = t_(f"Ec{d}", [128, 2, NG])
        pp["Es"] = t_(f"Es{d}", [128, 2, NG])
        par[d] = pp
        for nm, src in (("lamr", "s5_lam_re"), ("lami", "s5_lam_im")):
            P.dma("sp", pst[:NG, 0:64], Wd[src][l, d], writes=[b_pst])
            P.dma("sp", pst[:NG, 64:128], Wd[src][l, d], writes=[b_pst])
            tr(psum[0][:, :NG], pst[:NG, :], [b_pst], pbuf[0])
            dve(lambda: V.tensor_copy(out=pp[nm][0][:], in_=psum[0][:, :NG]), [pbuf[0]], [pp[nm][1]])
        dtt, b_dt = pp["dt"]
        P.dma("sp", dtt[:], Wd["s5_log_dt"][l, d:d + 1, :].to_broadcast([128, NG]), writes=[b_dt])
        act(lambda: A.activation(out=dtt[:], in_=dtt[:], func=AF.Exp), [b_dt], [b_dt])
        lamr, b_lamr = pp["lamr"]
        lami, b_lami = pp["lami"]
        rho, b_rho = pp["rho"]
        thp, b_thp = pp["thp"]
        sn, b_sn = pp["sn"]
        cs_, b_cs = pp["cs"]
        kre, b_kre = pp["kre"]
        kim, b_kim = pp["kim"]
        w1, b_w1 = pp["w1"]
        w2, b_w2 = pp["w2"]
        w3, b_w3 = pp["w3"]
        tt(w1[:], lamr[:], dtt[:], ALU.mult, [b_lamr, b_dt], [b_w1])
        act(lambda: A.activation(out=rho[:], in_=w1[:], func=AF.Exp), [b_w1], [b_rho])
        tt(thp[:], lami[:], dtt[:], ALU.mult, [b_lami, b_dt], [b_thp])
        ts(thp[:], thp[:], 1.0 / TWO_PI, None, ALU.mult, ALU.bypass, [b_thp], [b_thp])
        dve(lambda: V.tensor_copy(out=w1[:], in_=thp[:]), [b_thp], [b_w1])
        sincos(w1[:], b_w1, sn[:], b_sn, cs_[:], b_cs)
        tt(w1[:], rho[:], cs_[:], ALU.mult, [b_rho, b_cs], [b_w1])
        ts(w1[:], w1[:], -1.0, None, ALU.add, ALU.bypass, [b_w1], [b_w1])
        tt(w2[:], rho[:], sn[:], ALU.mult, [b_rho, b_sn], [b_w2])
        tt(w3[:], lamr[:], lamr[:], ALU.mult, [b_lamr], [b_w3])
        tt(kre[:], lami[:], lami[:], ALU.mult, [b_lami], [b_kre])
        tt(w3[:], w3[:], kre[:], ALU.add, [b_w3, b_kre], [b_w3])
        dve(lambda: V.reciprocal(out=w3[:], in_=w3[:]), [b_w3], [b_w3])
        tt(kre[:], w1[:], lamr[:], ALU.mult, [b_w1, b_lamr], [b_kre])
        tt(kim[:], w2[:], lami[:], ALU.mult, [b_w2, b_lami], [b_kim])
        tt(kre[:], kre[:], kim[:], ALU.add, [b_kre, b_kim], [b_kre])
        tt(kre[:], kre[:], w3[:], ALU.mult, [b_kre, b_w3], [b_kre])
        tt(kim[:], w2[:], lamr[:], ALU.mult, [b_w2, b_lamr], [b_kim])
        tt(w2[:], w1[:], lami[:], ALU.mult, [b_w1, b_lami], [b_w2])
        tt(kim[:], kim[:], w2[:], ALU.subtract, [b_kim, b_w2], [b_kim])
        tt(kim[:], kim[:], w3[:], ALU.mult, [b_kim, b_w3], [b_kim])
        Ec, b_Ec = pp["Ec"]
        Es, b_Es = pp["Es"]
        for k, noff in enumerate(NOFF):
            ts(w1[:], thp[:], float(noff), None, ALU.mult, ALU.bypass, [b_thp], [b_w1])
            sincos(w1[:], b_w1, Es[:, k, :], b_Es, Ec[:, k, :], b_Ec)

    bre, b_bre = t_("bre", [128, NG, 16])
    bim, b_bim = t_("bim", [128, NG, 16])
    for (tile_, b_, src) in ((bre, b_bre, "s5_b_re"), (bim, b_bim, "s5_b_im")):
        v_ = Wd[src][l].rearrange("g p c -> p g c")
        P.dma("sp", tile_[0:64], v_, writes=[b_])
        P.dma("sp", tile_[64:128], v_, writes=[b_])
    Pre, b_Pre = t_("Pre", [128, 8, 16])
    Pim, b_Pim = t_("Pim", [128, 8, 16])
    pw, b_pw = t_("pw", [128, 8, 16])
    X1, b_X1 = t_("X1", [128, 128])
    X2, b_X2 = t_("X2", [128, 128])
    TB = [t_(f"TB{i}", [128, 128]) for i in range(2)]
    TW = [t_(f"TW{i}", [128, 128]) for i in range(2)]
    cst, b_cst = t_("cstg", [128, 2, 128])
    LB = [[t_(f"LB{g}_{i}", [128, 128], F32R) for i in range(2)] for g in range(8)]
    WG = [[t_(f"WG{g}_{i}", [128, 128], F32R) for i in range(2)] for g in range(8)]
    cosm = [t_(f"cosm{g}", [128, 512]) for g in range(8)]
    sinm = [t_(f"sinm{g}", [128, 512]) for g in range(8)]
    NSET = 5
    SETS = []
    for i in range(NSET):
        SETS.append({"m1": t_(f"qm1{i}", [128, 512]), "vv": t_(f"qvv{i}", [128, 512]),
                     "P1": t_(f"qP1{i}", [128, 512], F32R), "P2": t_(f"qP2{i}", [128, 512], F32R)})
    tq, b_tq = t_("tq", [128, 512])
    st8, b_st8 = t_("st8", [128, 8])
    b_st = [Buf(f"st{g}") for g in range(8)]
    psw, b_psw = t_("psw", [128, 128])
    P.dma("sp", psw[:], Cd["psw"][:, :], writes=[b_psw])
    init8, b_init8 = t_("init8", [128, 8])
    t8, b_t8 = t_("t8", [128, 8])

    suo, b_suo = t_("suo", [128, T], F32R)
    yacc, b_yacc = t_("yacc", [128, T])
    yg = [t_(f"yg{i}", [128, T], F32R) for i in range(6)]

    TILES = {0: [(0, 256), (256, 512), (768, 512), (1280, 512), (1792, 512)],
             1: [(0, 256), (1792, 512), (1280, 512), (768, 512), (256, 512)]}
    GROUPS3 = [(0, 1, 2), (3, 4, 5), (6, 7)]
    setctr = [0]

    for o in range(6):
        P.dma("pool", suo[:], E.suT[o * 128:(o + 1) * 128, :], reads=[bd["suT"]], writes=[b_suo])
        P.dma("sp", yacc[:], E.suT[o * 128:(o + 1) * 128, :], reads=[bd["suT"]], writes=[b_yacc])
        ts(yacc[:], yacc[:], E.vcol("s5_d", l * 6 + o), None, ALU.mult, ALU.bypass, [b_yacc, E.b_vecT], [b_yacc])
        for d in range(2):
            pp = par[d]
            kre, b_kre = pp["kre"]
            kim, b_kim = pp["kim"]
            gs = slice(o * 8, (o + 1) * 8)
            kreb = kre[:, gs].unsqueeze(2).to_broadcast([128, 8, 16])
            kimb = kim[:, gs].unsqueeze(2).to_broadcast([128, 8, 16])
            tt(Pre[:], bre[:, gs, :], kreb, ALU.mult, [b_bre, b_kre], [b_Pre])
            tt(pw[:], bim[:, gs, :], kimb, ALU.mult, [b_bim, b_kim], [b_pw])
            tt(Pre[:], Pre[:], pw[:], ALU.subtract, [b_Pre, b_pw], [b_Pre])
            tt(Pim[:], bim[:, gs, :], kreb, ALU.mult, [b_bim, b_kre], [b_Pim])
            tt(pw[:], bre[:, gs, :], kimb, ALU.mult, [b_bre, b_kim], [b_pw])
            tt(Pim[:], Pim[:], pw[:], ALU.add, [b_Pim, b_pw], [b_Pim])
            Pre2 = Pre[:].rearrange("p g c -> p (g c)")
            Pim2 = Pim[:].rearrange("p g c -> p (g c)")
            dve(lambda: V.tensor_copy(out=X1[0:64, :], in_=Pre2[0:64, :]), [b_Pre], [b_X1])
            dve(lambda: V.tensor_copy(out=X1[64:128, :], in_=Pim2[64:128, :]), [b_Pim], [b_X1])
            dve(lambda: V.tensor_copy(out=X2[0:64, :], in_=Pim2[0:64, :]), [b_Pim], [b_X2])
            ts(X2[64:128, :], Pre2[64:128, :], -1.0, None, ALU.mult, ALU.bypass, [b_Pre], [b_X2])
            for i, (X_, b_X) in enumerate(((X1, b_X1), (X2, b_X2))):
                tr(psum[i][:, :128], X_[:, :], [b_X], pbuf[i])
                dve(lambda: V.tensor_copy(out=TB[i][0][:], in_=psum[i][:, :128]), [pbuf[i]], [TB[i][1]])
            crv = Wd["s5_c_re"][l, d].rearrange("g c p -> (g c) p")[o * 128:(o + 1) * 128, :]
            civ = Wd["s5_c_im"][l, d].rearrange("g c p -> (g c) p")[o * 128:(o + 1) * 128, :]
            P.dma("sp", cst[:, 0, 0:64], crv, writes=[b_cst])
            P.dma("sp", cst[:, 0, 64:128], civ, writes=[b_cst])
            P.dma("sp", cst[:, 1, 0:64], civ, writes=[b_cst])
            P.dma("sp", cst[:, 1, 64:128], crv, writes=[b_cst])
            for i in range(2):
                tr(psum[2 + i][:, :128], cst[:, i, :], [b_cst], pbuf[2 + i])
            dve(lambda: V.tensor_copy(out=TW[0][0][0:64, :], in_=psum[2][0:64, :128]), [pbuf[2]], [TW[0][1]])
            ts(TW[0][0][64:128, :], psum[2][64:128, :128], -1.0, None, ALU.mult, ALU.bypass, [pbuf[2]], [TW[0][1]])
            ts(TW[1][0][:], psum[3][:, :128], -1.0, None, ALU.mult, ALU.bypass, [pbuf[3]], [TW[1][1]])
            rho, b_rho = pp["rho"]
            thp, b_thp = pp["thp"]
            Ec, b_Ec = pp["Ec"]
            Es, b_Es = pp["Es"]
            for gl in range(8):
                g = o * 8 + gl
                for i in range(2):
                    ts(LB[gl][i][0][:], TB[i][0][:], rmask[:, gl:gl + 1], None, ALU.mult, ALU.bypass,
                       [TB[i][1], b_rmask], [LB[gl][i][1]])
                    tt(WG[gl][i][0][:], TW[i][0][:], colmask[:, gl, :], ALU.mult, [TW[i][1], b_colmask],
                       [WG[gl][i][1]])
                ts(tq[:], iota[:], thp[:, g:g + 1], None, ALU.mult, ALU.bypass, [b_iota, b_thp], [b_tq])
                sincos(tq[:], b_tq, sinm[gl][0][:], sinm[gl][1], cosm[gl][0][:], cosm[gl][1])
            rv = (lambda ap: ap[:, ::-1]) if d == 1 else (lambda ap: ap)
            units = [(k, gi) for k in range(5) for gi in range(len(GROUPS3))]
            uctx = {}

            def bu_stage(u):
                k_, gi_ = units[u]
                f0_, n_ = TILES[d][k_]
                c_ = {}
                for gl in GROUPS3[gi_]:
                    c_[gl] = SETS[setctr[0] % NSET]
                    setctr[0] += 1
                uctx[u] = c_
                for gl in GROUPS3[gi_]:
                    pa = (gl % 3) * 2
                    mm(psum[pa][:, :n_], LB[gl][0][0][:], suo[:, f0_:f0_ + n_], True, True,
                       [LB[gl][0][1], b_suo], pbuf[pa])
                    mm(psum[pa + 1][:, :n_], LB[gl][1][0][:], suo[:, f0_:f0_ + n_], True, True,
                       [LB[gl][1][1], b_suo], pbuf[pa + 1])

            bu_stage(0)
            for u, (k, gi) in enumerate(units):
                f0, n = TILES[d][k]
                grp = GROUPS3[gi]
                py = 6 + k % 2
                ctx_ = uctx.pop(u)
                if k >= 1 and gi == 0:
                    pq = 6 + (k + 1) % 2
                    ei = 0 if k == 1 else 1
                    gs8 = slice(o * 8, (o + 1) * 8)
                    mm(psum[pq][:, :8], psw[:], st8[:, :], True, True, [b_psw] + b_st, pbuf[pq])
                    tt(t8[:], psum[pq][:, :8], Es[:, ei, gs8], ALU.mult, [pbuf[pq], b_Es], [b_t8])
                    tt(init8[:], st8[:], Ec[:, ei, gs8], ALU.mult, b_st + [b_Ec], [b_init8])
                    tt(init8[:], init8[:], t8[:], ALU.add, [b_init8, b_t8], [b_init8])
                tabs = {gl: (cosm[gl], sinm[gl]) for gl in grp}
                for gl in grp:
                    pa = (gl % 3) * 2
                    m1_, b_m1_ = ctx_[gl]["m1"]
                    tt(m1_[:, :n], psum[pa][:, :n], rv(tabs[gl][0][0][:, :n]), ALU.mult,
                       [pbuf[pa], tabs[gl][0][1]], [b_m1_])
                for gl in grp:
                    pa = (gl % 3) * 2
                    vv_, b_vv_ = ctx_[gl]["vv"]
                    tt(vv_[:, :n], psum[pa + 1][:, :n], rv(tabs[gl][1][0][:, :n]), ALU.mult,
                       [pbuf[pa + 1], tabs[gl][1][1]], [b_vv_])
                if u + 1 < len(units):
                    bu_stage(u + 1)
                for gl in grp:
                    m1_, b_m1_ = ctx_[gl]["m1"]
                    vv_, b_vv_ = ctx_[gl]["vv"]
                    P.op("pool", lambda: nc.gpsimd.tensor_tensor(out=vv_[:, :n], in0=vv_[:, :n], in1=m1_[:, :n],
                                                                 op=ALU.add),
                         reads=[b_vv_, b_m1_], writes=[b_vv_])
                for gl in grp:
                    g = o * 8 + gl
                    m1_, b_m1_ = ctx_[gl]["m1"]
                    vv_, b_vv_ = ctx_[gl]["vv"]
                    init = 0.0 if k == 0 else init8[:, gl:gl + 1]
                    dve(lambda: V.tensor_tensor_scan(out=rv(m1_[:, :n]),
                                                     data0=rho[:, g:g + 1].to_broadcast([128, n]),
                                                     data1=rv(vv_[:, :n]), initial=init,
                                                     op0=ALU.mult, op1=ALU.add),
                        [b_rho, b_vv_, b_init8], [b_m1_])
                lastc = n - 1 if d == 0 else 0
                for gl in grp:
                    m1_, b_m1_ = ctx_[gl]["m1"]
                    act(lambda: A.copy(out=st8[:, gl:gl + 1], in_=m1_[:, lastc:lastc + 1]), [b_m1_], [b_st[gl]])
                for gl in grp:
                    m1_, b_m1_ = ctx_[gl]["m1"]
                    P1_, b_P1_ = ctx_[gl]["P1"]
                    tt(P1_[:, :n], m1_[:, :n], rv(tabs[gl][0][0][:, :n]), ALU.mult, [b_m1_, tabs[gl][0][1]], [b_P1_])
                for gl in grp:
                    m1_, b_m1_ = ctx_[gl]["m1"]
                    P2_, b_P2_ = ctx_[gl]["P2"]
                    P.op("pool", lambda: nc.gpsimd.tensor_tensor(out=P2_[:, :n], in0=m1_[:, :n],
                                                                 in1=rv(tabs[gl][1][0][:, :n]), op=ALU.mult),
                         reads=[b_m1_, tabs[gl][1][1]], writes=[b_P2_])
                for gl in grp:
                    P1_, b_P1_ = ctx_[gl]["P1"]
                    P2_, b_P2_ = ctx_[gl]["P2"]
                    first = (gi == 0 and gl == grp[0])
                    lastm = (gi == len(GROUPS3) - 1 and gl == grp[-1])
                    mm(psum[py][:, :n], WG[gl][0][0][:], P1_[:, :n], first, False, [WG[gl][0][1], b_P1_], pbuf[py])
                    mm(psum[py][:, :n], WG[gl][1][0][:], P2_[:, :n], False, lastm, [WG[gl][1][1], b_P2_], pbuf[py])
                if gi == len(GROUPS3) - 1:
                    tt(yacc[:, f0:f0 + n], psum[py][:, :n], yacc[:, f0:f0 + n], ALU.add, [pbuf[py], b_yacc], [b_yacc])
        m1, b_m1 = SETS[0]["m1"]
        vv, b_vv = SETS[0]["vv"]
        ygt, b_yg = yg[o]
        for (t0, n) in TBLK:
            xa = yacc[:, t0:t0 + n]
            act(lambda: A.activation(out=m1[:, :n], in_=xa, func=AF.Square), [b_yacc], [b_m1])
            ts(m1[:, :n], m1[:, :n], 0.044715, 1.0, ALU.mult, ALU.add, [b_m1], [b_m1])
            tt(m1[:, :n], m1[:, :n], xa, ALU.mult, [b_m1, b_yacc], [b_m1])
            act(lambda: A.activation(out=vv[:, :n], in_=m1[:, :n], func=AF.Sigmoid,
                                     scale=2.0 * math.sqrt(2.0 / math.pi)), [b_m1], [b_vv])
            tt(ygt[:, t0:t0 + n], xa, vv[:, :n], ALU.mult, [b_yacc, b_vv], [b_yg])
    wgl, b_wgl = t_("wgl", [128, 6, 128], F32R)
    wv = Wd["w_s5_glu"][l].rearrange("(k p) c -> p k c", p=128)
    kk = 0
    for fo in range(6):
        P.dma("pool", wgl[:], wv[:, :, fo * 128:(fo + 1) * 128], writes=[b_wgl])
        for (t0, n) in TBLK:
            pb = kk % 2
            kk += 1
            for k in range(6):
                mm(psum[pb][:, :n], wgl[:, k, :], yg[k][0][:, t0:t0 + n], k == 0, k == 5,
                   [b_wgl, yg[k][1]], pbuf[pb])
            sgt, b_sgt = SETS[0]["m1"] if pb == 0 else SETS[0]["vv"]
            ott, b_ott = SETS[1]["m1"] if pb == 0 else SETS[1]["vv"]
            act(lambda: A.activation(out=sgt[:, :n], in_=psum[pb][:, :n], func=AF.Sigmoid), [pbuf[pb]], [b_sgt])
            tt(ott[:, :n], yg[fo][0][:, t0:t0 + n].bitcast(F32), sgt[:, :n], ALU.mult, [yg[fo][1], b_sgt], [b_ott])
            P.dma("sp", E.os5T[fo * 128:(fo + 1) * 128, t0:t0 + n], ott[:, :n], reads=[b_ott], writes=[bd["os5T"]])


def phase_att(nc, P, ph, l, env):
    E = NS(env)
    V, A = nc.vector, nc.scalar
    sb, psum, pbuf, bd = E.sb, E.psum, E.pbuf, E.bd
    mm, dve, act = E.mm, E.dve, E.act
    L = f"T{l}"
    kT = sb(L + "kT", [128, T], F32R, scope=ph)
    b_kT = Buf("kT")
    vv = sb(L + "vv", [128, NT, 128], F32R, scope=ph)
    b_vv = Buf("vv")
    qb = [sb(L + f"qb{i}", [128, 512], F32R, scope=ph) for i in range(2)]
    b_qb = [Buf(f"qb{i}") for i in range(2)]
    pt = [sb(L + f"pt{i}", [128, 512], F32R, scope=ph) for i in range(3)]
    b_pt = [Buf(f"pt{i}") for i in range(3)]
    rden = sb(L + "rden", [128, 512], scope=ph)
    b_rden = Buf("rden")
    ob = [sb(L + f"ob{i}", [128, 512], scope=ph) for i in range(2)]
    b_ob = [Buf(f"ob{i}") for i in range(2)]
    k = 0
    for g in range(AKV_H):
        P.dma("pool", kT[:], E.kaT[g * 128:(g + 1) * 128, :], reads=[bd["kaT"]], writes=[b_kT])
        P.dma("pool", vv[:], E.va[:, g * 128:(g + 1) * 128].rearrange("(j p) d -> p j d", p=128),
              reads=[bd["va"]], writes=[b_vv])
        for hh in range(4):
            h = g * 4 + hh
            for (t0, n) in TBLK:
                key_tiles = [0, 1] if t0 == 0 else list(range(NT))
                s = k % 2
                po, pd = 4 + 2 * s, 5 + 2 * s
                k += 1
                P.dma("pool", qb[s][:, :n], E.qaT[h * 128:(h + 1) * 128, t0:t0 + n],
                      reads=[bd["qaT"]], writes=[b_qb[s]])
                def st_mm(idx_):
                    j_ = key_tiles[idx_]
                    mm(psum[idx_ % 3][:, :n], kT[:, j_ * 128:(j_ + 1) * 128], qb[s][:, :n], True, True,
                       [b_kT, b_qb[s]], pbuf[idx_ % 3])

                st_mm(0)
                for idx, j in enumerate(key_tiles):
                    ps = idx % 3
                    last = idx == len(key_tiles) - 1
                    if not last:
                        st_mm(idx + 1)
                    act(lambda: A.activation(out=pt[ps][:, :n], in_=psum[ps][:, :n], func=AF.Exp,
                                             scale=HD ** -0.5), [pbuf[ps]], [b_pt[ps]])
                    mm(psum[po][:, :n], vv[:, j, :], pt[ps][:, :n], idx == 0, last,
                       [b_vv, b_pt[ps]], pbuf[po])
                    mm(psum[pd][:, :n], E.onesr[:], pt[ps][:, :n], idx == 0, last,
                       [E.b_onesr, b_pt[ps]], pbuf[pd])
                dve(lambda: V.reciprocal(out=rden[:, :n], in_=psum[pd][:, :n]), [pbuf[pd]], [b_rden])
                dve(lambda: V.tensor_tensor(out=ob[s][:, :n], in0=psum[po][:, :n], in1=rden[:, :n],
                                            op=ALU.mult), [pbuf[po], b_rden], [b_ob[s]])
                P.dma("sp", E.oattT[h * 128:(h + 1) * 128, t0:t0 + n], ob[s][:, :n],
                      reads=[b_ob[s]], writes=[bd["oattT"]])


def phase_C(nc, P, ph, l, env):
    E = NS(env)
    V, A = nc.vector, nc.scalar
    sb, psum, pbuf, Wd, bd = E.sb, E.psum, E.pbuf, E.Wd, E.bd
    mm, dve, act = E.mm, E.dve, E.act
    L = f"C{l}"
    R1 = sb(L + "R1", [128, KT, 512], F32R, scope=ph)
    R2 = sb(L + "R2", [128, KT, 512], F32, scope=ph)
    R3 = sb(L + "R3", [128, 11, 512], F32R, scope=ph)
    R4 = sb(L + "R4", [128, KT, 512], F32R, scope=ph)
    O2 = sb(L + "O2", [128, 11, 512], F32R, scope=ph)
    bO2 = [Buf(f"O2_{i}") for i in range(11)]
    bR1 = [Buf(f"R1_{i}") for i in range(KT)]
    bR2 = [Buf(f"R2_{i}") for i in range(KT)]
    bR3 = [Buf(f"R3_{i}") for i in range(11)]
    bR4 = [Buf(f"R4_{i}") for i in range(KT)]
    NW = 4
    wb = [sb(L + f"wb{i}", [128, 2048], F32R, scope=ph) for i in range(NW)]
    b_wb = [Buf(f"wb{i}") for i in range(NW)]
    SC = [sb(L + f"sc{i}", [128, 512], scope=ph) for i in range(7)]
    bSC = [Buf(f"sc{i}") for i in range(7)]
    xs, b_xs = SC[0:2], bSC[0:2]
    sg, b_sg = SC[0:3], bSC[0:3]
    tmp, b_tmp = SC[3:5], bSC[3:5]
    sqs, b_sqs = SC[0:2], bSC[0:2]
    mean, b_mean = SC[2], bSC[2]
    rstd, b_rstd = SC[5], bSC[5]
    msq, b_msq = SC[6], bSC[6]
    cnt = {"w": 0, "x": 0, "t": 0, "q": 0}

    def wload(src3, nk):
        s = cnt["w"] % NW
        cnt["w"] += 1
        view = wb[s][:, :nk * 128].rearrange("p (k c) -> p k c", c=128)
        P.dma("pool", view, src3, writes=[b_wb[s]])
        return view, b_wb[s]

    def otile(i, n):
        if i < 11:
            return R3[:, i, :n], bR3[i]
        return O2[:, i - 11, :n], bO2[i - 11]

    def layer_norm(n, wname, bname, post):
        for f in range(KT):
            s = f % 2
            act(lambda: A.activation(out=sqs[s][:, :n], in_=R2[:, f, :n], func=AF.Square),
                [bR2[f]], [b_sqs[s]])
            mm(psum[6][:, :n], E.onesf[:], R2[:, f, :n], f == 0, f == KT - 1, [E.b_onesf, bR2[f]], pbuf[6])
            mm(psum[7][:, :n], E.onesf[:], sqs[s][:, :n], f == 0, f == KT - 1, [E.b_onesf, b_sqs[s]], pbuf[7])
        act(lambda: A.mul(out=mean[:, :n], in_=psum[6][:, :n], mul=1.0 / D), [pbuf[6]], [b_mean])
        dve(lambda: V.tensor_tensor(out=msq[:, :n], in0=mean[:, :n], in1=mean[:, :n], op=ALU.mult),
            [b_mean], [b_msq])
        dve(lambda: V.scalar_tensor_tensor(out=msq[:, :n], in0=psum[7][:, :n], scalar=1.0 / D,
                                           in1=msq[:, :n], op0=ALU.mult, op1=ALU.subtract),
            [pbuf[7], b_msq], [b_msq])
        act(lambda: A.activation(out=rstd[:, :n], in_=msq[:, :n], func=AF.Sqrt, bias=E.epsc[:, 0:1],
                                 scale=1.0), [b_msq, E.b_epsc], [b_rstd])
        dve(lambda: V.reciprocal(out=rstd[:, :n], in_=rstd[:, :n]), [b_rstd], [b_rstd])
        for f in range(KT):
            s = cnt["t"] % 2
            cnt["t"] += 1
            dve(lambda: V.tensor_tensor(out=tmp[s][:, :n], in0=R2[:, f, :n], in1=mean[:, :n],
                                        op=ALU.subtract), [bR2[f], b_mean], [b_tmp[s]])
            dve(lambda: V.tensor_tensor(out=tmp[s][:, :n], in0=tmp[s][:, :n], in1=rstd[:, :n],
                                        op=ALU.mult), [b_tmp[s], b_rstd], [b_tmp[s]])
            act(lambda: A.activation(out=R2[:, f, :n], in_=tmp[s][:, :n], func=AF.Identity,
                                     scale=E.vcol(wname, l * 16 + f), bias=E.vcol(bname, l * 16 + f)),
                [b_tmp[s], E.b_vecT], [bR2[f]])
            post(f)

    for bi, (t0, n) in enumerate(TBLK):
        col = 1 if bi == 0 else 0
        for ft in range(KT):
            s = cnt["x"] % 2
            cnt["x"] += 1
            P.dma("sp", xs[s][:, :n], E.xT_v[:, ft, t0:t0 + n], reads=[E.b_xT[bi]], writes=[b_xs[s]])
            dve(lambda: V.tensor_scalar(out=R1[:, ft, :n], in0=xs[s][:, :n],
                                        scalar1=E.mcol(l, 1, ft, col), scalar2=E.mcol(l, 0, ft, col),
                                        op0=ALU.mult, op1=ALU.add),
                [b_xs[s], E.b_modT], [bR1[ft]])
        P.dma("pool", R3[:, 0:8, :n], E.fv(E.oglaT)[:, :, t0:t0 + n], reads=[bd["oglaT"]], writes=bR3[0:8])
        P.dma("pool", R3[:, 8:11, :n], E.fv(E.os5T)[:, 0:3, t0:t0 + n], reads=[bd["os5T"]], writes=bR3[8:11])
        P.dma("pool", O2[:, 0:3, :n], E.fv(E.os5T)[:, 3:6, t0:t0 + n],
              reads=[bd["os5T"]], writes=bO2[0:3])
        P.dma("pool", O2[:, 3:11, :n], E.fv(E.oattT)[:, :, t0:t0 + n],
              reads=[bd["oattT"]], writes=bO2[3:11])
        for f in range(KT):
            for br in range(3):
                wv_, bw = wload(Wd["bg_t"][l, br * 16 + f], KT)
                for kt in range(KT):
                    mm(psum[br][:, :n], wv_[:, kt, :], R1[:, kt, :n], kt == 0, kt == KT - 1,
                       [bw, bR1[kt]], pbuf[br])
                act(lambda: A.activation(out=sg[br][:, :n], in_=psum[br][:, :n], func=AF.Sigmoid),
                    [pbuf[br]], [b_sg[br]])
            for br, (pv, nk, o0) in enumerate((("wpg_t", 8, 0), ("wps_t", 6, 8), ("wpa_t", 8, 14))):
                wv_, bw = wload(Wd[pv][l, f], nk)
                for kk in range(nk):
                    oap, ob_ = otile(o0 + kk, n)
                    mm(psum[3 + br][:, :n], wv_[:, kk, :], oap, kk == 0, kk == nk - 1, [bw, ob_], pbuf[3 + br])
            dve(lambda: V.tensor_tensor(out=tmp[0][:, :n], in0=psum[3][:, :n], in1=sg[0][:, :n], op=ALU.mult),
                [pbuf[3], b_sg[0]], [b_tmp[0]])
            dve(lambda: V.tensor_tensor(out=tmp[1][:, :n], in0=psum[4][:, :n], in1=sg[1][:, :n], op=ALU.mult),
                [pbuf[4], b_sg[1]], [b_tmp[1]])
            dve(lambda: V.tensor_tensor(out=tmp[0][:, :n], in0=tmp[0][:, :n], in1=tmp[1][:, :n], op=ALU.add),
                [b_tmp[0], b_tmp[1]], [b_tmp[0]])
            dve(lambda: V.tensor_tensor(out=tmp[1][:, :n], in0=psum[5][:, :n], in1=sg[2][:, :n], op=ALU.mult),
                [pbuf[5], b_sg[2]], [b_tmp[1]])
            dve(lambda: V.tensor_tensor(out=R4[:, f, :n], in0=tmp[0][:, :n], in1=tmp[1][:, :n], op=ALU.add),
                [b_tmp[0], b_tmp[1]], [bR4[f]])
        for f in range(KT):
            wv_, bw = wload(Wd["wo_t"][l, f], KT)
            pb = f % 2
            for kt in range(KT):
                mm(psum[pb][:, :n], wv_[:, kt, :], R4[:, kt, :n], kt == 0, kt == KT - 1, [bw, bR4[kt]], pbuf[pb])
            s = cnt["x"] % 2
            cnt["x"] += 1
            P.dma("sp", xs[s][:, :n], E.xT_v[:, f, t0:t0 + n], reads=[E.b_xT[bi]], writes=[b_xs[s]])
            act(lambda: A.mul(out=R2[:, f, :n], in_=xs[s][:, :n], mul=DN_ALPHA), [b_xs[s]], [bR2[f]])
            dve(lambda: V.scalar_tensor_tensor(out=R2[:, f, :n], in0=psum[pb][:, :n],
                                               scalar=E.mcol(l, 2, f, col), in1=R2[:, f, :n],
                                               op0=ALU.mult, op1=ALU.add),
                [pbuf[pb], bR2[f], E.b_modT], [bR2[f]])

        def post1(f):
            act(lambda: A.activation(out=R1[:, f, :n], in_=R2[:, f, :n], func=AF.Identity,
                                     scale=E.mcol(l, 4, f, col), bias=E.mcol(l, 3, f, col)),
                [bR2[f], E.b_modT], [bR1[f]])
        layer_norm(n, "ln1_w", "ln1_b", post1)
        for q in range(4):
            for j in range(11):
                jj = q * 11 + j
                pa, pbk = (0, 1) if j % 2 == 0 else (2, 3)
                wa, bwa = wload(Wd["wfi_t"][l, jj], KT)
                for kt in range(KT):
                    mm(psum[pa][:, :n], wa[:, kt, :], R1[:, kt, :n], kt == 0, kt == KT - 1, [bwa, bR1[kt]], pbuf[pa])
                wb_, bwb = wload(Wd["wfi_t"][l, 44 + jj], KT)
                for kt in range(KT):
                    mm(psum[pbk][:, :n], wb_[:, kt, :], R1[:, kt, :n], kt == 0, kt == KT - 1, [bwb, bR1[kt]], pbuf[pbk])
                s = j % 2
                act(lambda: A.activation(out=sg[s][:, :n], in_=psum[pa][:, :n], func=AF.Silu),
                    [pbuf[pa]], [b_sg[s]])
                dve(lambda: V.tensor_tensor(out=R3[:, j, :n], in0=psum[pbk][:, :n], in1=sg[s][:, :n], op=ALU.mult),
                    [pbuf[pbk], b_sg[s]], [bR3[j]])
            for f in range(KT):
                wv_, bw = wload(Wd["wfo_t"][l, q, f], 11)
                pb = 4 + f % 2
                for j in range(11):
                    mm(psum[pb][:, :n], wv_[:, j, :], R3[:, j, :n], j == 0, j == 10, [bw, bR3[j]], pbuf[pb])
                if q == 0:
                    dve(lambda: V.tensor_copy(out=R4[:, f, :n], in_=psum[pb][:, :n]),
                        [pbuf[pb]], [bR4[f]])
                else:
                    dve(lambda: V.tensor_tensor(out=R4[:, f, :n], in0=psum[pb][:, :n],
                                                in1=R4[:, f, :n].bitcast(F32), op=ALU.add),
                        [pbuf[pb], bR4[f]], [bR4[f]])
        for f in range(KT):
            act(lambda: A.mul(out=R2[:, f, :n], in_=R2[:, f, :n], mul=DN_ALPHA), [bR2[f]], [bR2[f]])
            dve(lambda: V.scalar_tensor_tensor(out=R2[:, f, :n], in0=R4[:, f, :n].bitcast(F32),
                                               scalar=E.mcol(l, 5, f, col), in1=R2[:, f, :n],
                                               op0=ALU.mult, op1=ALU.add),
                [bR4[f], bR2[f], E.b_modT], [bR2[f]])

        def post2(f):
            P.dma("sp", E.xT_v[:, f, t0:t0 + n], R2[:, f, :n], reads=[bR2[f]], writes=[E.b_xT[bi]])
        layer_norm(n, "ln2_w", "ln2_b", post2)


_CONSTS = None


def _consts():
    global _CONSTS
    if _CONSTS is None:
        c = {}
        c["ident"] = np.eye(128, dtype=np.float32)
        c["ones128"] = np.ones((128, 128), np.float32)
        pr = np.zeros((128, 128), np.float32)
        for i in range(64):
            pr[2 * i + 1, 2 * i] = -1.0
            pr[2 * i, 2 * i + 1] = 1.0
        c["prot"] = pr
        t = np.arange(T)
        c["cmask"] = np.broadcast_to((t % CH != 0).astype(np.float32), (128, T)).copy()
        jj, ii = np.meshgrid(np.arange(64), np.arange(64), indexing="ij")
        c["trif"] = (ii >= jj).astype(np.float32)
        c["trib"] = (ii <= jj).astype(np.float32)
        r = np.arange(128)
        c["rmask"] = (r[:, None] // 16 == np.arange(8)[None, :]).astype(np.float32)
        cm = (np.arange(128)[None, :] // 16 == np.arange(8)[:, None]).astype(np.float32)
        c["colmask"] = np.broadcast_to(cm.reshape(1, 8 * 128), (128, 8 * 128)).copy()
        c["iota512"] = np.broadcast_to(np.arange(512, dtype=np.float32), (128, 512)).copy()
        tl = np.arange(NLAT)
        pos = np.zeros((128, NLAT), np.float32)
        pos[:64, :] = (tl // 64)[None, :]
        pos[64:, :] = (tl % 64)[None, :]
        c["ropepos"] = pos
        d = np.arange(128)
        c["fidx"] = ((d % 64) // 2).astype(np.float32).reshape(128, 1)
        ps_ = np.zeros((128, 128), np.float32)
        for m in range(64):
            ps_[m + 64, m] = -1.0
            ps_[m, m + 64] = 1.0
        c["psw"] = ps_
        _CONSTS = c
    return _CONSTS


def make_in_maps(inputs, n_cores=8):
    c = _consts()
    f = lambda a: np.ascontiguousarray(a, dtype=np.float32)
    x = f(inputs["x"])
    ctx = f(inputs["ctx"])
    cvec = f(inputs["c"])
    c_ctx = f(inputs["c_ctx"])
    shared = {n: f(inputs[n]) for n, _ in WEIGHT_NAMES}
    L_ = DEPTH
    w_in = inputs["w_in"]
    shared["w_inA"] = f(w_in[:, :, :OFF_BG])
    shared["bg_t"] = f(np.asarray(w_in[:, :, OFF_BG:]).reshape(L_, 16, 128, 48, 128).transpose(0, 3, 2, 1, 4))

    def tile_w(w, nk):
        return f(np.asarray(w).reshape(L_, nk, 128, 16, 128).transpose(0, 3, 2, 1, 4))

    shared["wpg_t"] = tile_w(inputs["w_proj_gla"], 8)
    shared["wps_t"] = tile_w(inputs["w_proj_s5"], 6)
    shared["wpa_t"] = tile_w(inputs["w_proj_attn"], 8)
    shared["wo_t"] = tile_w(inputs["w_out"], 16)
    shared["wfi_t"] = f(np.asarray(inputs["w_ffn_in"]).reshape(L_, 16, 128, 88, 128).transpose(0, 3, 2, 1, 4))
    shared["wfo_t"] = f(np.asarray(inputs["w_ffn_out"]).reshape(L_, 4, 11, 128, 16, 128).transpose(0, 1, 4, 3, 2, 5))
    shared.update(c)
    in_maps = []
    for core in range(n_cores):
        b = core % 4
        m = {"x": x[b], "ctx": ctx[b], "cc": np.stack([cvec[b], c_ctx], 0)}
        m.update(shared)
        in_maps.append(m)
    return in_maps


def kernel(**inputs):
    n_cores = 8
    nc = build_program()
    in_maps = make_in_maps(inputs, n_cores)
    res = run_bass_kernel_spmd(nc, in_maps, core_ids=list(range(n_cores)))
    out = np.stack([res.results[b]["out"] for b in range(4)], axis=0)
    return out.astype(np.float32)
```
